# Optimizing a Trainium2 kernel written in Bass

```python
import jax, jax.numpy as jnp
from jax import lax
import numpy as np

D_MODEL = 1024
BATCH = 8
SEQ = 2048
DEPTH = 1

HEAD_DIM = 64
N_HEADS = D_MODEL // HEAD_DIM
N_MOBA_HEADS = N_HEADS // 2
N_FOX_HEADS = N_HEADS - N_MOBA_HEADS
MOBA_WIDTH = N_MOBA_HEADS * HEAD_DIM
FOX_WIDTH = N_FOX_HEADS * HEAD_DIM
MOBA_BLOCK = 256
MOBA_TOPK = 3
MOBA_Q_CHUNK = 16
FOX_Q_BLOCK = 128
ROPE_THETA = 500000.0
ROPE_DIM = HEAD_DIM // 4
D_FF = ((8 * D_MODEL // 3 + 127) // 128) * 128
CONV_WIDTH = 3
NORM_EPS = 1e-6
NEG_INF = -1e30
IN_COLS = 3 * MOBA_WIDTH + 3 * FOX_WIDTH + N_FOX_HEADS

kernel_name = 'hybrid_moba_fox_convffn_block'


def rms_norm(x, gain):
    xf = x.astype(jnp.float32)
    y = xf * lax.rsqrt(jnp.mean(xf * xf, axis=-1, keepdims=True) + NORM_EPS)
    return (y * gain.astype(jnp.float32)).astype(x.dtype)


def partial_rope(x, positions):
    half = ROPE_DIM // 2
    inv_freq = jnp.power(ROPE_THETA, -2.0 * jnp.arange(half, dtype=jnp.float32) / ROPE_DIM)
    ang = positions.astype(jnp.float32)[:, None] * inv_freq[None, :]
    cos = jnp.cos(ang)[None, :, None, :]
    sin = jnp.sin(ang)[None, :, None, :]
    xf = x.astype(jnp.float32)
    x1 = xf[..., :half]
    x2 = xf[..., half:ROPE_DIM]
    out = jnp.concatenate([x1 * cos - x2 * sin, x2 * cos + x1 * sin, xf[..., ROPE_DIM:]], axis=-1)
    return out.astype(x.dtype)


def moba_attention(q, k, v):
    b, h, t, dh = q.shape
    n_blocks = -(-t // MOBA_BLOCK)
    t_pad = n_blocks * MOBA_BLOCK
    pad = t_pad - t
    if pad:
        q, k, v = [jnp.pad(a, ((0, 0), (0, 0), (0, pad), (0, 0))) for a in (q, k, v)]
    scale = dh ** -0.5
    kb = k.reshape(b, h, n_blocks, MOBA_BLOCK, dh)
    vb = v.reshape(b, h, n_blocks, MOBA_BLOCK, dh)
    own_blk = jnp.arange(t_pad) // MOBA_BLOCK
    n_sel = min(MOBA_TOPK, n_blocks - 1)
    if n_sel > 0:
        k_mean = jnp.mean(kb.astype(jnp.float32), axis=3)
        gate = jnp.einsum('bhtd,bhnd->bhtn', q.astype(jnp.float32), k_mean)
        is_past = jnp.arange(n_blocks)[None, :] < own_blk[:, None]
        gate = jnp.where(is_past, gate, NEG_INF)
        _, sel_idx = lax.top_k(gate, n_sel)
        sel_valid = sel_idx < own_blk[None, None, :, None]
    bi = jnp.arange(b)[:, None, None, None]
    hi = jnp.arange(h)[None, :, None, None]

    def chunk(start):
        qc = lax.dynamic_slice_in_dim(q, start, MOBA_Q_CHUNK, axis=2).astype(jnp.float32)
        q_pos = start + jnp.arange(MOBA_Q_CHUNK)
        blk = start // MOBA_BLOCK
        k_own = lax.dynamic_index_in_dim(kb, blk, axis=2, keepdims=False).astype(jnp.float32)
        v_own = lax.dynamic_index_in_dim(vb, blk, axis=2, keepdims=False).astype(jnp.float32)
        s_own = jnp.einsum('bhqd,bhsd->bhqs', qc, k_own) * scale
        k_pos = blk * MOBA_BLOCK + jnp.arange(MOBA_BLOCK)
        s_own = jnp.where(k_pos[None, :] <= q_pos[:, None], s_own, NEG_INF)
        if n_sel > 0:
            idx = lax.dynamic_slice_in_dim(sel_idx, start, MOBA_Q_CHUNK, axis=2)
            valid = lax.dynamic_slice_in_dim(sel_valid, start, MOBA_Q_CHUNK, axis=2)
            k_sel = kb[bi, hi, idx].astype(jnp.float32)
            v_sel = vb[bi, hi, idx].astype(jnp.float32)
            s_sel = jnp.einsum('bhqd,bhqnsd->bhqns', qc, k_sel) * scale
            s_sel = jnp.where(valid[..., None], s_sel, NEG_INF)
            s_sel = s_sel.reshape(b, h, MOBA_Q_CHUNK, n_sel * MOBA_BLOCK)
            p = jax.nn.softmax(jnp.concatenate([s_sel, s_own], axis=-1), axis=-1)
            p_sel = p[..., :n_sel * MOBA_BLOCK].reshape(b, h, MOBA_Q_CHUNK, n_sel, MOBA_BLOCK)
            p_own = p[..., n_sel * MOBA_BLOCK:]
            out = (jnp.einsum('bhqns,bhqnsd->bhqd', p_sel, v_sel)
                   + jnp.einsum('bhqs,bhsd->bhqd', p_own, v_own))
        else:
            p = jax.nn.softmax(s_own, axis=-1)
            out = jnp.einsum('bhqs,bhsd->bhqd', p, v_own)
        return out.astype(q.dtype)

    starts = jnp.arange(0, t_pad, MOBA_Q_CHUNK)
    out = lax.map(chunk, starts)
    out = jnp.moveaxis(out, 0, 2).reshape(b, h, t_pad, dh)
    return out[:, :, :t]


def forgetting_attention(q, k, v, log_f):
    b, h, t, dh = q.shape
    scale = dh ** -0.5
    cum = jnp.cumsum(log_f, axis=-1)
    k_pos = jnp.arange(t)
    kf = k.astype(jnp.float32)
    vf = v.astype(jnp.float32)

    def block(start):
        qc = lax.dynamic_slice_in_dim(q, start, FOX_Q_BLOCK, axis=2).astype(jnp.float32)
        cq = lax.dynamic_slice_in_dim(cum, start, FOX_Q_BLOCK, axis=2)
        s = (jnp.einsum('bhqd,bhsd->bhqs', qc, kf) * scale
             + cq[..., :, None] - cum[..., None, :])
        q_pos = start + jnp.arange(FOX_Q_BLOCK)
        s = jnp.where(k_pos[None, :] <= q_pos[:, None], s, NEG_INF)
        p = jax.nn.softmax(s, axis=-1)
        return jnp.einsum('bhqs,bhsd->bhqd', p, vf).astype(q.dtype)

    out = lax.map(block, jnp.arange(0, t, FOX_Q_BLOCK))
    return jnp.moveaxis(out, 0, 2).reshape(b, h, t, dh)


def hybrid_layer(x, c, w_ada, b_ada, g_mix, w_in, b_forget, moba_q_gain, moba_k_gain,
                 fox_q_gain, fox_k_gain, w_out, g_ffn, w_up, conv_w, conv_b, w_down):
    b, t, _ = x.shape
    mod = (jax.nn.silu(c) @ w_ada + b_ada)[:, None, :]
    sh1, sc1, gt1, sh2, sc2, gt2 = jnp.split(mod, 6, axis=-1)

    hn = rms_norm(x, g_mix) * (1.0 + sc1) + sh1
    proj = hn @ w_in
    offs = np.cumsum([MOBA_WIDTH, MOBA_WIDTH, MOBA_WIDTH, FOX_WIDTH, FOX_WIDTH, FOX_WIDTH]).tolist()
    mq, mk, mv, fq, fk, fv, f_logit = jnp.split(proj, offs, axis=-1)
    positions = jnp.arange(t)
    mq = partial_rope(rms_norm(mq.reshape(b, t, N_MOBA_HEADS, HEAD_DIM), moba_q_gain), positions)
    mk = partial_rope(rms_norm(mk.reshape(b, t, N_MOBA_HEADS, HEAD_DIM), moba_k_gain), positions)
    mv = mv.reshape(b, t, N_MOBA_HEADS, HEAD_DIM)
    fq = rms_norm(fq.reshape(b, t, N_FOX_HEADS, HEAD_DIM), fox_q_gain)
    fk = rms_norm(fk.reshape(b, t, N_FOX_HEADS, HEAD_DIM), fox_k_gain)
    fv = fv.reshape(b, t, N_FOX_HEADS, HEAD_DIM)
    to_bhtd = lambda a: a.transpose(0, 2, 1, 3)
    log_f = jax.nn.log_sigmoid((f_logit + b_forget).astype(jnp.float32)).transpose(0, 2, 1)
    o_moba = moba_attention(to_bhtd(mq), to_bhtd(mk), to_bhtd(mv))
    o_fox = forgetting_attention(to_bhtd(fq), to_bhtd(fk), to_bhtd(fv), log_f)
    o = jnp.concatenate([to_bhtd(o_moba).reshape(b, t, MOBA_WIDTH),
                         to_bhtd(o_fox).reshape(b, t, FOX_WIDTH)], axis=-1)
    x = x + gt1 * (o @ w_out)

    hn = rms_norm(x, g_ffn) * (1.0 + sc2) + sh2
    u = hn @ w_up
    u_pad = jnp.pad(u, ((0, 0), (CONV_WIDTH - 1, 0), (0, 0)))
    u = sum(conv_w[i] * u_pad[:, i:i + t] for i in range(CONV_WIDTH)) + conv_b
    a, val = jnp.split(u, 2, axis=-1)
    x = x + gt2 * ((jax.nn.silu(a) * val) @ w_down)
    return x


def setup_inputs(seed: int = 0) -> dict:
    key = jax.random.key(seed)
    ks = jax.random.split(key, 19)
    f32 = jnp.float32
    nrm = lambda k, s: jax.random.normal(k, s, dtype=f32)
    gain = lambda k, n: 1.0 + 0.02 * nrm(k, (DEPTH, n))
    return {
        'x': nrm(ks[0], (BATCH, SEQ, D_MODEL)),
        'c': nrm(ks[1], (BATCH, D_MODEL)),
        'w_ada': nrm(ks[2], (DEPTH, D_MODEL, 6 * D_MODEL)) * (0.5 * D_MODEL ** -0.5),
        'b_ada': 0.02 * nrm(ks[3], (DEPTH, 6 * D_MODEL)),
        'g_mix': gain(ks[4], D_MODEL),
        'w_in': nrm(ks[5], (DEPTH, D_MODEL, IN_COLS)) * D_MODEL ** -0.5,
        'b_forget': jax.random.uniform(ks[6], (DEPTH, N_FOX_HEADS), dtype=f32, minval=1.0, maxval=4.0),
        'moba_q_gain': gain(ks[7], HEAD_DIM),
        'moba_k_gain': gain(ks[8], HEAD_DIM),
        'fox_q_gain': gain(ks[9], HEAD_DIM),
        'fox_k_gain': gain(ks[10], HEAD_DIM),
        'w_out': nrm(ks[11], (DEPTH, D_MODEL, D_MODEL)) * D_MODEL ** -0.5,
        'g_ffn': gain(ks[12], D_MODEL),
        'w_up': nrm(ks[13], (DEPTH, D_MODEL, 2 * D_FF)) * D_MODEL ** -0.5,
        'conv_w': nrm(ks[14], (DEPTH, CONV_WIDTH, 2 * D_FF)) * CONV_WIDTH ** -0.5,
        'conv_b': 0.02 * nrm(ks[15], (DEPTH, 2 * D_FF)),
        'w_down': nrm(ks[16], (DEPTH, D_FF, D_MODEL)) * D_FF ** -0.5,
    }


def reference(x, c, w_ada, b_ada, g_mix, w_in, b_forget, moba_q_gain, moba_k_gain,
              fox_q_gain, fox_k_gain, w_out, g_ffn, w_up, conv_w, conv_b, w_down):
    for l in range(DEPTH):
        x = hybrid_layer(x, c, w_ada[l], b_ada[l], g_mix[l], w_in[l], b_forget[l],
                         moba_q_gain[l], moba_k_gain[l], fox_q_gain[l], fox_k_gain[l],
                         w_out[l], g_ffn[l], w_up[l], conv_w[l], conv_b[l], w_down[l])
    return x
```

```python
import math
import numpy as np
import concourse.bass as bass
import concourse.mybir as mybir
from concourse.bass_utils import run_bass_kernel_spmd

F32 = mybir.dt.float32
BF16 = mybir.dt.bfloat16
AF = mybir.ActivationFunctionType
ALU = mybir.AluOpType
AX = mybir.AxisListType

T = 2048
D = 1024
NT = 16
H = 8
DH = 64
DFF = 2816
NM = 22
INC = 3080
EPS = 1e-6
NEG = -30000.0
ROPE_THETA = 500000.0

SEM_EPOCH = 20000
ROPE_ON_DEVICE = True
GRAN = 128

S_C, S_GMIX, S_GFFN, S_BADA, S_CONVW, S_CONVB = 0, 8, 16, 24, 72, 204
S_MQG, S_MKG, S_FQG, S_FKG, S_BF, S_CS, S_NS = 256, 320, 384, 448, 512, 576, 832
S_TOT = 1152


class Op:
    __slots__ = ("eng", "fn", "deps", "signal", "sig_idx", "is_dma", "dma_sem", "dma_val", "seq")

    def __init__(self, eng, fn, is_dma):
        self.eng = eng
        self.fn = fn
        self.deps = []
        self.signal = is_dma
        self.sig_idx = -1
        self.is_dma = is_dma
        self.dma_sem = None
        self.dma_val = 0
        self.seq = -1


def _ap_range(ap):
    sp = str(ap.space)
    esz = mybir.dt.size(ap.dtype) if hasattr(mybir.dt, "size") else None
    if esz is None:
        esz = 2 if ap.dtype == BF16 else 4
    dims = list(ap.ap)
    row = dims[0][0]
    off = ap.offset % row if row > 0 else ap.offset
    span = 1
    for st, cnt in dims[1:]:
        span += (cnt - 1) * abs(st)
    return sp, off * esz, (off + span) * esz


class Prog:
    ENGS = ("pe", "act", "dve", "pool", "sp")

    def __init__(self, nc, n_dma_sems=12):
        self.nc = nc
        self.ops = {e: [] for e in self.ENGS}
        self.n_dma_sems = n_dma_sems
        self.sb_w = {}
        self.sb_r = {}
        self.ps = {}
        self.nseq = 0

    @staticmethod
    def _esz(dt):
        return 2 if dt == BF16 else 4

    def _range(self, ap):
        sps = str(ap.space)
        sp = "psum" if sps == "PSUM" else ("sbuf" if sps == "SB" else "dram")
        esz = self._esz(ap.dtype)
        dims = list(ap.ap)
        row = dims[0][0]
        off = ap.offset % row if row > 0 else ap.offset
        span = 1
        for st, cnt in dims[1:]:
            span += (cnt - 1) * abs(st)
        return sp, off * esz, (off + span) * esz

    def _add(self, o, reads, writes):
        deps = {}

        def add_dep(d):
            if d is not None and d is not o:
                deps[id(d)] = d

        acc = []
        for ap in reads:
            acc.append((ap, False))
        for ap in writes:
            acc.append((ap, True))
        for ap, is_w in acc:
            sp, lo, hi = self._range(ap)
            if sp == "dram":
                continue
            if sp == "psum":
                for b in range(lo // 2048, (hi - 1) // 2048 + 1):
                    st = self.ps.setdefault(b, {})
                    for e2, (op2, w2) in st.items():
                        if e2 != o.eng:
                            add_dep(op2)
                        elif o.eng != "pe" and (w2 or is_w):
                            add_dep(op2)
            else:
                for g in range(lo // GRAN, (hi - 1) // GRAN + 1):
                    add_dep(self.sb_w.get(g))
                    if is_w:
                        rs = self.sb_r.get(g)
                        if rs:
                            for r in rs.values():
                                add_dep(r)
        for ap, is_w in acc:
            sp, lo, hi = self._range(ap)
            if sp == "dram":
                continue
            if sp == "psum":
                for b in range(lo // 2048, (hi - 1) // 2048 + 1):
                    st = self.ps.setdefault(b, {})
                    prev = st.get(o.eng)
                    if prev is not None and prev[0] is o:
                        st[o.eng] = (o, prev[1] or is_w)
                    else:
                        st[o.eng] = (o, is_w)
            else:
                key = ("dma", id(o)) if o.is_dma else o.eng
                for g in range(lo // GRAN, (hi - 1) // GRAN + 1):
                    if is_w:
                        self.sb_w[g] = o
                        self.sb_r[g] = {}
                    else:
                        self.sb_r.setdefault(g, {})[key] = o
        best = {}
        dl = []
        for d in deps.values():
            if d.is_dma:
                dl.append(d)
                continue
            if d.eng == "pe" and o.eng == "pe" and not o.is_dma:
                continue
            b = best.get(d.eng)
            if b is None or d.seq > b.seq:
                best[d.eng] = d
        dl.extend(best.values())
        o.deps = dl
        for d in dl:
            d.signal = True
        o.seq = self.nseq
        self.nseq += 1
        self.ops[o.eng].append(o)
        return o

    def op(self, eng, fn, reads=(), writes=()):
        return self._add(Op(eng, fn, False), reads, writes)

    def dma(self, eng, fn, reads=(), writes=()):
        return self._add(Op(eng, fn, True), reads, writes)

    def emit(self):
        nc = self.nc
        nsig = {}
        for e in self.ENGS:
            k = 0
            for o in self.ops[e]:
                if (not o.is_dma) and o.signal:
                    k += 1
                    o.sig_idx = k
            nsig[e] = k
        esems = {}
        for e in self.ENGS:
            n_ep = max(1, (nsig[e] + SEM_EPOCH - 1) // SEM_EPOCH)
            esems[e] = [nc.alloc_semaphore(f"s_{e}_{i}") for i in range(n_ep)]
        dsems, dcount = {}, {}
        for e in self.ENGS:
            dl = [o for o in self.ops[e] if o.is_dma]
            if dl:
                n = min(self.n_dma_sems, len(dl))
                dsems[e] = [nc.alloc_semaphore(f"d_{e}_{i}") for i in range(n)]
                dcount[e] = [0] * n
                for k, o in enumerate(dl):
                    j = k % n
                    dcount[e][j] += 16
                    o.dma_sem = (e, j)
                    o.dma_val = dcount[e][j]

        def sem_of(d):
            if d.is_dma:
                e, j = d.dma_sem
                return ("d", e, j), dsems[e][j], d.dma_val
            ep = (d.sig_idx - 1) // SEM_EPOCH
            return ("c", d.eng, ep), esems[d.eng][ep], d.sig_idx - ep * SEM_EPOCH

        prog = self

        def run_engine(ename, eng):
            seen = {}
            for o in prog.ops[ename]:
                waits = {}
                for d in o.deps:
                    key, sem, val = sem_of(d)
                    if seen.get(key, 0) >= val:
                        continue
                    if key not in waits or waits[key][1] < val:
                        waits[key] = (sem, val)
                if o.is_dma:
                    e, j = o.dma_sem
                    key = ("d", e, j)
                    pv = o.dma_val - 16
                    if pv > 0 and seen.get(key, 0) < pv:
                        if key not in waits or waits[key][1] < pv:
                            waits[key] = (dsems[e][j], pv)
                for key, (sem, val) in waits.items():
                    eng.wait_ge(sem, val)
                    seen[key] = val
                ins = o.fn(eng)
                if o.is_dma:
                    e, j = o.dma_sem
                    ins.then_inc(dsems[e][j], 16)
                elif o.signal:
                    ep = (o.sig_idx - 1) // SEM_EPOCH
                    ins.then_inc(esems[ename][ep], 1)
            if ename in dsems:
                for j, s in enumerate(dsems[ename]):
                    if dcount[ename][j] > 0:
                        eng.wait_ge(s, dcount[ename][j])

        with nc.Block() as block:
            @block.tensor
            def _(e):
                run_engine("pe", e)

            @block.scalar
            def _(e):
                run_engine("act", e)

            @block.vector
            def _(e):
                run_engine("dve", e)

            @block.gpsimd
            def _(e):
                run_engine("pool", e)

            @block.sync
            def _(e):
                run_engine("sp", e)


def build_program(debug=False, stop_after=None):
    nc = bass.Bass("TRN2", target_bir_lowering=False)
    P = Prog(nc)

    def din(name, shape):
        return nc.dram_tensor(name, shape, F32, kind="ExternalInput").ap()

    x_d = din("x", [T, D])
    wada_d = din("w_ada", [128, 8 * 6144])
    win_d = din("w_in", [128, 8 * INC])
    wout_d = din("w_out", [128, 8 * D])
    wup_d = din("w_up", [128, NM * 2 * 8 * 128])
    wdn_d = din("w_down", [128, NM * D])
    smalls_d = din("smalls", [128, S_TOT])
    bcast_d = din("bcastb", [128, 2048])
    out_d = nc.dram_tensor("out", [T, D], F32, kind="ExternalOutput").ap()
    dbg = {}

    ARENA_W = 53180
    arena = nc.alloc_sbuf_tensor("arena", [128, ARENA_W], F32)
    psum = nc.alloc_psum_tensor("psum", [128, 4096], F32)

    def sb(off_b, n, dt):
        assert off_b % 4 == 0
        if dt == F32:
            assert off_b // 4 + n <= ARENA_W, (off_b, n)
            return arena[:, off_b // 4: off_b // 4 + n]
        assert off_b + 2 * n <= ARENA_W * 4, (off_b, n)
        nw = (n + 1) // 2
        return arena[:, off_b // 4: off_b // 4 + nw].bitcast(BF16)[:, 0:n]

    def pbank(b, n=512, nb=1):
        return psum[:, b * 512: b * 512 + n]

    def pbank_bf(b):
        return psum[:, b * 512:(b + 1) * 512].bitcast(BF16)

    class Alloc:
        def __init__(self, base, limit):
            self.p = base
            self.limit = limit

        def take(self, nbytes, align=128):
            self.p = (self.p + align - 1) // align * align
            o = self.p
            self.p += nbytes
            assert self.p <= self.limit, ("arena overflow", self.p, self.limit)
            return o

    LIMIT = ARENA_W * 4
    CA = Alloc(0, 27 * 1024)
    smalls = sb(CA.take(S_TOT * 4), S_TOT, F32)
    gtb = sb(CA.take(2048 * 4), 2048, F32)
    ident_f = sb(CA.take(512), 128, F32)
    tri_f = sb(CA.take(512), 128, F32)
    ones_f = sb(CA.take(512), 128, F32)
    ident_b = sb(CA.take(256), 128, BF16)
    ones_b = sb(CA.take(256), 128, BF16)
    tribias_b = sb(CA.take(256), 128, BF16)
    modfm = sb(CA.take(192), 48, F32)
    ab1 = sb(CA.take(64), 16, F32)
    ab2 = sb(CA.take(64), 16, F32)
    scf = sb(CA.take(32), 8, F32)
    scb16 = sb(CA.take(16), 8, BF16)
    scbc = sb(CA.take(2048), 1024, BF16)
    ss_a = sb(CA.take(64), 16, F32)
    rstd_a = sb(CA.take(64), 16, F32)
    epsc = sb(CA.take(16), 4, F32)
    invc = sb(CA.take(8), 4, BF16)
    zc4 = sb(CA.take(16), 4, F32)
    wflb = sb(CA.take(8 * 8 * 2), 64, BF16).rearrange("p (k c) -> p k c", k=8)
    ss8 = [sb(CA.take(32), 8, F32) for _ in range(2)]
    sd8 = [sb(CA.take(32), 8, F32) for _ in range(2)]
    rs8 = [sb(CA.take(32), 8, F32) for _ in range(2)]
    g_mq8 = sb(CA.take(256), 64, F32)
    g_fk = sb(CA.take(256), 64, F32)
    zf = sb(CA.take(512), 128, F32)
    cum = sb(CA.take(512), 128, F32)
    negcum = sb(CA.take(512), 128, F32)
    totb = sb(CA.take(512), 128, F32)
    pref = sb(CA.take(512), 128, F32)
    res1 = sb(CA.take(512), 128, F32)
    kmT = sb(CA.take(128), 64, BF16)
    gate_s = [sb(CA.take(256), 64, F32) for _ in range(2)]
    cmp_s = sb(CA.take(1568), 392, F32)
    rank_s = sb(CA.take(224), 56, F32)
    junk = sb(CA.take(2048), 1024, BF16)
    CEND = CA.p

    A = Alloc(CEND, LIMIT)
    hnT_o = A.take(8 * T * 2)
    hnT = sb(hnT_o, 8 * T, BF16).rearrange("p (k t) -> p k t", k=8)
    slab_o = A.p
    SW = 72
    FW = 70
    MQ_O = A.take(NT * H * SW * 2)
    MQ = sb(MQ_O, NT * H * SW, BF16).rearrange("p (i h w) -> p i h w", i=NT, h=H)
    MK = sb(A.take(NT * H * SW * 2), NT * H * SW, BF16).rearrange("p (i h w) -> p i h w", i=NT, h=H)
    FQ = sb(A.take(NT * H * FW * 2), NT * H * FW, BF16).rearrange("p (i h w) -> p i h w", i=NT, h=H)
    FK = sb(A.take(NT * H * FW * 2), NT * H * FW, BF16).rearrange("p (i h w) -> p i h w", i=NT, h=H)
    MV_O = A.take(NT * 512 * 2)
    MV = sb(MV_O, NT * 512, BF16).rearrange("p (i h d) -> p i h d", i=NT, h=H)
    FV = sb(A.take(NT * 512 * 2), NT * 512, BF16).rearrange("p (i h d) -> p i h d", i=NT, h=H)
    slab_end = A.p
    t2_o = A.p
    wch = [sb(A.take(8 * 512 * 2), 8 * 512, BF16).rearrange("p (k c) -> p k c", k=8) for _ in range(2)]
    sq_s = [sb(A.take(2048), 512, F32) for _ in range(2)]
    t_s = [sb(A.take(2048), 512, F32) for _ in range(3)]
    t2_s = [sb(A.take(512), 128, F32) for _ in range(2)]
    rA = [sb(A.take(512), 128, F32) for _ in range(2)]
    rB = [sb(A.take(512), 128, F32) for _ in range(2)]
    qt0 = [sb(A.take(2048), 1024, BF16).rearrange("p (h q) -> p h q", h=8) for _ in range(2)]
    wlate = sb(A.take(8 * 512 * 2), 8 * 512, BF16).rearrange("p (k c) -> p k c", k=8)
    t2_end = A.p
    rope_w = sb(t2_o + 16384, 704, F32)
    B1 = Alloc(slab_o, slab_end)
    wada_s = [sb(B1.take(8 * 512 * 2), 8 * 512, BF16).rearrange("p (k c) -> p k c", k=8) for _ in range(4)]
    xg = [sb(B1.take(4 * 1024 * 4), 4096, F32).rearrange("p (s d) -> p s d", s=4) for _ in range(4)]

    B3 = Alloc(hnT_o, slab_o)
    qt = [sb(B3.take(4096), 2048, BF16) for _ in range(2)]
    kt = [sb(B3.take(4096), 2048, BF16) for _ in range(2)]
    vaug = [sb(B3.take(4096), 2048, BF16).rearrange("p (i c) -> p i c", i=NT) for _ in range(2)]
    pt = [sb(B3.take(1024), 512, BF16) for _ in range(3)]
    dlo_b3 = sb(B3.take(1024), 512, BF16)
    assert B3.p <= slab_o
    OT_O = LIMIT - 8 * T * 2
    C3 = Alloc(t2_o, OT_O)
    NNB = 3
    osb = [sb(C3.take(2048), 512, F32) for _ in range(NNB)]
    dhi = [sb(C3.take(1024), 512, BF16) for _ in range(NNB)]
    dlo = [sb(C3.take(1024), 512, BF16) for _ in range(NNB - 1)] + [dlo_b3]
    rhi = [sb(C3.take(16, 16), 4, BF16) for _ in range(NNB)]
    rlo = [sb(C3.take(16, 16), 4, F32) for _ in range(NNB)]
    srow = [sb(B3.take(2048), 512, F32) for _ in range(NNB - 1)]
    srow.append(sb(C3.take(2048), 512, F32))
    rcol = [sb(C3.take(16, 16), 4, F32) for _ in range(NNB)]
    assert B3.p <= slab_o
    oT = sb(OT_O, 8 * T, BF16).rearrange("p (k t) -> p k t", k=8)

    B4 = Alloc(CEND, OT_O)
    x1 = sb(B4.take(NT * D * 4), NT * D, F32).rearrange("p (i d) -> p i d", i=NT)
    hn2T = sb(B4.take(8 * T * 2), 8 * T, BF16).rearrange("p (k t) -> p k t", k=8)
    T5_O = B4.p
    xn2_0 = sb(B4.take(4 * 1024 * 4), 4096, F32).rearrange("p (s d) -> p s d", s=4)
    xn2_1 = sb(B4.take(4 * 1024 * 4), 4096, F32).rearrange("p (s d) -> p s d", s=4)
    xn2 = [xn2_0, xn2_1]
    woutb = sb(MV_O, 8 * 1024, BF16).rearrange("p (k c) -> p k c", k=8)
    stage = [sb(MQ_O + 4096 * q_, 1024, F32) for q_ in range(2)]
    B5 = Alloc(T5_O, LIMIT)
    GSZ = [6, 6, 5, 5]
    hT = sb(B5.take(6 * T * 2), 6 * T, BF16).rearrange("p (m t) -> p m t", m=6)
    wdnb = sb(B5.take(6 * 1024 * 2), 6 * 1024, BF16).rearrange("p (m c) -> p m c", m=6)
    wupb = [sb(B5.take(2 * 8 * 128 * 2), 2048, BF16).rearrange("p (s k c) -> p s k c", s=2, k=8) for _ in range(2)]
    stage5 = [sb(B5.take(4096), 1024, F32) for _ in range(2)]
    HW_ = 1024
    ua = [sb(B5.take((HW_ + 2) * 4), HW_ + 2, F32) for _ in range(2)]
    uv = [sb(B5.take((HW_ + 2) * 4), HW_ + 2, F32) for _ in range(2)]
    ca = [sb(B5.take(HW_ * 4), HW_, F32) for _ in range(2)]
    cv = [sb(B5.take(HW_ * 4), HW_, F32) for _ in range(2)]

    def V(eng, name, *args, reads=(), writes=(), **kw):
        def fn(e):
            return getattr(e, name)(*args, **kw)
        return P.op(eng, fn, reads=reads, writes=writes)

    def act(out, in_, func, bias=None, scale=None, accum_out=None, extra_reads=()):
        kw = {}
        rd = [in_] + list(extra_reads)
        if bias is not None:
            kw["bias"] = bias
            if not isinstance(bias, float):
                rd.append(bias)
        if scale is not None:
            kw["scale"] = scale
            if not isinstance(scale, float):
                rd.append(scale)
        wr = [out]
        if accum_out is not None:
            kw["accum_out"] = accum_out
            wr.append(accum_out)
        return P.op("act", lambda e: e.activation(out=out, in_=in_, func=func, **kw), reads=rd, writes=wr)

    def tt(eng, out, in0, in1, op):
        return P.op(eng, lambda e: e.tensor_tensor(out=out, in0=in0, in1=in1, op=op), reads=[in0, in1], writes=[out])

    def ts(eng, out, in0, s1, s2, op0, op1=None):
        rd = [in0]
        if not isinstance(s1, (float, int)):
            rd.append(s1)
        if s2 is not None and not isinstance(s2, (float, int)):
            rd.append(s2)
        if op1 is None:
            return P.op(eng, lambda e: e.tensor_scalar(out=out, in0=in0, scalar1=s1, scalar2=None, op0=op0),
                        reads=rd, writes=[out])
        return P.op(eng, lambda e: e.tensor_scalar(out=out, in0=in0, scalar1=s1, scalar2=s2, op0=op0, op1=op1),
                    reads=rd, writes=[out])

    def stt(eng, out, in0, scalar, in1, op0, op1):
        rd = [in0, in1]
        if not isinstance(scalar, (float, int)):
            rd.append(scalar)
        return P.op(eng, lambda e: e.scalar_tensor_tensor(out=out, in0=in0, scalar=scalar, in1=in1, op0=op0, op1=op1),
                    reads=rd, writes=[out])

    def copy(eng, out, in_):
        return P.op(eng, lambda e: e.tensor_copy(out=out, in_=in_), reads=[in_], writes=[out])

    def memset(eng, ap, val):
        return P.op(eng, lambda e: e.memset(ap, val), writes=[ap])

    def mm(out, lhsT, rhs, start, stop):
        return P.op("pe", lambda e: e.matmul(out, lhsT=lhsT, rhs=rhs, start=start, stop=stop),
                    reads=[lhsT, rhs], writes=[out])

    def tr(out, in_, ident):
        return P.op("pe", lambda e: e.transpose(out=out, in_=in_, identity=ident), reads=[in_, ident], writes=[out])

    def dma(eng, out, in_):
        return P.dma(eng, lambda e: e.dma_start(out=out, in_=in_), reads=[in_], writes=[out])

    def dump(name, ap, shape):
        if not debug:
            return
        d = nc.dram_tensor("dbg_" + name, list(shape), ap.dtype, kind="ExternalOutput").ap()
        dbg[name] = d
        dma("sp", d, ap)

    def warmup(n, bank, rhs):
        pbw = pbank(bank)
        for _ in range(n):
            mm(pbw, ident_b, rhs, True, True)

    dma("sp", smalls, smalls_d)
    dma("sp", gtb, bcast_d)

    def asel(ap, cmp_op, fill, step, cm):
        return P.op("pool", lambda e: e.affine_select(out=ap, in_=ap, pattern=[[step, 128]], compare_op=cmp_op,
                                                      fill=fill, base=0, channel_multiplier=cm),
                    reads=[ap], writes=[ap])
    memset("pool", ident_f, 1.0)
    asel(ident_f, ALU.is_equal, 0.0, -1, 1)
    memset("pool", tri_f, 1.0)
    asel(tri_f, ALU.is_ge, 0.0, 1, -1)
    memset("pool", res1, 0.0)
    asel(res1, ALU.is_ge, NEG, 1, -1)
    copy("pool", tribias_b, res1)
    copy("pool", ident_b, ident_f)
    memset("pool", ones_f, 1.0)
    memset("pool", ones_b, 1.0)
    memset("pool", epsc[:, 0:1], EPS)
    memset("pool", epsc[:, 1:2], 0.0)
    memset("pool", epsc[:, 2:3], 1.0)
    memset("pool", invc, 1.0 / 256.0)
    memset("pool", zc4, 0.0)
    eps_ap = epsc[:, 0:1]
    zero_ap = epsc[:, 1:2]
    one_ap = epsc[:, 2:3]

    act(scf, smalls[:, S_C:S_C + 8], AF.Silu)
    copy("dve", scb16, scf)
    copy("dve", scbc.rearrange("p (k m) -> p k m", k=8), scf.unsqueeze(2).to_broadcast([128, 8, 128]))
    scbc3 = scbc.rearrange("p (k m) -> p k m", k=8)

    wada_v = wada_d.rearrange("p (k c) -> p k c", k=8)

    def ada_dma(seg, half, buf):
        dma("pool", buf, wada_v[:, :, seg * 1024 + half * 512: seg * 1024 + (half + 1) * 512])

    def ada_compute(seg, half, buf, bank, row_form=False):
        pb = pbank(bank)
        if row_form and seg not in (2, 5):
            for k in range(8):
                mm(pb, scbc3[:, k, :], buf[:, k, :], k == 0, k == 7)
            rowf = junk.bitcast(F32)
            copy("dve", rowf[0:1, :], pb[0:1, :])
            pc_ = pbank(6)[:, 300:304]
            for jj in range(4):
                tr(pc_[:, jj:jj + 1], rowf[0:1, jj * 128:(jj + 1) * 128], ones_f[0:1, 0:1])
            c0 = seg * 8 + half * 4
            tt("dve", modfm[:, c0:c0 + 4], pc_, smalls[:, S_BADA + c0: S_BADA + c0 + 4], ALU.add)
            return
        if seg in (2, 5):
            slot = 0 if seg == 2 else 1
            for k in range(8):
                mm(pb, scbc3[:, k, :], buf[:, k, :], k == 0, k == 7)
            dst = gtb[:, slot * 1024 + half * 512: slot * 1024 + (half + 1) * 512]
            tt("dve", dst, pb, dst, ALU.add)
        else:
            for jj in range(4):
                for k in range(8):
                    mm(pb[:, jj:jj + 1], buf[:, k, jj * 128:(jj + 1) * 128], scb16[:, k:k + 1], k == 0, k == 7)
            c0 = seg * 8 + half * 4
            tt("dve", modfm[:, c0:c0 + 4], pb[:, 0:4], smalls[:, S_BADA + c0: S_BADA + c0 + 4], ALU.add)

    def make_ab(ab, gcol, scseg, shseg):
        ts("dve", ab[:, 0:8], modfm[:, scseg * 8:scseg * 8 + 8], 1.0, None, ALU.add)
        tt("dve", ab[:, 0:8], ab[:, 0:8], smalls[:, gcol:gcol + 8], ALU.mult)
        copy("dve", ab[:, 8:16], modfm[:, shseg * 8:shseg * 8 + 8])

    early = [(1, 0), (1, 1), (0, 0), (0, 1)]
    for pi, (seg, half) in enumerate(early):
        ada_dma(seg, half, wada_s[pi])
    late_pieces = [(4, 0), (4, 1), (3, 0), (3, 1), (2, 0), (2, 1), (5, 0), (5, 1)]
    ts("dve", g_mq8, smalls[:, S_MQG:S_MQG + 64], 0.125, None, ALU.mult)
    tt("dve", g_fk, smalls[:, S_FQG:S_FQG + 64], smalls[:, S_FKG:S_FKG + 64], ALU.mult)
    ts("dve", g_fk, g_fk, 0.125, None, ALU.mult)
    CSv = smalls[:, S_CS:S_CS + 256].rearrange("p (i c) -> p i c", i=NT)
    NSv = smalls[:, S_NS:S_NS + 256].rearrange("p (i c) -> p i c", i=NT)
    if ROPE_ON_DEVICE:
        TWO_PI = 6.283185307179586
        C1 = 6.28125
        C2 = TWO_PI - C1
        pos_i = rope_w[:, 0:16].bitcast(mybir.dt.int32)
        pos_f = rope_w[:, 16:32]
        invf = rope_w[:, 32:40]
        ang = rope_w[:, 64:192].rearrange("p (i j) -> p i j", i=NT)
        kf = rope_w[:, 192:320].rearrange("p (i j) -> p i j", i=NT)
        ki = rope_w[:, 320:448].bitcast(mybir.dt.int32).rearrange("p (i j) -> p i j", i=NT)
        rr = rope_w[:, 448:576].rearrange("p (i j) -> p i j", i=NT)
        cr = rope_w[:, 576:704].rearrange("p (i j) -> p i j", i=NT)
        P.op("pool", lambda e: e.iota(pos_i, pattern=[[128, 16]], base=0, channel_multiplier=1), writes=[pos_i])
        copy("dve", pos_f, pos_i)
        for j_ in range(8):
            memset("pool", invf[:, j_:j_ + 1], float(np.power(np.float32(ROPE_THETA), np.float32(-2.0 * j_ / 16))))
        tt("dve", ang, pos_f.unsqueeze(2).to_broadcast([128, NT, 8]), invf.unsqueeze(1).to_broadcast([128, NT, 8]), ALU.mult)

        def reduce_to_pi(dst, shift):
            ts("dve", kf, ang, 1.0 / TWO_PI, shift / TWO_PI, ALU.mult, ALU.add)
            copy("dve", ki, kf)
            copy("dve", kf, ki)
            stt("dve", dst, kf, -C1, ang, ALU.mult, ALU.add)
            stt("dve", dst, kf, -C2, dst, ALU.mult, ALU.add)
            if shift != 0.0:
                ts("dve", dst, dst, shift, None, ALU.add)
            ts("dve", kf, dst, math.pi, -TWO_PI, ALU.is_gt, ALU.mult)
            tt("dve", dst, dst, kf, ALU.add)
            ts("dve", kf, dst, -math.pi, TWO_PI, ALU.is_lt, ALU.mult)
            tt("dve", dst, dst, kf, ALU.add)
        reduce_to_pi(rr, 0.0)
        reduce_to_pi(cr, math.pi / 2)
        act(rr, rr, AF.Sin)
        act(cr, cr, AF.Sin)
        copy("dve", CSv[:, :, 0:8], cr)
        copy("dve", CSv[:, :, 8:16], cr)
        copy("dve", NSv[:, :, 8:16], rr)
        ts("dve", NSv[:, :, 0:8], rr, -1.0, None, ALU.mult)


    def norm_rstd(i, src):
        act(junk, src, AF.Square, accum_out=ss_a[:, i:i + 1])
        act(rstd_a[:, i:i + 1], ss_a[:, i:i + 1], AF.Sqrt, bias=eps_ap, scale=1.0 / D)
        P.op("dve", (lambda o, a: (lambda e: e.reciprocal(out=o, in_=a)))(rstd_a[:, i:i + 1], rstd_a[:, i:i + 1]),
             reads=[rstd_a[:, i:i + 1]], writes=[rstd_a[:, i:i + 1]])

    def norm_stats(i, src, dst):
        norm_rstd(i, src)
        ts("dve", dst, src, rstd_a[:, i:i + 1], None, ALU.mult)

    trb = [0]

    def norm_tr(g, xb, ab, dstT, tr_banks=(0, 1, 2, 3), split=False):
        for k in range(8):
            pb = pbank(tr_banks[trb[0] % len(tr_banks)])
            trb[0] += 1
            for s_ in range(4):
                tr(pb[:, s_ * 128:(s_ + 1) * 128], xb[:, s_, k * 128:(k + 1) * 128], ident_f)
            if split and k % 2 == 1:
                ts("dve", dstT[:, k, g * 512:(g + 1) * 512], pb, ab[:, k:k + 1], ab[:, 8 + k:9 + k], ALU.mult, ALU.add)
            else:
                act(dstT[:, k, g * 512:(g + 1) * 512], pb, AF.Identity, bias=ab[:, 8 + k:9 + k], scale=ab[:, k:k + 1])

    x_t = x_d.rearrange("(i p) d -> i p d", p=128)
    memset("dve", ss_a, 0.0)
    for i in range(NT):
        dst_ = xg[i // 4][:, i % 4, :]
        dma("sp", dst_, x_t[i])
        norm_stats(i, dst_, dst_)
    for pi, (seg, half) in enumerate(early):
        ada_compute(seg, half, wada_s[pi], 6 + pi % 2)
    make_ab(ab1, S_GMIX, 1, 0)
    for g in range(4):
        norm_tr(g, xg[g], ab1, hnT, split=True)
    dump("hnT", hnT, [128, 8, T])
    if stop_after == "T1":
        P.emit()
        return nc, dbg

    win_v = win_d.rearrange("p (k c) -> p k c", k=8)
    memset("pool", MQ[:, :, :, 64:72], 0.0)
    memset("pool", MK[:, :, :, 64:72], 0.0)
    for n in range(8):
        memset("pool", MK[:, 2 * n:2 * n + 2, :, 64 + n:65 + n], 1.0)
    memset("pool", FK[:, :, :, 64:67], 1.0)
    memset("pool", FQ[:, :, :, 67:70], 1.0)

    CSv = smalls[:, S_CS:S_CS + 256].rearrange("p (i c) -> p i c", i=NT)
    NSv = smalls[:, S_NS:S_NS + 256].rearrange("p (i c) -> p i c", i=NT)
    groups = ["mq", "mk", "fl", "mv", "fq", "fk", "fv"]
    gcol = {"mq": 0, "mk": 512, "mv": 1024, "fq": 1536, "fk": 2048, "fv": 2560, "fl": 3072}
    zf3 = zf.rearrange("p (i h) -> p i h", i=NT)
    seq = []
    for i_ in range(NT):
        seq += [("mq", i_), ("mv", i_)]
    for t_ in range(NT + 3):
        if t_ < NT:
            seq.append(("mk", t_))
        if t_ >= 3:
            seq.append(("fv", t_ - 3))
    for g_ in ("fl", "fq", "fk"):
        seq += [(g_, i_) for i_ in range(NT)]
    units = [(0, g_, i_) for (g_, i_) in seq]
    NU = len(units)
    pos_of = {(g_, i_): n_ for n_, (g_, i_) in enumerate(seq)}
    wfl = wflb[:, :, 0:8]
    gbuf = {"mq": wch[0], "mv": wch[1], "mk": wlate, "fv": wch[0], "fl": wfl, "fq": wch[1], "fk": wch[0]}

    def w_dma(g_):
        c0_ = gcol[g_]
        if g_ == "fl":
            dma("pool", wfl, win_v[:, :, c0_:c0_ + 8])
        else:
            dma("pool", gbuf[g_], win_v[:, :, c0_:c0_ + 512])
    for g_ in ("mq", "mv", "mk", "fl"):
        w_dma(g_)
    wdma_at = {pos_of[("mq", 15)] + 1: ["fv"], pos_of[("mv", 15)] + 2: ["fq"], pos_of[("fv", 15)] + 2: ["fk"]}
    warmup(16, 4, hnT[:, 0, 0:512])

    def pe_unit(n):
        gi, gname, i = units[n]
        for g_ in wdma_at.get(n, []):
            w_dma(g_)
        wb = gbuf[gname]
        pb = pbank(n % 4)
        ncol = 8 if gname == "fl" else 512
        for k in range(8):
            mm(pb[:, 0:ncol], hnT[:, k, i * 128:(i + 1) * 128], wb[:, k, 0:ncol], k == 0, k == 7)

    def stage_a(n):
        gi, gname, i = units[n]
        pb = pbank(n % 4)
        sl = n % 2
        pb3 = pb.rearrange("p (h d) -> p h d", h=8)
        if gname == "fl":
            tt("dve", zf3[:, i, :], pb[:, 0:8], smalls[:, S_BF:S_BF + 8], ALU.add)
            return
        if gname in ("mv", "fv"):
            dst = MV if gname == "mv" else FV
            act(dst[:, i], pb3, AF.Copy)
            return
        act(sq_s[sl], pb, AF.Square)
        P.op("dve", (lambda o, a: (lambda e: e.tensor_reduce(out=o, in_=a, axis=AX.X, op=ALU.add)))(
            ss8[sl], sq_s[sl].rearrange("p (h d) -> p h d", h=8)),
            reads=[sq_s[sl]], writes=[ss8[sl]])
        act(sd8[sl], ss8[sl], AF.Sqrt, bias=eps_ap, scale=1.0 / DH)

    def stage_a2(n):
        gi, gname, i = units[n]
        if gname in ("fl", "mv", "fv"):
            return
        pb = pbank(n % 4)
        sl = n % 2
        pb3 = pb.rearrange("p (h d) -> p h d", h=8)
        P.op("dve", (lambda o, a: (lambda e: e.reciprocal(out=o, in_=a)))(rs8[sl], sd8[sl]),
             reads=[sd8[sl]], writes=[rs8[sl]])
        rsb = rs8[sl].unsqueeze(2).to_broadcast([128, 8, 64])
        if gname == "fq":
            tt("dve", FQ[:, i, :, 0:64], pb3, rsb, ALU.mult)
            return
        t3 = t_s[n % 3].rearrange("p (h d) -> p h d", h=8)
        tt("dve", t3, pb3, rsb, ALU.mult)

    def stage_b(n):
        gi, gname, i = units[n]
        sl = n % 2
        if gname not in ("mq", "mk", "fk"):
            return
        t3 = t_s[n % 3].rearrange("p (h d) -> p h d", h=8)
        if gname == "fk":
            tt("pool", FK[:, i, :, 0:64], t3, g_fk.unsqueeze(1).to_broadcast([128, 8, 64]), ALU.mult)
            return
        gsrc = g_mq8 if gname == "mq" else smalls[:, S_MKG:S_MKG + 64]
        dst = MQ if gname == "mq" else MK
        u3 = t2_s[sl].rearrange("p (h d) -> p h d", h=8)
        tt("pool", u3, t3[:, :, 0:16], gsrc[:, 0:16].unsqueeze(1).to_broadcast([128, 8, 16]), ALU.mult)
        tt("pool", dst[:, i, :, 16:64], t3[:, :, 16:64], gsrc[:, 16:64].unsqueeze(1).to_broadcast([128, 8, 48]), ALU.mult)
        a3 = rA[sl].rearrange("p (h c) -> p h c", h=8)
        b3 = rB[sl].rearrange("p (h c) -> p h c", h=8)
        tt("dve", a3, u3, CSv[:, i].unsqueeze(1).to_broadcast([128, 8, 16]), ALU.mult)
        tt("dve", b3[:, :, 0:8], u3[:, :, 8:16], NSv[:, i, 0:8].unsqueeze(1).to_broadcast([128, 8, 8]), ALU.mult)
        tt("dve", b3[:, :, 8:16], u3[:, :, 0:8], NSv[:, i, 8:16].unsqueeze(1).to_broadcast([128, 8, 8]), ALU.mult)
        tt("dve", dst[:, i, :, 0:16], a3, b3, ALU.add)

    def cum_part(part):
        cum3 = cum.rearrange("p (i h) -> p i h", i=NT)
        r3 = res1.rearrange("p (i h) -> p i h", i=NT)
        pref3 = pref.rearrange("p (i h) -> p i h", i=NT)
        tot3 = totb.rearrange("p (i h) -> p i h", i=NT)
        if part == 0:
            act(zf, zf, AF.Exp, scale=-1.0)
            act(zf, zf, AF.Ln, bias=one_ap, scale=1.0)
            ts("dve", zf, zf, -1.0, None, ALU.mult)
            pbc = pbank(6)[:, 0:128]
            pbt = pbank(6)[:, 128:256]
            mm(pbc, tri_f, zf, True, True)
            mm(pbt, ones_f, zf, True, True)
            copy("dve", totb, pbt)
            memset("dve", pref3[:, 0, :], 0.0)
            for i in range(1, 8):
                tt("dve", pref3[:, i, :], pref3[:, i - 1, :], tot3[:, i - 1, :], ALU.add)
        elif part == 1:
            for i in range(8, NT):
                tt("dve", pref3[:, i, :], pref3[:, i - 1, :], tot3[:, i - 1, :], ALU.add)
            tt("dve", cum, pbank(6)[:, 0:128], pref, ALU.add)
            ts("dve", negcum, cum, -1.0, None, ALU.mult)
        else:
            copy("dve", FQ[:, :, :, 64], cum3)
            tt("dve", r3, cum3, FQ[:, :, :, 64], ALU.subtract)
            copy("dve", FQ[:, :, :, 65], r3)
            tt("dve", r3, r3, FQ[:, :, :, 65], ALU.subtract)
            copy("dve", FQ[:, :, :, 66], r3)
            for q_ in range(3):
                ts("dve", FK[:, :, :, 67 + q_], FQ[:, :, :, 64 + q_], -1.0, None, ALU.mult)

    kmT3 = kmT.rearrange("p (h n) -> p h n", h=8)

    def kmean_block():
        pbk = pbank(6)
        for h in range(H):
            for n_ in range(7):
                for s_ in range(2):
                    mm(pbk[0:64, h * 8 + n_: h * 8 + n_ + 1], MK[:, 2 * n_ + s_, h, 0:64], invc[:, 0:1], s_ == 0, s_ == 1)
        memset("dve", kmT[0:64, :], 0.0)
        for h in range(H):
            copy("dve", kmT3[0:64, h, 0:7], pbk[0:64, h * 8: h * 8 + 7])

    def gate_tile_a(i):
        sl = i % 2
        pbt_ = pbank_bf(5)
        for h in range(H):
            tr(pbt_[0:64, h * 128:(h + 1) * 128], MQ[:, i, h, 0:64], ident_b)
        copy("dve", qt0[sl][0:64], pbt_[0:64, 0:1024].rearrange("p (h q) -> p h q", h=8))

    def gate_tile(i):
        b = i // 2
        sl = i % 2
        pbg = pbank(7)
        for h in range(H):
            mm(pbg[:, h * 8:(h + 1) * 8], qt0[sl][0:64, h, :], kmT3[0:64, h, :], True, True)
        copy("dve", gate_s[sl], pbg[:, 0:64])
        g3 = gate_s[sl].rearrange("p (h n) -> p h n", h=8)
        cmp4 = cmp_s[:, 0:8 * b * b].rearrange("p (h n m) -> p h n m", h=8, n=b)
        in0 = g3[:, :, 0:b].unsqueeze(2).to_broadcast([128, 8, b, b])
        in1 = g3[:, :, 0:b].unsqueeze(3).to_broadcast([128, 8, b, b])
        tt("dve", cmp4, in0, in1, ALU.is_gt)
        rk = rank_s[:, 0:8 * b].rearrange("p (h n) -> p h n", h=8)
        P.op("dve", (lambda o, a: (lambda e: e.tensor_reduce(out=o, in_=a, axis=AX.X, op=ALU.add)))(rk, cmp4),
             reads=[cmp4], writes=[rk])
        ts("dve", MQ[:, i, :, 64:64 + b], rk, 3.0, NEG, ALU.is_ge, ALU.mult)

    n_km = pos_of[("mk", 15)] + 6
    n_cum = pos_of[("fl", 15)] + 5
    extra_at = {n_km: [kmean_block], n_cum: [lambda: cum_part(0)], n_cum + 4: [lambda: cum_part(1)],
                n_cum + 8: [lambda: cum_part(2)]}
    for i_ in range(8, NT):
        extra_at.setdefault(n_km + 2 + 3 * (i_ - 8), []).append((lambda ii: (lambda: gate_tile_a(ii)))(i_))
        extra_at.setdefault(n_km + 4 + 3 * (i_ - 8), []).append((lambda ii: (lambda: gate_tile(ii)))(i_))
    n_late = pos_of[("mk", 15)] + 4

    for n in range(NU + 6):
        if n < NU:
            pe_unit(n)
        lk = n - n_late
        if lk >= 0 and lk % 6 == 0:
            if 1 <= lk // 6 <= len(late_pieces):
                sg, hf = late_pieces[lk // 6 - 1]
                ada_compute(sg, hf, wlate, 4, row_form=True)
            if lk // 6 < len(late_pieces):
                sg, hf = late_pieces[lk // 6]
                ada_dma(sg, hf, wlate)
        if 0 <= n - 1 < NU:
            stage_a(n - 1)
        if 0 <= n - 4 < NU:
            stage_b(n - 4)
        if 0 <= n - 2 < NU:
            stage_a2(n - 2)
        for f_ in extra_at.get(n, []):
            f_()
    make_ab(ab2, S_GFFN, 4, 3)
    dump("modfm", modfm, [128, 48])
    dump("gtb", gtb, [128, 2048])
    dump("cum", cum, [128, 128])

    dump("MQ", MQ, [128, NT, H, SW])
    dump("MK", MK, [128, NT, H, SW])
    dump("FQ", FQ, [128, NT, H, FW])
    dump("FK", FK, [128, NT, H, FW])
    dump("MV", MV, [128, NT, H, 64])
    dump("FV", FV, [128, NT, H, 64])
    if stop_after == "T2":
        P.emit()
        return nc, dbg

    memset("pool", vaug[0][:, :, 64:128], 0.0)
    memset("pool", vaug[0][:, :, 64:65], 1.0)
    memset("pool", vaug[1][:, :, 0:64], 0.0)
    memset("pool", vaug[1][:, :, 0:1], 1.0)
    negcum3 = negcum.rearrange("p (i h) -> p i h", i=NT)
    s_banks = (0, 1, 2)
    o_banks = (3, 4)
    tr_banks = (5,)
    LOOK = 2
    PRE = 34
    DEFER = 8

    def head_cfg(hg):
        typ = 0 if hg < 8 else 1
        h = hg % 8
        QS, KS, VS, W = (MQ, MK, MV, SW) if typ == 0 else (FQ, FK, FV, FW)
        par = hg % 2
        return dict(typ=typ, h=h, QS=QS, KS=KS, VS=VS, W=W, par=par, pair=hg // 2,
                    qtb=qt[hg % 2], ktb=kt[hg % 2], va=vaug[par],
                    Mv=65 if par == 0 else 128,
                    rows=slice(0, 64) if par == 0 else slice(64, 128),
                    srp=64 if par == 0 else 0)

    trc = [0]

    trbank_of = {}

    def prologue_part(hg, part):
        cf = head_cfg(hg)
        h = cf["h"]
        if part == 0:
            if cf["par"] == 0:
                copy("pool", vaug[0][:, :, 0:64], cf["VS"][:, :, h, :])
            else:
                copy("pool", vaug[1][:, :, 64:128], cf["VS"][:, :, h, :])
            return
        W = cf["W"]
        grp = (part - 1) // 2
        sub = (part - 1) % 2
        S_, dstT = ((cf["QS"], cf["qtb"]), (cf["KS"], cf["ktb"]))[grp // 2]
        half = grp % 2
        if sub == 0:
            tb_ = (5, 6, 7) if hg == 0 else tr_banks
            trbank_of[(hg, grp)] = pbank_bf(tb_[trc[0] % len(tb_)])
            trc[0] += 1
        pbt_ = trbank_of[(hg, grp)]
        for s_ in range(4 * sub, 4 * sub + 4):
            i = half * 8 + s_
            tr(pbt_[0:W, s_ * 128:(s_ + 1) * 128], S_[:, i, h, 0:W], ident_b)
        if sub == 1:
            copy("dve", dstT[0:W, half * 1024:(half + 1) * 1024], pbt_[0:W, 0:1024])

    steps = []
    for hg in range(16):
        for c in range(4):
            for j in range(4 * c + 4):
                steps.append((hg, c, j))
    NS_ = len(steps)
    first_step = {}
    for si, (hg, c, j) in enumerate(steps):
        if hg not in first_step:
            first_step[hg] = si
    pro_at = {}
    for hg in range(16):
        for part in range(9):
            at = max(0, first_step[hg] - PRE + 3 * part) if hg > 0 else 0
            pro_at.setdefault(at, []).append((hg, part))
    st_info = [None] * NS_
    obank_of = {}
    ocnt = 0
    deferred = []
    ncnt = [0]

    def do_S(si):
        hg, c, j = steps[si]
        cf = head_cfg(hg)
        W = cf["W"]
        q0 = max(512 * c, 128 * j)
        N = 512 * (c + 1) - q0
        sbk = pbank(s_banks[si % 3])
        diag = j >= 4 * c
        mm(sbk[:, 0:N], cf["ktb"][0:W, j * 128:(j + 1) * 128], cf["qtb"][0:W, q0:q0 + N], True, not diag)
        if diag:
            mm(sbk[:, 0:128], ident_b, tribias_b, False, True)
        st_info[si] = (q0, N, sbk)

    def do_rest(si):
        nonlocal ocnt
        hg, c, j = steps[si]
        cf = head_cfg(hg)
        q0, N, sbk = st_info[si]
        if j == 0:
            obank_of[(hg, c)] = pbank(o_banks[ocnt % 2])
            ocnt += 1
        ob = obank_of[(hg, c)]
        ptb = pt[si % 3]
        act(ptb[:, 0:N], sbk[:, 0:N], AF.Exp, bias=0.0, scale=1.0)
        last = (j == 4 * c + 3)
        mm(ob[0:cf["Mv"], q0 - 512 * c:512], cf["va"][:, j, 0:cf["Mv"]], ptb[:, 0:N], j == 0, last)
        if last:
            nb = ncnt[0] % NNB
            ncnt[0] += 1
            srp, rows, pair = cf["srp"], cf["rows"], cf["pair"]
            copy("dve", srow[nb][srp:srp + 1, :], ob[srp:srp + 1, :])
            copy("dve", osb[nb][rows, :], ob[rows, :])

            def mk_pcol(j4, nb=nb, srp=srp):
                def f():
                    pcol = pbank(6)[:, 0:4]
                    tr(pcol[:, j4:j4 + 1], srow[nb][srp:srp + 1, j4 * 128:(j4 + 1) * 128], ones_f[srp:srp + 1, 0:1])
                    if j4 == 3:
                        P.op("dve", (lambda o, a: (lambda e: e.reciprocal(out=o, in_=a)))(rcol[nb], pcol),
                             reads=[pcol], writes=[rcol[nb]])
                        copy("dve", rhi[nb], rcol[nb])
                        tt("dve", rlo[nb], rcol[nb], rhi[nb], ALU.subtract)
                        idb = ident_b.unsqueeze(1).to_broadcast([128, 4, 128])
                        tt("dve", dhi[nb].rearrange("p (j q) -> p j q", j=4), idb,
                           rhi[nb].unsqueeze(2).to_broadcast([128, 4, 128]), ALU.mult)
                        tt("dve", dlo[nb].rearrange("p (j q) -> p j q", j=4), idb,
                           rlo[nb].unsqueeze(2).to_broadcast([128, 4, 128]), ALU.mult)
                return f

            def fin_hi(nb=nb):
                mm(pbank(7), ones_b, dhi[nb], True, False)

            def fin_lo(nb=nb, rows=rows, pair=pair, c=c):
                bcb = pbank(7)
                mm(bcb, ones_b, dlo[nb], False, True)
                tt("dve", oT[rows, pair, c * 512:(c + 1) * 512], osb[nb][rows, :], bcb[rows, :], ALU.mult)
            base = si + LOOK
            for j4 in range(4):
                deferred.append((base + 6 + j4, mk_pcol(j4)))
            deferred.append((base + 18, fin_hi))
            deferred.append((base + 19, fin_lo))

    for n in range(NS_ + LOOK + 26):
        for (hg_, part_) in pro_at.get(n, []):
            prologue_part(hg_, part_)
        wo_i = n - first_step[9] - 4
        if wo_i >= 0 and wo_i % 5 == 0 and wo_i // 5 < 8:
            p_ = wo_i // 5
            wout_v = wout_d.rearrange("p (k c) -> p k c", k=8)
            st_ = stage[p_ % 2]
            dma("sp", st_, wout_v[:, p_, :])
            tt("pool", woutb[:, p_, :], st_, gtb[:, 0:1024], ALU.mult)
        if n < NS_:
            do_S(n)
        deferred.sort(key=lambda t_: t_[0])
        while deferred and deferred[0][0] <= n:
            deferred.pop(0)[1]()
        if 0 <= n - LOOK < NS_:
            do_rest(n - LOOK)
    while deferred:
        deferred.pop(0)[1]()
    dump("oT", oT, [128, 8, T])
    if stop_after == "T3":
        P.emit()
        return nc, dbg

    for i in range(NT):
        dma("sp", x1[:, i, :], x_t[i])
    memset("dve", ss_a, 0.0)
    wup_v = wup_d.rearrange("p (m s k c) -> p m s k c", m=NM, s=2, k=8)
    dma("pool", wupb[0], wup_v[:, 0])
    dma("pool", wupb[1], wup_v[:, 1])

    def outproj_group(g):
        for i in range(4 * g, 4 * g + 4):
            yb = psum[:, (4 + 2 * (i % 2)) * 512:(4 + 2 * (i % 2)) * 512 + 1024]
            for cb in range(2):
                for p_ in range(8):
                    mm(yb[:, cb * 512:(cb + 1) * 512], oT[:, p_, i * 128:(i + 1) * 128], woutb[:, p_, cb * 512:(cb + 1) * 512],
                       p_ == 0, p_ == 7)
            tt("dve", x1[:, i, :], yb, x1[:, i, :], ALU.add)
            norm_stats(i, x1[:, i, :], xn2[g % 2][:, i % 4, :])

    for g in range(4):
        for i in range(4 * g, 4 * g + 4):
            yb = psum[:, (4 + 2 * (i % 2)) * 512:(4 + 2 * (i % 2)) * 512 + 1024]
            for cb in range(2):
                for p_ in range(8):
                    mm(yb[:, cb * 512:(cb + 1) * 512], oT[:, p_, i * 128:(i + 1) * 128], woutb[:, p_, cb * 512:(cb + 1) * 512],
                       p_ == 0, p_ == 7)
            tt("dve", x1[:, i, :], yb, x1[:, i, :], ALU.add)
            norm_rstd(i, x1[:, i, :])

    def stats_group(g):
        for i in range(4 * g, 4 * g + 4):
            ts("dve", xn2[g % 2][:, i % 4, :], x1[:, i, :], rstd_a[:, i:i + 1], None, ALU.mult)
    stats_group(0)
    for g in range(4):
        if g + 1 < 4:
            stats_group(g + 1)
        norm_tr(g, xn2[g % 2], ab2, hn2T)
    dump("x1", x1, [128, NT, D])
    dump("hn2T", hn2T, [128, 8, T])
    if stop_after == "T4":
        P.emit()
        return nc, dbg

    wup_v = wup_d.rearrange("p (m s k c) -> p m s k c", m=NM, s=2, k=8)
    wdn_v = wdn_d.rearrange("p (m c) -> p m c", m=NM)
    cw = smalls[:, S_CONVW:S_CONVW + 132].rearrange("p (i m) -> p i m", i=3)
    cbv = smalls[:, S_CONVB:S_CONVB + 44]
    out_t = out_d.rearrange("(i p) d -> i p d", p=128)
    m0 = 0
    ucnt = 0
    wcnt = 0
    scnt5 = 0
    ycnt = 0
    for g, gs in enumerate(GSZ):
        for ml in range(gs):
            st_ = stage5[scnt5 % 2]
            scnt5 += 1
            dma("sp", st_, wdn_v[:, m0 + ml, :])
            tt("dve", wdnb[:, ml, :], st_, gtb[:, 1024:2048], ALU.mult)
        for ml in range(gs):
            m = m0 + ml
            wb = wupb[m % 2]
            if 1 <= m and m + 1 < NM:
                dma("pool", wupb[(m + 1) % 2], wup_v[:, m + 1])
            for half in range(2):
                t0 = half * HW_
                bufi = ucnt % 2
                for part, (ub, cbuf) in enumerate(((ua[bufi], ca[bufi]), (uv[bufi], cv[bufi]))):
                    ch = part * NM + m
                    pb2 = psum[:, (2 * (ucnt % 2)) * 512 + 0: (2 * (ucnt % 2)) * 512 + 1024] if part == 0 else \
                        psum[:, (2 * (ucnt % 2)) * 512 + 0: (2 * (ucnt % 2)) * 512 + 1024]
                    dbk = (2 * ucnt + part) % 4
                    pb2 = psum[:, dbk * 1024: dbk * 1024 + 1024]
                    for cb in range(2):
                        for k in range(8):
                            mm(pb2[:, cb * 512:(cb + 1) * 512], wb[:, part, k, :],
                               hn2T[:, k, t0 + cb * 512: t0 + (cb + 1) * 512], k == 0, k == 7)
                    if half == 0:
                        act(ub[:, 0:2], zc4[:, 0:2], AF.Copy)
                    else:
                        prev = (ua if part == 0 else uv)[(ucnt - 1) % 2]
                        act(ub[:, 0:2], prev[:, HW_:HW_ + 2], AF.Copy)
                    act(ub[:, 2:2 + HW_], pb2, AF.Copy)
                    if part == 0:
                        act(cbuf, ub[:, 2:2 + HW_], AF.Identity, bias=cbv[:, ch:ch + 1], scale=cw[:, 2, ch:ch + 1])
                    else:
                        ts("dve", cbuf, ub[:, 2:2 + HW_], cw[:, 2, ch:ch + 1], cbv[:, ch:ch + 1], ALU.mult, ALU.add)
                    stt("dve", cbuf, ub[:, 1:1 + HW_], cw[:, 1, ch:ch + 1], cbuf, ALU.mult, ALU.add)
                    stt("dve", cbuf, ub[:, 0:HW_], cw[:, 0, ch:ch + 1], cbuf, ALU.mult, ALU.add)
                act(ca[bufi], ca[bufi], AF.Silu)
                tt("pool", hT[:, ml, t0:t0 + HW_], ca[bufi], cv[bufi], ALU.mult)
                ucnt += 1
        for i in range(NT):
            yb = psum[:, (4 + 2 * (ycnt % 2)) * 512:(4 + 2 * (ycnt % 2)) * 512 + 1024]
            ycnt += 1
            for cb in range(2):
                for ml in range(gs):
                    mm(yb[:, cb * 512:(cb + 1) * 512], hT[:, ml, i * 128:(i + 1) * 128], wdnb[:, ml, cb * 512:(cb + 1) * 512],
                       ml == 0, ml == gs - 1)
            tt("dve", x1[:, i, :], yb, x1[:, i, :], ALU.add)
            if g == len(GSZ) - 1:
                dma("sp", out_t[i], x1[:, i, :])
        m0 += gs
    P.emit()
    return nc, dbg


def _rope_tables():
    half = 8
    inv_freq = np.power(np.float32(ROPE_THETA), (-2.0 * np.arange(half, dtype=np.float32) / 16).astype(np.float32)).astype(np.float32)
    pos = np.arange(T, dtype=np.float32)
    ang = pos[:, None] * inv_freq[None, :]
    cos = np.cos(ang).astype(np.float32)
    sin = np.sin(ang).astype(np.float32)
    cs = np.concatenate([cos, cos], axis=1)
    ns = np.concatenate([-sin, sin], axis=1)
    cs = cs.reshape(NT, 128, 16).transpose(1, 0, 2).reshape(128, NT * 16)
    ns = ns.reshape(NT, 128, 16).transpose(1, 0, 2).reshape(128, NT * 16)
    return cs, ns


def _prep_shared(inp):
    f = lambda a: np.ascontiguousarray(np.asarray(a, dtype=np.float32))
    pk = lambda w, k: f(w.reshape(k, 128, -1).transpose(1, 0, 2).reshape(128, -1))
    sh = {}
    sh["w_ada"] = pk(f(inp["w_ada"])[0], 8)
    sh["w_in"] = pk(f(inp["w_in"])[0], 8)
    sh["w_out"] = pk(f(inp["w_out"])[0], 8)
    wup = f(inp["w_up"])[0]
    wup = wup.reshape(8, 128, 2, NM, 128)
    sh["w_up"] = f(wup.transpose(1, 3, 2, 0, 4).reshape(128, -1))
    sh["w_down"] = pk(f(inp["w_down"])[0], NM)
    sm = np.zeros((128, S_TOT), np.float32)
    fm = lambda v, k: f(v).reshape(k, 128).T
    sm[:, S_GMIX:S_GMIX + 8] = fm(inp["g_mix"][0], 8)
    sm[:, S_GFFN:S_GFFN + 8] = fm(inp["g_ffn"][0], 8)
    sm[:, S_BADA:S_BADA + 48] = fm(inp["b_ada"][0], 48)
    cw = f(inp["conv_w"])[0]
    for i in range(3):
        sm[:, S_CONVW + i * 44: S_CONVW + (i + 1) * 44] = fm(cw[i], 44)
    sm[:, S_CONVB:S_CONVB + 44] = fm(inp["conv_b"][0], 44)
    sm[:, S_MQG:S_MQG + 64] = f(inp["moba_q_gain"])[0][None, :]
    sm[:, S_MKG:S_MKG + 64] = f(inp["moba_k_gain"])[0][None, :]
    sm[:, S_FQG:S_FQG + 64] = f(inp["fox_q_gain"])[0][None, :]
    sm[:, S_FKG:S_FKG + 64] = f(inp["fox_k_gain"])[0][None, :]
    sm[:, S_BF:S_BF + 8] = f(inp["b_forget"])[0][None, :]
    if not ROPE_ON_DEVICE:
        cs, ns = _rope_tables()
        sm[:, S_CS:S_CS + 256] = cs
        sm[:, S_NS:S_NS + 256] = ns
    sh["smalls"] = sm
    ba = f(inp["b_ada"])[0]
    bc = np.zeros((128, 2048), np.float32)
    bc[:, 0:1024] = ba[None, 2048:3072]
    bc[:, 1024:2048] = ba[None, 5120:6144]
    sh["bcastb"] = bc
    return sh


def _core_inputs(sh, x_b, c_b):
    m = dict(sh)
    sm = sh["smalls"].copy()
    sm[:, S_C:S_C + 8] = np.asarray(c_b, np.float32).reshape(8, 128).T
    m["smalls"] = sm
    m["x"] = np.ascontiguousarray(np.asarray(x_b, np.float32))
    return m


_CACHE = {}


def kernel(**inputs):
    x = np.asarray(inputs["x"], np.float32)
    c = np.asarray(inputs["c"], np.float32)
    sh = _prep_shared(inputs)
    if "nc" not in _CACHE:
        _CACHE["nc"] = build_program(debug=False)[0]
    nc = _CACHE["nc"]
    in_maps = [_core_inputs(sh, x[b], c[b]) for b in range(8)]
    res = run_bass_kernel_spmd(nc, in_maps, core_ids=list(range(8)))
    out = np.stack([np.asarray(res.results[b]["out"], np.float32).reshape(T, D) for b in range(8)], axis=0)
    return out
```

```python
import math
import numpy as np
import concourse.bass as bass
import concourse.mybir as mybir
from concourse.bass_utils import run_bass_kernel_spmd

F32 = mybir.dt.float32
BF16 = mybir.dt.bfloat16
AF = mybir.ActivationFunctionType
ALU = mybir.AluOpType
AX = mybir.AxisListType

T = 2048
D = 1024
NT = 16
H = 8
DH = 64
DFF = 2816
NM = 22
INC = 3080
EPS = 1e-6
NEG = -30000.0
ROPE_THETA = 500000.0

SEM_EPOCH = 20000
ROPE_ON_DEVICE = True
GRAN = 128

S_C, S_GMIX, S_GFFN, S_BADA, S_CONVW, S_CONVB = 0, 8, 16, 24, 72, 204
S_MQG, S_MKG, S_FQG, S_FKG, S_BF, S_CS, S_NS = 256, 320, 384, 448, 512, 576, 832
S_TOT = 1152


class Op:
    __slots__ = ("eng", "fn", "deps", "signal", "sig_idx", "is_dma", "dma_sem", "dma_val", "seq")

    def __init__(self, eng, fn, is_dma):
        self.eng = eng
        self.fn = fn
        self.deps = []
        self.signal = is_dma
        self.sig_idx = -1
        self.is_dma = is_dma
        self.dma_sem = None
        self.dma_val = 0
        self.seq = -1


def _ap_range(ap):
    sp = str(ap.space)
    esz = mybir.dt.size(ap.dtype) if hasattr(mybir.dt, "size") else None
    if esz is None:
        esz = 2 if ap.dtype == BF16 else 4
    dims = list(ap.ap)
    row = dims[0][0]
    off = ap.offset % row if row > 0 else ap.offset
    span = 1
    for st, cnt in dims[1:]:
        span += (cnt - 1) * abs(st)
    return sp, off * esz, (off + span) * esz


class Prog:
    ENGS = ("pe", "act", "dve", "pool", "sp")

    def __init__(self, nc, n_dma_sems=12):
        self.nc = nc
        self.ops = {e: [] for e in self.ENGS}
        self.n_dma_sems = n_dma_sems
        self.sb_w = {}
        self.sb_r = {}
        self.ps = {}
        self.nseq = 0

    @staticmethod
    def _esz(dt):
        return 2 if dt == BF16 else 4

    def _range(self, ap):
        sps = str(ap.space)
        sp = "psum" if sps == "PSUM" else ("sbuf" if sps == "SB" else "dram")
        esz = self._esz(ap.dtype)
        dims = list(ap.ap)
        row = dims[0][0]
        off = ap.offset % row if row > 0 else ap.offset
        span = 1
        for st, cnt in dims[1:]:
            span += (cnt - 1) * abs(st)
        return sp, off * esz, (off + span) * esz

    def _add(self, o, reads, writes):
        deps = {}

        def add_dep(d):
            if d is not None and d is not o:
                deps[id(d)] = d

        acc = []
        for ap in reads:
            acc.append((ap, False))
        for ap in writes:
            acc.append((ap, True))
        for ap, is_w in acc:
            sp, lo, hi = self._range(ap)
            if sp == "dram":
                continue
            if sp == "psum":
                for b in range(lo // 2048, (hi - 1) // 2048 + 1):
                    st = self.ps.setdefault(b, {})
                    for e2, (op2, w2) in st.items():
                        if e2 != o.eng:
                            add_dep(op2)
                        elif o.eng != "pe" and (w2 or is_w):
                            add_dep(op2)
            else:
                for g in range(lo // GRAN, (hi - 1) // GRAN + 1):
                    add_dep(self.sb_w.get(g))
                    if is_w:
                        rs = self.sb_r.get(g)
                        if rs:
                            for r in rs.values():
                                add_dep(r)
        for ap, is_w in acc:
            sp, lo, hi = self._range(ap)
            if sp == "dram":
                continue
            if sp == "psum":
                for b in range(lo // 2048, (hi - 1) // 2048 + 1):
                    st = self.ps.setdefault(b, {})
                    prev = st.get(o.eng)
                    if prev is not None and prev[0] is o:
                        st[o.eng] = (o, prev[1] or is_w)
                    else:
                        st[o.eng] = (o, is_w)
            else:
                key = ("dma", id(o)) if o.is_dma else o.eng
                for g in range(lo // GRAN, (hi - 1) // GRAN + 1):
                    if is_w:
                        self.sb_w[g] = o
                        self.sb_r[g] = {}
                    else:
                        self.sb_r.setdefault(g, {})[key] = o
        best = {}
        dl = []
        for d in deps.values():
            if d.is_dma:
                dl.append(d)
                continue
            if d.eng == "pe" and o.eng == "pe" and not o.is_dma:
                continue
            b = best.get(d.eng)
            if b is None or d.seq > b.seq:
                best[d.eng] = d
        dl.extend(best.values())
        o.deps = dl
        for d in dl:
            d.signal = True
        o.seq = self.nseq
        self.nseq += 1
        self.ops[o.eng].append(o)
        return o

    def op(self, eng, fn, reads=(), writes=()):
        return self._add(Op(eng, fn, False), reads, writes)

    def dma(self, eng, fn, reads=(), writes=()):
        return self._add(Op(eng, fn, True), reads, writes)

    def emit(self):
        nc = self.nc
        nsig = {}
        for e in self.ENGS:
            k = 0
            for o in self.ops[e]:
                if (not o.is_dma) and o.signal:
                    k += 1
                    o.sig_idx = k
            nsig[e] = k
        esems = {}
        for e in self.ENGS:
            n_ep = max(1, (nsig[e] + SEM_EPOCH - 1) // SEM_EPOCH)
            esems[e] = [nc.alloc_semaphore(f"s_{e}_{i}") for i in range(n_ep)]
        dsems, dcount = {}, {}
        for e in self.ENGS:
            dl = [o for o in self.ops[e] if o.is_dma]
            if dl:
                n = min(self.n_dma_sems, len(dl))
                dsems[e] = [nc.alloc_semaphore(f"d_{e}_{i}") for i in range(n)]
                dcount[e] = [0] * n
                for k, o in enumerate(dl):
                    j = k % n
                    dcount[e][j] += 16
                    o.dma_sem = (e, j)
                    o.dma_val = dcount[e][j]

        def sem_of(d):
            if d.is_dma:
                e, j = d.dma_sem
                return ("d", e, j), dsems[e][j], d.dma_val
            ep = (d.sig_idx - 1) // SEM_EPOCH
            return ("c", d.eng, ep), esems[d.eng][ep], d.sig_idx - ep * SEM_EPOCH

        prog = self

        def run_engine(ename, eng):
            seen = {}
            for o in prog.ops[ename]:
                waits = {}
                for d in o.deps:
                    key, sem, val = sem_of(d)
                    if seen.get(key, 0) >= val:
                        continue
                    if key not in waits or waits[key][1] < val:
                        waits[key] = (sem, val)
                if o.is_dma:
                    e, j = o.dma_sem
                    key = ("d", e, j)
                    pv = o.dma_val - 16
                    if pv > 0 and seen.get(key, 0) < pv:
                        if key not in waits or waits[key][1] < pv:
                            waits[key] = (dsems[e][j], pv)
                for key, (sem, val) in waits.items():
                    eng.wait_ge(sem, val)
                    seen[key] = val
                ins = o.fn(eng)
                if o.is_dma:
                    e, j = o.dma_sem
                    ins.then_inc(dsems[e][j], 16)
                elif o.signal:
                    ep = (o.sig_idx - 1) // SEM_EPOCH
                    ins.then_inc(esems[ename][ep], 1)
            if ename in dsems:
                for j, s in enumerate(dsems[ename]):
                    if dcount[ename][j] > 0:
                        eng.wait_ge(s, dcount[ename][j])

        with nc.Block() as block:
            @block.tensor
            def _(e):
                run_engine("pe", e)

            @block.scalar
            def _(e):
                run_engine("act", e)

            @block.vector
            def _(e):
                run_engine("dve", e)

            @block.gpsimd
            def _(e):
                run_engine("pool", e)

            @block.sync
            def _(e):
                run_engine("sp", e)


def build_program(debug=False, stop_after=None):
    nc = bass.Bass("TRN2", target_bir_lowering=False)
    P = Prog(nc)

    def din(name, shape):
        return nc.dram_tensor(name, shape, F32, kind="ExternalInput").ap()

    x_d = din("x", [T, D])
    wada_d = din("w_ada", [128, 8 * 6144])
    win_d = din("w_in", [128, 8 * INC])
    wout_d = din("w_out", [128, 8 * D])
    wup_d = din("w_up", [128, NM * 2 * 8 * 128])
    wdn_d = din("w_down", [128, NM * D])
    smalls_d = din("smalls", [128, S_TOT])
    bcast_d = din("bcastb", [128, 2048])
    out_d = nc.dram_tensor("out", [T, D], F32, kind="ExternalOutput").ap()
    dbg = {}

    ARENA_W = 53180
    arena = nc.alloc_sbuf_tensor("arena", [128, ARENA_W], F32)
    psum = nc.alloc_psum_tensor("psum", [128, 4096], F32)

    def sb(off_b, n, dt):
        assert off_b % 4 == 0
        if dt == F32:
            assert off_b // 4 + n <= ARENA_W, (off_b, n)
            return arena[:, off_b // 4: off_b // 4 + n]
        assert off_b + 2 * n <= ARENA_W * 4, (off_b, n)
        nw = (n + 1) // 2
        return arena[:, off_b // 4: off_b // 4 + nw].bitcast(BF16)[:, 0:n]

    def pbank(b, n=512, nb=1):
        return psum[:, b * 512: b * 512 + n]

    def pbank_bf(b):
        return psum[:, b * 512:(b + 1) * 512].bitcast(BF16)

    class Alloc:
        def __init__(self, base, limit):
            self.p = base
            self.limit = limit

        def take(self, nbytes, align=128):
            self.p = (self.p + align - 1) // align * align
            o = self.p
            self.p += nbytes
            assert self.p <= self.limit, ("arena overflow", self.p, self.limit)
            return o

    LIMIT = ARENA_W * 4
    CA = Alloc(0, 27 * 1024)
    smalls = sb(CA.take(S_TOT * 4), S_TOT, F32)
    gtb = sb(CA.take(2048 * 4), 2048, F32)
    ident_f = sb(CA.take(512), 128, F32)
    tri_f = sb(CA.take(512), 128, F32)
    ones_f = sb(CA.take(512), 128, F32)
    ident_b = sb(CA.take(256), 128, BF16)
    ones_b = sb(CA.take(256), 128, BF16)
    tribias_b = sb(CA.take(256), 128, BF16)
    modfm = sb(CA.take(192), 48, F32)
    ab1 = sb(CA.take(64), 16, F32)
    ab2 = sb(CA.take(64), 16, F32)
    scf = sb(CA.take(32), 8, F32)
    scb16 = sb(CA.take(16), 8, BF16)
    scbc = sb(CA.take(2048), 1024, BF16)
    ss_a = sb(CA.take(64), 16, F32)
    rstd_a = sb(CA.take(64), 16, F32)
    epsc = sb(CA.take(16), 4, F32)
    invc = sb(CA.take(8), 4, BF16)
    zc4 = sb(CA.take(16), 4, F32)
    wflb = sb(CA.take(8 * 8 * 2), 64, BF16).rearrange("p (k c) -> p k c", k=8)
    ss8 = [sb(CA.take(32), 8, F32) for _ in range(2)]
    sd8 = [sb(CA.take(32), 8, F32) for _ in range(2)]
    rs8 = [sb(CA.take(32), 8, F32) for _ in range(2)]
    g_mq8 = sb(CA.take(256), 64, F32)
    g_fk = sb(CA.take(256), 64, F32)
    zf = sb(CA.take(512), 128, F32)
    cum = sb(CA.take(512), 128, F32)
    negcum = sb(CA.take(512), 128, F32)
    totb = sb(CA.take(512), 128, F32)
    pref = sb(CA.take(512), 128, F32)
    res1 = sb(CA.take(512), 128, F32)
    kmT = sb(CA.take(128), 64, BF16)
    gate_s = [sb(CA.take(256), 64, F32) for _ in range(2)]
    cmp_s = sb(CA.take(1568), 392, F32)
    rank_s = sb(CA.take(224), 56, F32)
    junk = sb(CA.take(2048), 1024, BF16)
    CEND = CA.p

    A = Alloc(CEND, LIMIT)
    hnT_o = A.take(8 * T * 2)
    hnT = sb(hnT_o, 8 * T, BF16).rearrange("p (k t) -> p k t", k=8)
    slab_o = A.p
    SW = 72
    FW = 70
    MQ_O = A.take(NT * H * SW * 2)
    MQ = sb(MQ_O, NT * H * SW, BF16).rearrange("p (i h w) -> p i h w", i=NT, h=H)
    MK = sb(A.take(NT * H * SW * 2), NT * H * SW, BF16).rearrange("p (i h w) -> p i h w", i=NT, h=H)
    FQ = sb(A.take(NT * H * FW * 2), NT * H * FW, BF16).rearrange("p (i h w) -> p i h w", i=NT, h=H)
    FK = sb(A.take(NT * H * FW * 2), NT * H * FW, BF16).rearrange("p (i h w) -> p i h w", i=NT, h=H)
    MV_O = A.take(NT * 512 * 2)
    MV = sb(MV_O, NT * 512, BF16).rearrange("p (i h d) -> p i h d", i=NT, h=H)
    FV = sb(A.take(NT * 512 * 2), NT * 512, BF16).rearrange("p (i h d) -> p i h d", i=NT, h=H)
    slab_end = A.p
    t2_o = A.p
    wch = [sb(A.take(8 * 512 * 2), 8 * 512, BF16).rearrange("p (k c) -> p k c", k=8) for _ in range(2)]
    sq_s = [sb(A.take(2048), 512, F32) for _ in range(2)]
    t_s = [sb(A.take(2048), 512, F32) for _ in range(3)]
    t2_s = [sb(A.take(512), 128, F32) for _ in range(2)]
    rA = [sb(A.take(512), 128, F32) for _ in range(2)]
    rB = [sb(A.take(512), 128, F32) for _ in range(2)]
    qt0 = [sb(A.take(2048), 1024, BF16).rearrange("p (h q) -> p h q", h=8) for _ in range(2)]
    wlate = sb(A.take(8 * 512 * 2), 8 * 512, BF16).rearrange("p (k c) -> p k c", k=8)
    t2_end = A.p
    rope_w = sb(t2_o + 16384, 704, F32)
    B1 = Alloc(slab_o, slab_end)
    wada_s = [sb(B1.take(8 * 512 * 2), 8 * 512, BF16).rearrange("p (k c) -> p k c", k=8) for _ in range(4)]
    xg = [sb(B1.take(4 * 1024 * 4), 4096, F32).rearrange("p (s d) -> p s d", s=4) for _ in range(4)]

    B3 = Alloc(hnT_o, slab_o)
    qt = [sb(B3.take(4096), 2048, BF16) for _ in range(2)]
    kt = [sb(B3.take(4096), 2048, BF16) for _ in range(2)]
    vaug = [sb(B3.take(4096), 2048, BF16).rearrange("p (i c) -> p i c", i=NT) for _ in range(2)]
    pt = [sb(B3.take(1024), 512, BF16) for _ in range(3)]
    dlo_b3 = sb(B3.take(1024), 512, BF16)
    assert B3.p <= slab_o
    OT_O = LIMIT - 8 * T * 2
    C3 = Alloc(t2_o, OT_O)
    NNB = 3
    osb = [sb(C3.take(2048), 512, F32) for _ in range(NNB)]
    dhi = [sb(C3.take(1024), 512, BF16) for _ in range(NNB)]
    dlo = [sb(C3.take(1024), 512, BF16) for _ in range(NNB - 1)] + [dlo_b3]
    rhi = [sb(C3.take(16, 16), 4, BF16) for _ in range(NNB)]
    rlo = [sb(C3.take(16, 16), 4, F32) for _ in range(NNB)]
    srow = [sb(B3.take(2048), 512, F32) for _ in range(NNB - 1)]
    srow.append(sb(C3.take(2048), 512, F32))
    rcol = [sb(C3.take(16, 16), 4, F32) for _ in range(NNB)]
    assert B3.p <= slab_o
    oT = sb(OT_O, 8 * T, BF16).rearrange("p (k t) -> p k t", k=8)

    B4 = Alloc(CEND, OT_O)
    x1 = sb(B4.take(NT * D * 4), NT * D, F32).rearrange("p (i d) -> p i d", i=NT)
    hn2T = sb(B4.take(8 * T * 2), 8 * T, BF16).rearrange("p (k t) -> p k t", k=8)
    T5_O = B4.p
    xn2_0 = sb(B4.take(4 * 1024 * 4), 4096, F32).rearrange("p (s d) -> p s d", s=4)
    xn2_1 = sb(B4.take(4 * 1024 * 4), 4096, F32).rearrange("p (s d) -> p s d", s=4)
    xn2 = [xn2_0, xn2_1]
    woutb = sb(MV_O, 8 * 1024, BF16).rearrange("p (k c) -> p k c", k=8)
    stage = [sb(MQ_O + 4096 * q_, 1024, F32) for q_ in range(2)]
    B5 = Alloc(T5_O, LIMIT)
    GSZ = [6, 6, 5, 5]
    hT = sb(B5.take(6 * T * 2), 6 * T, BF16).rearrange("p (m t) -> p m t", m=6)
    wdnb = sb(B5.take(6 * 1024 * 2), 6 * 1024, BF16).rearrange("p (m c) -> p m c", m=6)
    wupb = [sb(B5.take(2 * 8 * 128 * 2), 2048, BF16).rearrange("p (s k c) -> p s k c", s=2, k=8) for _ in range(2)]
    stage5 = [sb(B5.take(4096), 1024, F32) for _ in range(2)]
    HW_ = 1024
    ua = [sb(B5.take((HW_ + 2) * 4), HW_ + 2, F32) for _ in range(2)]
    uv = [sb(B5.take((HW_ + 2) * 4), HW_ + 2, F32) for _ in range(2)]
    ca = [sb(B5.take(HW_ * 4), HW_, F32) for _ in range(2)]
    cv = [sb(B5.take(HW_ * 4), HW_, F32) for _ in range(2)]

    def V(eng, name, *args, reads=(), writes=(), **kw):
        def fn(e):
            return getattr(e, name)(*args, **kw)
        return P.op(eng, fn, reads=reads, writes=writes)

    def act(out, in_, func, bias=None, scale=None, accum_out=None, extra_reads=()):
        kw = {}
        rd = [in_] + list(extra_reads)
        if bias is not None:
            kw["bias"] = bias
            if not isinstance(bias, float):
                rd.append(bias)
        if scale is not None:
            kw["scale"] = scale
            if not isinstance(scale, float):
                rd.append(scale)
        wr = [out]
        if accum_out is not None:
            kw["accum_out"] = accum_out
            wr.append(accum_out)
        return P.op("act", lambda e: e.activation(out=out, in_=in_, func=func, **kw), reads=rd, writes=wr)

    def tt(eng, out, in0, in1, op):
        return P.op(eng, lambda e: e.tensor_tensor(out=out, in0=in0, in1=in1, op=op), reads=[in0, in1], writes=[out])

    def ts(eng, out, in0, s1, s2, op0, op1=None):
        rd = [in0]
        if not isinstance(s1, (float, int)):
            rd.append(s1)
        if s2 is not None and not isinstance(s2, (float, int)):
            rd.append(s2)
        if op1 is None:
            return P.op(eng, lambda e: e.tensor_scalar(out=out, in0=in0, scalar1=s1, scalar2=None, op0=op0),
                        reads=rd, writes=[out])
        return P.op(eng, lambda e: e.tensor_scalar(out=out, in0=in0, scalar1=s1, scalar2=s2, op0=op0, op1=op1),
                    reads=rd, writes=[out])

    def stt(eng, out, in0, scalar, in1, op0, op1):
        rd = [in0, in1]
        if not isinstance(scalar, (float, int)):
            rd.append(scalar)
        return P.op(eng, lambda e: e.scalar_tensor_tensor(out=out, in0=in0, scalar=scalar, in1=in1, op0=op0, op1=op1),
                    reads=rd, writes=[out])

    def copy(eng, out, in_):
        return P.op(eng, lambda e: e.tensor_copy(out=out, in_=in_), reads=[in_], writes=[out])

    def memset(eng, ap, val):
        return P.op(eng, lambda e: e.memset(ap, val), writes=[ap])

    def mm(out, lhsT, rhs, start, stop):
        return P.op("pe", lambda e: e.matmul(out, lhsT=lhsT, rhs=rhs, start=start, stop=stop),
                    reads=[lhsT, rhs], writes=[out])

    def tr(out, in_, ident):
        return P.op("pe", lambda e: e.transpose(out=out, in_=in_, identity=ident), reads=[in_, ident], writes=[out])

    def dma(eng, out, in_):
        return P.dma(eng, lambda e: e.dma_start(out=out, in_=in_), reads=[in_], writes=[out])

    def dump(name, ap, shape):
        if not debug:
            return
        d = nc.dram_tensor("dbg_" + name, list(shape), ap.dtype, kind="ExternalOutput").ap()
        dbg[name] = d
        dma("sp", d, ap)

    def warmup(n, bank, rhs):
        pbw = pbank(bank)
        for _ in range(n):
            mm(pbw, ident_b, rhs, True, True)

    dma("sp", smalls, smalls_d)
    dma("sp", gtb, bcast_d)

    def asel(ap, cmp_op, fill, step, cm):
        return P.op("pool", lambda e: e.affine_select(out=ap, in_=ap, pattern=[[step, 128]], compare_op=cmp_op,
                                                      fill=fill, base=0, channel_multiplier=cm),
                    reads=[ap], writes=[ap])
    memset("pool", ident_f, 1.0)
    asel(ident_f, ALU.is_equal, 0.0, -1, 1)
    memset("pool", tri_f, 1.0)
    asel(tri_f, ALU.is_ge, 0.0, 1, -1)
    memset("pool", res1, 0.0)
    asel(res1, ALU.is_ge, NEG, 1, -1)
    copy("pool", tribias_b, res1)
    copy("pool", ident_b, ident_f)
    memset("pool", ones_f, 1.0)
    memset("pool", ones_b, 1.0)
    memset("pool", epsc[:, 0:1], EPS)
    memset("pool", epsc[:, 1:2], 0.0)
    memset("pool", epsc[:, 2:3], 1.0)
    memset("pool", invc, 1.0 / 256.0)
    memset("pool", zc4, 0.0)
    eps_ap = epsc[:, 0:1]
    zero_ap = epsc[:, 1:2]
    one_ap = epsc[:, 2:3]

    act(scf, smalls[:, S_C:S_C + 8], AF.Silu)
    copy("dve", scb16, scf)
    copy("dve", scbc.rearrange("p (k m) -> p k m", k=8), scf.unsqueeze(2).to_broadcast([128, 8, 128]))
    scbc3 = scbc.rearrange("p (k m) -> p k m", k=8)

    wada_v = wada_d.rearrange("p (k c) -> p k c", k=8)

    def ada_dma(seg, half, buf):
        dma("pool", buf, wada_v[:, :, seg * 1024 + half * 512: seg * 1024 + (half + 1) * 512])

    def ada_compute(seg, half, buf, bank, row_form=False):
        pb = pbank(bank)
        if row_form and seg not in (2, 5):
            for k in range(8):
                mm(pb, scbc3[:, k, :], buf[:, k, :], k == 0, k == 7)
            rowf = junk.bitcast(F32)
            copy("dve", rowf[0:1, :], pb[0:1, :])
            pc_ = pbank(6)[:, 300:304]
            for jj in range(4):
                tr(pc_[:, jj:jj + 1], rowf[0:1, jj * 128:(jj + 1) * 128], ones_f[0:1, 0:1])
            c0 = seg * 8 + half * 4
            tt("dve", modfm[:, c0:c0 + 4], pc_, smalls[:, S_BADA + c0: S_BADA + c0 + 4], ALU.add)
            return
        if seg in (2, 5):
            slot = 0 if seg == 2 else 1
            for k in range(8):
                mm(pb, scbc3[:, k, :], buf[:, k, :], k == 0, k == 7)
            dst = gtb[:, slot * 1024 + half * 512: slot * 1024 + (half + 1) * 512]
            tt("dve", dst, pb, dst, ALU.add)
        else:
            for jj in range(4):
                for k in range(8):
                    mm(pb[:, jj:jj + 1], buf[:, k, jj * 128:(jj + 1) * 128], scb16[:, k:k + 1], k == 0, k == 7)
            c0 = seg * 8 + half * 4
            tt("dve", modfm[:, c0:c0 + 4], pb[:, 0:4], smalls[:, S_BADA + c0: S_BADA + c0 + 4], ALU.add)

    def make_ab(ab, gcol, scseg, shseg):
        ts("dve", ab[:, 0:8], modfm[:, scseg * 8:scseg * 8 + 8], 1.0, None, ALU.add)
        tt("dve", ab[:, 0:8], ab[:, 0:8], smalls[:, gcol:gcol + 8], ALU.mult)
        copy("dve", ab[:, 8:16], modfm[:, shseg * 8:shseg * 8 + 8])

    early = [(1, 0), (1, 1), (0, 0), (0, 1)]
    for pi, (seg, half) in enumerate(early):
        ada_dma(seg, half, wada_s[pi])
    late_pieces = [(4, 0), (4, 1), (3, 0), (3, 1), (2, 0), (2, 1), (5, 0), (5, 1)]
    ts("dve", g_mq8, smalls[:, S_MQG:S_MQG + 64], 0.125, None, ALU.mult)
    tt("dve", g_fk, smalls[:, S_FQG:S_FQG + 64], smalls[:, S_FKG:S_FKG + 64], ALU.mult)
    ts("dve", g_fk, g_fk, 0.125, None, ALU.mult)
    CSv = smalls[:, S_CS:S_CS + 256].rearrange("p (i c) -> p i c", i=NT)
    NSv = smalls[:, S_NS:S_NS + 256].rearrange("p (i c) -> p i c", i=NT)
    if ROPE_ON_DEVICE:
        TWO_PI = 6.283185307179586
        C1 = 6.28125
        C2 = TWO_PI - C1
        pos_i = rope_w[:, 0:16].bitcast(mybir.dt.int32)
        pos_f = rope_w[:, 16:32]
        invf = rope_w[:, 32:40]
        ang = rope_w[:, 64:192].rearrange("p (i j) -> p i j", i=NT)
        kf = rope_w[:, 192:320].rearrange("p (i j) -> p i j", i=NT)
        ki = rope_w[:, 320:448].bitcast(mybir.dt.int32).rearrange("p (i j) -> p i j", i=NT)
        rr = rope_w[:, 448:576].rearrange("p (i j) -> p i j", i=NT)
        cr = rope_w[:, 576:704].rearrange("p (i j) -> p i j", i=NT)
        P.op("pool", lambda e: e.iota(pos_i, pattern=[[128, 16]], base=0, channel_multiplier=1), writes=[pos_i])
        copy("dve", pos_f, pos_i)
        for j_ in range(8):
            memset("pool", invf[:, j_:j_ + 1], float(np.power(np.float32(ROPE_THETA), np.float32(-2.0 * j_ / 16))))
        tt("dve", ang, pos_f.unsqueeze(2).to_broadcast([128, NT, 8]), invf.unsqueeze(1).to_broadcast([128, NT, 8]), ALU.mult)

        def reduce_to_pi(dst, shift):
            ts("dve", kf, ang, 1.0 / TWO_PI, shift / TWO_PI, ALU.mult, ALU.add)
            copy("dve", ki, kf)
            copy("dve", kf, ki)
            stt("dve", dst, kf, -C1, ang, ALU.mult, ALU.add)
            stt("dve", dst, kf, -C2, dst, ALU.mult, ALU.add)
            if shift != 0.0:
                ts("dve", dst, dst, shift, None, ALU.add)
            ts("dve", kf, dst, math.pi, -TWO_PI, ALU.is_gt, ALU.mult)
            tt("dve", dst, dst, kf, ALU.add)
            ts("dve", kf, dst, -math.pi, TWO_PI, ALU.is_lt, ALU.mult)
            tt("dve", dst, dst, kf, ALU.add)
        reduce_to_pi(rr, 0.0)
        reduce_to_pi(cr, math.pi / 2)
        act(rr, rr, AF.Sin)
        act(cr, cr, AF.Sin)
        copy("dve", CSv[:, :, 0:8], cr)
        copy("dve", CSv[:, :, 8:16], cr)
        copy("dve", NSv[:, :, 8:16], rr)
        ts("dve", NSv[:, :, 0:8], rr, -1.0, None, ALU.mult)


    def norm_rstd(i, src):
        act(junk, src, AF.Square, accum_out=ss_a[:, i:i + 1])
        act(rstd_a[:, i:i + 1], ss_a[:, i:i + 1], AF.Sqrt, bias=float(EPS), scale=1.0 / D)
        P.op("dve", (lambda o, a: (lambda e: e.reciprocal(out=o, in_=a)))(rstd_a[:, i:i + 1], rstd_a[:, i:i + 1]),
             reads=[rstd_a[:, i:i + 1]], writes=[rstd_a[:, i:i + 1]])

    def norm_stats(i, src, dst):
        norm_rstd(i, src)
        ts("dve", dst, src, rstd_a[:, i:i + 1], None, ALU.mult)

    trb = [0]

    def norm_tr(g, xb, ab, dstT, tr_banks=(0, 1, 2, 3), split=False):
        for k in range(8):
            pb = pbank(tr_banks[trb[0] % len(tr_banks)])
            trb[0] += 1
            for s_ in range(4):
                tr(pb[:, s_ * 128:(s_ + 1) * 128], xb[:, s_, k * 128:(k + 1) * 128], ident_f)
            if split and k % 2 == 1:
                ts("dve", dstT[:, k, g * 512:(g + 1) * 512], pb, ab[:, k:k + 1], ab[:, 8 + k:9 + k], ALU.mult, ALU.add)
            else:
                act(dstT[:, k, g * 512:(g + 1) * 512], pb, AF.Identity, bias=ab[:, 8 + k:9 + k], scale=ab[:, k:k + 1])

    x_t = x_d.rearrange("(i p) d -> i p d", p=128)
    memset("dve", ss_a, 0.0)
    for i in range(NT):
        dst_ = xg[i // 4][:, i % 4, :]
        dma("sp", dst_, x_t[i])
        norm_stats(i, dst_, dst_)
    for pi, (seg, half) in enumerate(early):
        ada_compute(seg, half, wada_s[pi], 6 + pi % 2)
    make_ab(ab1, S_GMIX, 1, 0)
    for g in range(4):
        norm_tr(g, xg[g], ab1, hnT, split=True)
    dump("hnT", hnT, [128, 8, T])
    if stop_after == "T1":
        P.emit()
        return nc, dbg

    win_v = win_d.rearrange("p (k c) -> p k c", k=8)
    memset("pool", MQ[:, :, :, 64:72], 0.0)
    memset("pool", MK[:, :, :, 64:72], 0.0)
    for n in range(8):
        memset("pool", MK[:, 2 * n:2 * n + 2, :, 64 + n:65 + n], 1.0)
    memset("pool", FK[:, :, :, 64:67], 1.0)
    memset("pool", FQ[:, :, :, 67:70], 1.0)

    CSv = smalls[:, S_CS:S_CS + 256].rearrange("p (i c) -> p i c", i=NT)
    NSv = smalls[:, S_NS:S_NS + 256].rearrange("p (i c) -> p i c", i=NT)
    groups = ["mq", "mk", "fl", "mv", "fq", "fk", "fv"]
    gcol = {"mq": 0, "mk": 512, "mv": 1024, "fq": 1536, "fk": 2048, "fv": 2560, "fl": 3072}
    zf3 = zf.rearrange("p (i h) -> p i h", i=NT)
    seq = []
    for i_ in range(NT):
        seq += [("mq", i_), ("mv", i_)]
    for t_ in range(NT + 3):
        if t_ < NT:
            seq.append(("mk", t_))
        if t_ >= 3:
            seq.append(("fv", t_ - 3))
    for g_ in ("fl", "fq", "fk"):
        seq += [(g_, i_) for i_ in range(NT)]
    units = [(0, g_, i_) for (g_, i_) in seq]
    NU = len(units)
    pos_of = {(g_, i_): n_ for n_, (g_, i_) in enumerate(seq)}
    wfl = wflb[:, :, 0:8]
    gbuf = {"mq": wch[0], "mv": wch[1], "mk": wlate, "fv": wch[0], "fl": wfl, "fq": wch[1], "fk": wch[0]}

    def w_dma(g_):
        c0_ = gcol[g_]
        if g_ == "fl":
            dma("pool", wfl, win_v[:, :, c0_:c0_ + 8])
        else:
            dma("pool", gbuf[g_], win_v[:, :, c0_:c0_ + 512])
    for g_ in ("mq", "mv", "mk", "fl"):
        w_dma(g_)
    wdma_at = {pos_of[("mq", 15)] + 1: ["fv"], pos_of[("mv", 15)] + 2: ["fq"], pos_of[("fv", 15)] + 2: ["fk"]}
    warmup(16, 4, hnT[:, 0, 0:512])

    def pe_unit(n):
        gi, gname, i = units[n]
        for g_ in wdma_at.get(n, []):
            w_dma(g_)
        wb = gbuf[gname]
        pb = pbank(n % 4)
        ncol = 8 if gname == "fl" else 512
        for k in range(8):
            mm(pb[:, 0:ncol], hnT[:, k, i * 128:(i + 1) * 128], wb[:, k, 0:ncol], k == 0, k == 7)

    def stage_a(n):
        gi, gname, i = units[n]
        pb = pbank(n % 4)
        sl = n % 2
        pb3 = pb.rearrange("p (h d) -> p h d", h=8)
        if gname == "fl":
            tt("dve", zf3[:, i, :], pb[:, 0:8], smalls[:, S_BF:S_BF + 8], ALU.add)
            return
        if gname in ("mv", "fv"):
            dst = MV if gname == "mv" else FV
            act(dst[:, i], pb3, AF.Copy)
            return
        act(sq_s[sl], pb, AF.Square)
        P.op("dve", (lambda o, a: (lambda e: e.tensor_reduce(out=o, in_=a, axis=AX.X, op=ALU.add)))(
            ss8[sl], sq_s[sl].rearrange("p (h d) -> p h d", h=8)),
            reads=[sq_s[sl]], writes=[ss8[sl]])
        act(sd8[sl], ss8[sl], AF.Sqrt, bias=float(EPS), scale=1.0 / DH)

    def stage_a2(n):
        gi, gname, i = units[n]
        if gname in ("fl", "mv", "fv"):
            return
        pb = pbank(n % 4)
        sl = n % 2
        pb3 = pb.rearrange("p (h d) -> p h d", h=8)
        P.op("dve", (lambda o, a: (lambda e: e.reciprocal(out=o, in_=a)))(rs8[sl], sd8[sl]),
             reads=[sd8[sl]], writes=[rs8[sl]])
        rsb = rs8[sl].unsqueeze(2).to_broadcast([128, 8, 64])
        if gname == "fq":
            tt("dve", FQ[:, i, :, 0:64], pb3, rsb, ALU.mult)
            return
        t3 = t_s[n % 3].rearrange("p (h d) -> p h d", h=8)
        tt("dve", t3, pb3, rsb, ALU.mult)

    def stage_b(n):
        gi, gname, i = units[n]
        sl = n % 2
        if gname not in ("mq", "mk", "fk"):
            return
        t3 = t_s[n % 3].rearrange("p (h d) -> p h d", h=8)
        if gname == "fk":
            tt("pool", FK[:, i, :, 0:64], t3, g_fk.unsqueeze(1).to_broadcast([128, 8, 64]), ALU.mult)
            return
        gsrc = g_mq8 if gname == "mq" else smalls[:, S_MKG:S_MKG + 64]
        dst = MQ if gname == "mq" else MK
        u3 = t2_s[sl].rearrange("p (h d) -> p h d", h=8)
        tt("pool", u3, t3[:, :, 0:16], gsrc[:, 0:16].unsqueeze(1).to_broadcast([128, 8, 16]), ALU.mult)
        tt("pool", dst[:, i, :, 16:64], t3[:, :, 16:64], gsrc[:, 16:64].unsqueeze(1).to_broadcast([128, 8, 48]), ALU.mult)
        a3 = rA[sl].rearrange("p (h c) -> p h c", h=8)
        b3 = rB[sl].rearrange("p (h c) -> p h c", h=8)
        tt("dve", a3, u3, CSv[:, i].unsqueeze(1).to_broadcast([128, 8, 16]), ALU.mult)
        tt("dve", b3[:, :, 0:8], u3[:, :, 8:16], NSv[:, i, 0:8].unsqueeze(1).to_broadcast([128, 8, 8]), ALU.mult)
        tt("dve", b3[:, :, 8:16], u3[:, :, 0:8], NSv[:, i, 8:16].unsqueeze(1).to_broadcast([128, 8, 8]), ALU.mult)
        tt("dve", dst[:, i, :, 0:16], a3, b3, ALU.add)

    def cum_part(part):
        cum3 = cum.rearrange("p (i h) -> p i h", i=NT)
        r3 = res1.rearrange("p (i h) -> p i h", i=NT)
        pref3 = pref.rearrange("p (i h) -> p i h", i=NT)
        tot3 = totb.rearrange("p (i h) -> p i h", i=NT)
        if part == 0:
            act(zf, zf, AF.Exp, scale=-1.0)
            act(zf, zf, AF.Ln, bias=1.0, scale=1.0)
            ts("dve", zf, zf, -1.0, None, ALU.mult)
            pbc = pbank(6)[:, 0:128]
            pbt = pbank(6)[:, 128:256]
            mm(pbc, tri_f, zf, True, True)
            mm(pbt, ones_f, zf, True, True)
            copy("dve", totb, pbt)
            memset("dve", pref3[:, 0, :], 0.0)
            for i in range(1, 8):
                tt("dve", pref3[:, i, :], pref3[:, i - 1, :], tot3[:, i - 1, :], ALU.add)
        elif part == 1:
            for i in range(8, NT):
                tt("dve", pref3[:, i, :], pref3[:, i - 1, :], tot3[:, i - 1, :], ALU.add)
            tt("dve", cum, pbank(6)[:, 0:128], pref, ALU.add)
            ts("dve", negcum, cum, -1.0, None, ALU.mult)
        else:
            copy("dve", FQ[:, :, :, 64], cum3)
            tt("dve", r3, cum3, FQ[:, :, :, 64], ALU.subtract)
            copy("dve", FQ[:, :, :, 65], r3)
            tt("dve", r3, r3, FQ[:, :, :, 65], ALU.subtract)
            copy("dve", FQ[:, :, :, 66], r3)
            for q_ in range(3):
                ts("dve", FK[:, :, :, 67 + q_], FQ[:, :, :, 64 + q_], -1.0, None, ALU.mult)

    kmT3 = kmT.rearrange("p (h n) -> p h n", h=8)

    def kmean_block():
        pbk = pbank(6)
        for h in range(H):
            for n_ in range(7):
                for s_ in range(2):
                    mm(pbk[0:64, h * 8 + n_: h * 8 + n_ + 1], MK[:, 2 * n_ + s_, h, 0:64], invc[:, 0:1], s_ == 0, s_ == 1)
        memset("dve", kmT[0:64, :], 0.0)
        for h in range(H):
            copy("dve", kmT3[0:64, h, 0:7], pbk[0:64, h * 8: h * 8 + 7])

    def gate_tile_a(i):
        sl = i % 2
        pbt_ = pbank_bf(5)
        for h in range(H):
            tr(pbt_[0:64, h * 128:(h + 1) * 128], MQ[:, i, h, 0:64], ident_b)
        copy("dve", qt0[sl][0:64], pbt_[0:64, 0:1024].rearrange("p (h q) -> p h q", h=8))

    def gate_tile(i):
        b = i // 2
        sl = i % 2
        pbg = pbank(7)
        for h in range(H):
            mm(pbg[:, h * 8:(h + 1) * 8], qt0[sl][0:64, h, :], kmT3[0:64, h, :], True, True)
        copy("dve", gate_s[sl], pbg[:, 0:64])
        g3 = gate_s[sl].rearrange("p (h n) -> p h n", h=8)
        cmp4 = cmp_s[:, 0:8 * b * b].rearrange("p (h n m) -> p h n m", h=8, n=b)
        in0 = g3[:, :, 0:b].unsqueeze(2).to_broadcast([128, 8, b, b])
        in1 = g3[:, :, 0:b].unsqueeze(3).to_broadcast([128, 8, b, b])
        tt("dve", cmp4, in0, in1, ALU.is_gt)
        rk = rank_s[:, 0:8 * b].rearrange("p (h n) -> p h n", h=8)
        P.op("dve", (lambda o, a: (lambda e: e.tensor_reduce(out=o, in_=a, axis=AX.X, op=ALU.add)))(rk, cmp4),
             reads=[cmp4], writes=[rk])
        ts("dve", MQ[:, i, :, 64:64 + b], rk, 3.0, NEG, ALU.is_ge, ALU.mult)

    n_km = pos_of[("mk", 15)] + 6
    n_cum = pos_of[("fl", 15)] + 5
    extra_at = {n_km: [kmean_block], n_cum: [lambda: cum_part(0)], n_cum + 4: [lambda: cum_part(1)],
                n_cum + 8: [lambda: cum_part(2)]}
    for i_ in range(8, NT):
        extra_at.setdefault(n_km + 2 + 3 * (i_ - 8), []).append((lambda ii: (lambda: gate_tile_a(ii)))(i_))
        extra_at.setdefault(n_km + 4 + 3 * (i_ - 8), []).append((lambda ii: (lambda: gate_tile(ii)))(i_))
    n_late = pos_of[("mk", 15)] + 4

    for n in range(NU + 6):
        if n < NU:
            pe_unit(n)
        lk = n - n_late
        if lk >= 0 and lk % 6 == 0:
            if 1 <= lk // 6 <= len(late_pieces):
                sg, hf = late_pieces[lk // 6 - 1]
                ada_compute(sg, hf, wlate, 4, row_form=True)
            if lk // 6 < len(late_pieces):
                sg, hf = late_pieces[lk // 6]
                ada_dma(sg, hf, wlate)
        if 0 <= n - 1 < NU:
            stage_a(n - 1)
        if 0 <= n - 4 < NU:
            stage_b(n - 4)
        if 0 <= n - 2 < NU:
            stage_a2(n - 2)
        for f_ in extra_at.get(n, []):
            f_()
    make_ab(ab2, S_GFFN, 4, 3)
    dump("modfm", modfm, [128, 48])
    dump("gtb", gtb, [128, 2048])
    dump("cum", cum, [128, 128])

    dump("MQ", MQ, [128, NT, H, SW])
    dump("MK", MK, [128, NT, H, SW])
    dump("FQ", FQ, [128, NT, H, FW])
    dump("FK", FK, [128, NT, H, FW])
    dump("MV", MV, [128, NT, H, 64])
    dump("FV", FV, [128, NT, H, 64])
    if stop_after == "T2":
        P.emit()
        return nc, dbg

    memset("pool", vaug[0][:, :, 64:128], 0.0)
    memset("pool", vaug[0][:, :, 64:65], 1.0)
    memset("pool", vaug[1][:, :, 0:64], 0.0)
    memset("pool", vaug[1][:, :, 0:1], 1.0)
    negcum3 = negcum.rearrange("p (i h) -> p i h", i=NT)
    s_banks = (0, 1, 2)
    o_banks = (3, 4)
    tr_banks = (5,)
    LOOK = 2
    PRE = 34
    DEFER = 8

    def head_cfg(hg):
        typ = 0 if hg < 8 else 1
        h = hg % 8
        QS, KS, VS, W = (MQ, MK, MV, SW) if typ == 0 else (FQ, FK, FV, FW)
        par = hg % 2
        return dict(typ=typ, h=h, QS=QS, KS=KS, VS=VS, W=W, par=par, pair=hg // 2,
                    qtb=qt[hg % 2], ktb=kt[hg % 2], va=vaug[par],
                    Mv=65 if par == 0 else 128,
                    rows=slice(0, 64) if par == 0 else slice(64, 128),
                    srp=64 if par == 0 else 0)

    trc = [0]

    trbank_of = {}

    def prologue_part(hg, part):
        cf = head_cfg(hg)
        h = cf["h"]
        if part == 0:
            if cf["par"] == 0:
                copy("pool", vaug[0][:, :, 0:64], cf["VS"][:, :, h, :])
            else:
                copy("pool", vaug[1][:, :, 64:128], cf["VS"][:, :, h, :])
            return
        W = cf["W"]
        grp = (part - 1) // 2
        sub = (part - 1) % 2
        S_, dstT = ((cf["QS"], cf["qtb"]), (cf["KS"], cf["ktb"]))[grp // 2]
        half = grp % 2
        if sub == 0:
            tb_ = (5, 6, 7) if hg == 0 else tr_banks
            trbank_of[(hg, grp)] = pbank_bf(tb_[trc[0] % len(tb_)])
            trc[0] += 1
        pbt_ = trbank_of[(hg, grp)]
        for s_ in range(4 * sub, 4 * sub + 4):
            i = half * 8 + s_
            tr(pbt_[0:W, s_ * 128:(s_ + 1) * 128], S_[:, i, h, 0:W], ident_b)
        if sub == 1:
            copy("dve", dstT[0:W, half * 1024:(half + 1) * 1024], pbt_[0:W, 0:1024])

    steps = []
    for hg in range(16):
        for c in range(4):
            for j in range(4 * c + 4):
                steps.append((hg, c, j))
    NS_ = len(steps)
    first_step = {}
    for si, (hg, c, j) in enumerate(steps):
        if hg not in first_step:
            first_step[hg] = si
    pro_at = {}
    for hg in range(16):
        for part in range(9):
            at = max(0, first_step[hg] - PRE + 3 * part) if hg > 0 else 0
            pro_at.setdefault(at, []).append((hg, part))
    st_info = [None] * NS_
    obank_of = {}
    ocnt = 0
    deferred = []
    ncnt = [0]

    def do_S(si):
        hg, c, j = steps[si]
        cf = head_cfg(hg)
        W = cf["W"]
        q0 = max(512 * c, 128 * j)
        N = 512 * (c + 1) - q0
        sbk = pbank(s_banks[si % 3])
        diag = j >= 4 * c
        mm(sbk[:, 0:N], cf["ktb"][0:W, j * 128:(j + 1) * 128], cf["qtb"][0:W, q0:q0 + N], True, not diag)
        if diag:
            mm(sbk[:, 0:128], ident_b, tribias_b, False, True)
        st_info[si] = (q0, N, sbk)

    def do_rest(si):
        nonlocal ocnt
        hg, c, j = steps[si]
        cf = head_cfg(hg)
        q0, N, sbk = st_info[si]
        if j == 0:
            obank_of[(hg, c)] = pbank(o_banks[ocnt % 2])
            ocnt += 1
        ob = obank_of[(hg, c)]
        ptb = pt[si % 3]
        act(ptb[:, 0:N], sbk[:, 0:N], AF.Exp, bias=0.0, scale=1.0)
        last = (j == 4 * c + 3)
        mm(ob[0:cf["Mv"], q0 - 512 * c:512], cf["va"][:, j, 0:cf["Mv"]], ptb[:, 0:N], j == 0, last)
        if last:
            nb = ncnt[0] % NNB
            ncnt[0] += 1
            srp, rows, pair = cf["srp"], cf["rows"], cf["pair"]
            copy("dve", srow[nb][srp:srp + 1, :], ob[srp:srp + 1, :])
            copy("dve", osb[nb][rows, :], ob[rows, :])

            def mk_pcol(j4, nb=nb, srp=srp):
                def f():
                    pcol = pbank(6)[:, 0:4]
                    tr(pcol[:, j4:j4 + 1], srow[nb][srp:srp + 1, j4 * 128:(j4 + 1) * 128], ones_f[srp:srp + 1, 0:1])
                    if j4 == 3:
                        P.op("dve", (lambda o, a: (lambda e: e.reciprocal(out=o, in_=a)))(rcol[nb], pcol),
                             reads=[pcol], writes=[rcol[nb]])
                        copy("dve", rhi[nb], rcol[nb])
                        tt("dve", rlo[nb], rcol[nb], rhi[nb], ALU.subtract)
                        idb = ident_b.unsqueeze(1).to_broadcast([128, 4, 128])
                        tt("dve", dhi[nb].rearrange("p (j q) -> p j q", j=4), idb,
                           rhi[nb].unsqueeze(2).to_broadcast([128, 4, 128]), ALU.mult)
                        tt("dve", dlo[nb].rearrange("p (j q) -> p j q", j=4), idb,
                           rlo[nb].unsqueeze(2).to_broadcast([128, 4, 128]), ALU.mult)
                return f

            def fin_hi(nb=nb):
                mm(pbank(7), ones_b, dhi[nb], True, False)

            def fin_lo(nb=nb, rows=rows, pair=pair, c=c):
                bcb = pbank(7)
                mm(bcb, ones_b, dlo[nb], False, True)
                tt("dve", oT[rows, pair, c * 512:(c + 1) * 512], osb[nb][rows, :], bcb[rows, :], ALU.mult)
            base = si + LOOK
            for j4 in range(4):
                deferred.append((base + 6 + j4, mk_pcol(j4)))
            deferred.append((base + 18, fin_hi))
            deferred.append((base + 19, fin_lo))

    for n in range(NS_ + LOOK + 26):
        for (hg_, part_) in pro_at.get(n, []):
            prologue_part(hg_, part_)
        wo_i = n - first_step[9] - 4
        if wo_i >= 0 and wo_i % 5 == 0 and wo_i // 5 < 8:
            p_ = wo_i // 5
            wout_v = wout_d.rearrange("p (k c) -> p k c", k=8)
            st_ = stage[p_ % 2]
            dma("sp", st_, wout_v[:, p_, :])
            tt("pool", woutb[:, p_, :], st_, gtb[:, 0:1024], ALU.mult)
        if n < NS_:
            do_S(n)
        deferred.sort(key=lambda t_: t_[0])
        while deferred and deferred[0][0] <= n:
            deferred.pop(0)[1]()
        if 0 <= n - LOOK < NS_:
            do_rest(n - LOOK)
    while deferred:
        deferred.pop(0)[1]()
    dump("oT", oT, [128, 8, T])
    if stop_after == "T3":
        P.emit()
        return nc, dbg

    for i in range(NT):
        dma("sp", x1[:, i, :], x_t[i])
    memset("dve", ss_a, 0.0)
    wup_v = wup_d.rearrange("p (m s k c) -> p m s k c", m=NM, s=2, k=8)
    dma("pool", wupb[0], wup_v[:, 0])
    dma("pool", wupb[1], wup_v[:, 1])

    def outproj_group(g):
        for i in range(4 * g, 4 * g + 4):
            yb = psum[:, (4 + 2 * (i % 2)) * 512:(4 + 2 * (i % 2)) * 512 + 1024]
            for cb in range(2):
                for p_ in range(8):
                    mm(yb[:, cb * 512:(cb + 1) * 512], oT[:, p_, i * 128:(i + 1) * 128], woutb[:, p_, cb * 512:(cb + 1) * 512],
                       p_ == 0, p_ == 7)
            tt("dve", x1[:, i, :], yb, x1[:, i, :], ALU.add)
            norm_stats(i, x1[:, i, :], xn2[g % 2][:, i % 4, :])

    for g in range(4):
        for i in range(4 * g, 4 * g + 4):
            yb = psum[:, (4 + 2 * (i % 2)) * 512:(4 + 2 * (i % 2)) * 512 + 1024]
            for cb in range(2):
                for p_ in range(8):
                    mm(yb[:, cb * 512:(cb + 1) * 512], oT[:, p_, i * 128:(i + 1) * 128], woutb[:, p_, cb * 512:(cb + 1) * 512],
                       p_ == 0, p_ == 7)
            tt("dve", x1[:, i, :], yb, x1[:, i, :], ALU.add)
            norm_rstd(i, x1[:, i, :])

    def stats_group(g):
        for i in range(4 * g, 4 * g + 4):
            ts("dve", xn2[g % 2][:, i % 4, :], x1[:, i, :], rstd_a[:, i:i + 1], None, ALU.mult)
    stats_group(0)
    for g in range(4):
        if g + 1 < 4:
            stats_group(g + 1)
        norm_tr(g, xn2[g % 2], ab2, hn2T)
    dump("x1", x1, [128, NT, D])
    dump("hn2T", hn2T, [128, 8, T])
    if stop_after == "T4":
        P.emit()
        return nc, dbg

    wup_v = wup_d.rearrange("p (m s k c) -> p m s k c", m=NM, s=2, k=8)
    wdn_v = wdn_d.rearrange("p (m c) -> p m c", m=NM)
    cw = smalls[:, S_CONVW:S_CONVW + 132].rearrange("p (i m) -> p i m", i=3)
    cbv = smalls[:, S_CONVB:S_CONVB + 44]
    out_t = out_d.rearrange("(i p) d -> i p d", p=128)
    m0 = 0
    ucnt = 0
    wcnt = 0
    scnt5 = 0
    ycnt = 0
    for g, gs in enumerate(GSZ):
        for ml in range(gs):
            st_ = stage5[scnt5 % 2]
            scnt5 += 1
            dma("sp", st_, wdn_v[:, m0 + ml, :])
            tt("dve", wdnb[:, ml, :], st_, gtb[:, 1024:2048], ALU.mult)
        for ml in range(gs):
            m = m0 + ml
            wb = wupb[m % 2]
            if 1 <= m and m + 1 < NM:
                dma("pool", wupb[(m + 1) % 2], wup_v[:, m + 1])
            for half in range(2):
                t0 = half * HW_
                bufi = ucnt % 2
                for part, (ub, cbuf) in enumerate(((ua[bufi], ca[bufi]), (uv[bufi], cv[bufi]))):
                    ch = part * NM + m
                    pb2 = psum[:, (2 * (ucnt % 2)) * 512 + 0: (2 * (ucnt % 2)) * 512 + 1024] if part == 0 else \
                        psum[:, (2 * (ucnt % 2)) * 512 + 0: (2 * (ucnt % 2)) * 512 + 1024]
                    dbk = (2 * ucnt + part) % 4
                    pb2 = psum[:, dbk * 1024: dbk * 1024 + 1024]
                    for cb in range(2):
                        for k in range(8):
                            mm(pb2[:, cb * 512:(cb + 1) * 512], wb[:, part, k, :],
                               hn2T[:, k, t0 + cb * 512: t0 + (cb + 1) * 512], k == 0, k == 7)
                    if half == 0:
                        act(ub[:, 0:2], zc4[:, 0:2], AF.Copy)
                    else:
                        prev = (ua if part == 0 else uv)[(ucnt - 1) % 2]
                        act(ub[:, 0:2], prev[:, HW_:HW_ + 2], AF.Copy)
                    act(ub[:, 2:2 + HW_], pb2, AF.Copy)
                    act(cbuf, ub[:, 2:2 + HW_], AF.Identity, bias=cbv[:, ch:ch + 1], scale=cw[:, 2, ch:ch + 1])
                    stt("dve", cbuf, ub[:, 1:1 + HW_], cw[:, 1, ch:ch + 1], cbuf, ALU.mult, ALU.add)
                    stt("dve", cbuf, ub[:, 0:HW_], cw[:, 0, ch:ch + 1], cbuf, ALU.mult, ALU.add)
                act(ca[bufi], ca[bufi], AF.Silu)
                tt("pool", hT[:, ml, t0:t0 + HW_], ca[bufi], cv[bufi], ALU.mult)
                ucnt += 1
        for i in range(NT):
            yb = psum[:, (4 + 2 * (ycnt % 2)) * 512:(4 + 2 * (ycnt % 2)) * 512 + 1024]
            ycnt += 1
            for cb in range(2):
                for ml in range(gs):
                    mm(yb[:, cb * 512:(cb + 1) * 512], hT[:, ml, i * 128:(i + 1) * 128], wdnb[:, ml, cb * 512:(cb + 1) * 512],
                       ml == 0, ml == gs - 1)
            tt("dve", x1[:, i, :], yb, x1[:, i, :], ALU.add)
            if g == len(GSZ) - 1:
                dma("sp", out_t[i], x1[:, i, :])
        m0 += gs
    P.emit()
    return nc, dbg


def _rope_tables():
    half = 8
    inv_freq = np.power(np.float32(ROPE_THETA), (-2.0 * np.arange(half, dtype=np.float32) / 16).astype(np.float32)).astype(np.float32)
    pos = np.arange(T, dtype=np.float32)
    ang = pos[:, None] * inv_freq[None, :]
    cos = np.cos(ang).astype(np.float32)
    sin = np.sin(ang).astype(np.float32)
    cs = np.concatenate([cos, cos], axis=1)
    ns = np.concatenate([-sin, sin], axis=1)
    cs = cs.reshape(NT, 128, 16).transpose(1, 0, 2).reshape(128, NT * 16)
    ns = ns.reshape(NT, 128, 16).transpose(1, 0, 2).reshape(128, NT * 16)
    return cs, ns


def _prep_shared(inp):
    f = lambda a: np.ascontiguousarray(np.asarray(a, dtype=np.float32))
    pk = lambda w, k: f(w.reshape(k, 128, -1).transpose(1, 0, 2).reshape(128, -1))
    sh = {}
    sh["w_ada"] = pk(f(inp["w_ada"])[0], 8)
    sh["w_in"] = pk(f(inp["w_in"])[0], 8)
    sh["w_out"] = pk(f(inp["w_out"])[0], 8)
    wup = f(inp["w_up"])[0]
    wup = wup.reshape(8, 128, 2, NM, 128)
    sh["w_up"] = f(wup.transpose(1, 3, 2, 0, 4).reshape(128, -1))
    sh["w_down"] = pk(f(inp["w_down"])[0], NM)
    sm = np.zeros((128, S_TOT), np.float32)
    fm = lambda v, k: f(v).reshape(k, 128).T
    sm[:, S_GMIX:S_GMIX + 8] = fm(inp["g_mix"][0], 8)
    sm[:, S_GFFN:S_GFFN + 8] = fm(inp["g_ffn"][0], 8)
    sm[:, S_BADA:S_BADA + 48] = fm(inp["b_ada"][0], 48)
    cw = f(inp["conv_w"])[0]
    for i in range(3):
        sm[:, S_CONVW + i * 44: S_CONVW + (i + 1) * 44] = fm(cw[i], 44)
    sm[:, S_CONVB:S_CONVB + 44] = fm(inp["conv_b"][0], 44)
    sm[:, S_MQG:S_MQG + 64] = f(inp["moba_q_gain"])[0][None, :]
    sm[:, S_MKG:S_MKG + 64] = f(inp["moba_k_gain"])[0][None, :]
    sm[:, S_FQG:S_FQG + 64] = f(inp["fox_q_gain"])[0][None, :]
    sm[:, S_FKG:S_FKG + 64] = f(inp["fox_k_gain"])[0][None, :]
    sm[:, S_BF:S_BF + 8] = f(inp["b_forget"])[0][None, :]
    if not ROPE_ON_DEVICE:
        cs, ns = _rope_tables()
        sm[:, S_CS:S_CS + 256] = cs
        sm[:, S_NS:S_NS + 256] = ns
    sh["smalls"] = sm
    ba = f(inp["b_ada"])[0]
    bc = np.zeros((128, 2048), np.float32)
    bc[:, 0:1024] = ba[None, 2048:3072]
    bc[:, 1024:2048] = ba[None, 5120:6144]
    sh["bcastb"] = bc
    return sh


def _core_inputs(sh, x_b, c_b):
    m = dict(sh)
    sm = sh["smalls"].copy()
    sm[:, S_C:S_C + 8] = np.asarray(c_b, np.float32).reshape(8, 128).T
    m["smalls"] = sm
    m["x"] = np.ascontiguousarray(np.asarray(x_b, np.float32))
    return m


_CACHE = {}


def kernel(**inputs):
    x = np.asarray(inputs["x"], np.float32)
    c = np.asarray(inputs["c"], np.float32)
    sh = _prep_shared(inputs)
    if "nc" not in _CACHE:
        _CACHE["nc"] = build_program(debug=False)[0]
    nc = _CACHE["nc"]
    in_maps = [_core_inputs(sh, x[b], c[b]) for b in range(8)]
    res = run_bass_kernel_spmd(nc, in_maps, core_ids=list(range(8)))
    out = np.stack([np.asarray(res.results[b]["out"], np.float32).reshape(T, D) for b in range(8)], axis=0)
    return out
```

```python
import math
import numpy as np
import concourse.bass as bass
import concourse.mybir as mybir
from concourse.bass_utils import run_bass_kernel_spmd

F32 = mybir.dt.float32
BF16 = mybir.dt.bfloat16
AF = mybir.ActivationFunctionType
ALU = mybir.AluOpType
AX = mybir.AxisListType

T = 2048
D = 1024
NT = 16
H = 8
DH = 64
DFF = 2816
NM = 22
INC = 3080
EPS = 1e-6
NEG = -30000.0
ROPE_THETA = 500000.0

SEM_EPOCH = 20000
ROPE_ON_DEVICE = True
GRAN = 128

S_C, S_GMIX, S_GFFN, S_BADA, S_CONVW, S_CONVB = 0, 8, 16, 24, 72, 204
S_MQG, S_MKG, S_FQG, S_FKG, S_BF, S_CS, S_NS = 256, 320, 384, 448, 512, 576, 832
S_TOT = 1152


class Op:
    __slots__ = ("eng", "fn", "deps", "signal", "sig_idx", "is_dma", "dma_sem", "dma_val", "seq")

    def __init__(self, eng, fn, is_dma):
        self.eng = eng
        self.fn = fn
        self.deps = []
        self.signal = is_dma
        self.sig_idx = -1
        self.is_dma = is_dma
        self.dma_sem = None
        self.dma_val = 0
        self.seq = -1


def _ap_range(ap):
    sp = str(ap.space)
    esz = mybir.dt.size(ap.dtype) if hasattr(mybir.dt, "size") else None
    if esz is None:
        esz = 2 if ap.dtype == BF16 else 4
    dims = list(ap.ap)
    row = dims[0][0]
    off = ap.offset % row if row > 0 else ap.offset
    span = 1
    for st, cnt in dims[1:]:
        span += (cnt - 1) * abs(st)
    return sp, off * esz, (off + span) * esz


class Prog:
    ENGS = ("pe", "act", "dve", "pool", "sp")

    def __init__(self, nc, n_dma_sems=12):
        self.nc = nc
        self.ops = {e: [] for e in self.ENGS}
        self.n_dma_sems = n_dma_sems
        self.sb_w = {}
        self.sb_r = {}
        self.ps = {}
        self.nseq = 0

    @staticmethod
    def _esz(dt):
        return 2 if dt == BF16 else 4

    def _range(self, ap):
        sps = str(ap.space)
        sp = "psum" if sps == "PSUM" else ("sbuf" if sps == "SB" else "dram")
        esz = self._esz(ap.dtype)
        dims = list(ap.ap)
        row = dims[0][0]
        off = ap.offset % row if row > 0 else ap.offset
        span = 1
        for st, cnt in dims[1:]:
            span += (cnt - 1) * abs(st)
        return sp, off * esz, (off + span) * esz

    def _add(self, o, reads, writes):
        deps = {}

        def add_dep(d):
            if d is not None and d is not o:
                deps[id(d)] = d

        acc = []
        for ap in reads:
            acc.append((ap, False))
        for ap in writes:
            acc.append((ap, True))
        for ap, is_w in acc:
            sp, lo, hi = self._range(ap)
            if sp == "dram":
                continue
            if sp == "psum":
                for b in range(lo // 2048, (hi - 1) // 2048 + 1):
                    st = self.ps.setdefault(b, {})
                    for e2, (op2, w2) in st.items():
                        if e2 != o.eng:
                            add_dep(op2)
                        elif o.eng != "pe" and (w2 or is_w):
                            add_dep(op2)
            else:
                for g in range(lo // GRAN, (hi - 1) // GRAN + 1):
                    add_dep(self.sb_w.get(g))
                    if is_w:
                        rs = self.sb_r.get(g)
                        if rs:
                            for r in rs.values():
                                add_dep(r)
        for ap, is_w in acc:
            sp, lo, hi = self._range(ap)
            if sp == "dram":
                continue
            if sp == "psum":
                for b in range(lo // 2048, (hi - 1) // 2048 + 1):
                    st = self.ps.setdefault(b, {})
                    prev = st.get(o.eng)
                    if prev is not None and prev[0] is o:
                        st[o.eng] = (o, prev[1] or is_w)
                    else:
                        st[o.eng] = (o, is_w)
            else:
                key = ("dma", id(o)) if o.is_dma else o.eng
                for g in range(lo // GRAN, (hi - 1) // GRAN + 1):
                    if is_w:
                        self.sb_w[g] = o
                        self.sb_r[g] = {}
                    else:
                        self.sb_r.setdefault(g, {})[key] = o
        best = {}
        dl = []
        for d in deps.values():
            if d.is_dma:
                dl.append(d)
                continue
            if d.eng == "pe" and o.eng == "pe" and not o.is_dma:
                continue
            b = best.get(d.eng)
            if b is None or d.seq > b.seq:
                best[d.eng] = d
        dl.extend(best.values())
        o.deps = dl
        for d in dl:
            d.signal = True
        o.seq = self.nseq
        self.nseq += 1
        self.ops[o.eng].append(o)
        return o

    def op(self, eng, fn, reads=(), writes=()):
        return self._add(Op(eng, fn, False), reads, writes)

    def dma(self, eng, fn, reads=(), writes=()):
        return self._add(Op(eng, fn, True), reads, writes)

    def emit(self):
        nc = self.nc
        nsig = {}
        for e in self.ENGS:
            k = 0
            for o in self.ops[e]:
                if (not o.is_dma) and o.signal:
                    k += 1
                    o.sig_idx = k
            nsig[e] = k
        esems = {}
        for e in self.ENGS:
            n_ep = max(1, (nsig[e] + SEM_EPOCH - 1) // SEM_EPOCH)
            esems[e] = [nc.alloc_semaphore(f"s_{e}_{i}") for i in range(n_ep)]
        dsems, dcount = {}, {}
        for e in self.ENGS:
            dl = [o for o in self.ops[e] if o.is_dma]
            if dl:
                n = min(self.n_dma_sems, len(dl))
                dsems[e] = [nc.alloc_semaphore(f"d_{e}_{i}") for i in range(n)]
                dcount[e] = [0] * n
                for k, o in enumerate(dl):
                    j = k % n
                    dcount[e][j] += 16
                    o.dma_sem = (e, j)
                    o.dma_val = dcount[e][j]

        def sem_of(d):
            if d.is_dma:
                e, j = d.dma_sem
                return ("d", e, j), dsems[e][j], d.dma_val
            ep = (d.sig_idx - 1) // SEM_EPOCH
            return ("c", d.eng, ep), esems[d.eng][ep], d.sig_idx - ep * SEM_EPOCH

        prog = self

        def run_engine(ename, eng):
            seen = {}
            for o in prog.ops[ename]:
                waits = {}
                for d in o.deps:
                    key, sem, val = sem_of(d)
                    if seen.get(key, 0) >= val:
                        continue
                    if key not in waits or waits[key][1] < val:
                        waits[key] = (sem, val)
                if o.is_dma:
                    e, j = o.dma_sem
                    key = ("d", e, j)
                    pv = o.dma_val - 16
                    if pv > 0 and seen.get(key, 0) < pv:
                        if key not in waits or waits[key][1] < pv:
                            waits[key] = (dsems[e][j], pv)
                for key, (sem, val) in waits.items():
                    eng.wait_ge(sem, val)
                    seen[key] = val
                ins = o.fn(eng)
                if o.is_dma:
                    e, j = o.dma_sem
                    ins.then_inc(dsems[e][j], 16)
                elif o.signal:
                    ep = (o.sig_idx - 1) // SEM_EPOCH
                    ins.then_inc(esems[ename][ep], 1)
            if ename in dsems:
                for j, s in enumerate(dsems[ename]):
                    if dcount[ename][j] > 0:
                        eng.wait_ge(s, dcount[ename][j])

        with nc.Block() as block:
            @block.tensor
            def _(e):
                run_engine("pe", e)

            @block.scalar
            def _(e):
                run_engine("act", e)

            @block.vector
            def _(e):
                run_engine("dve", e)

            @block.gpsimd
            def _(e):
                run_engine("pool", e)

            @block.sync
            def _(e):
                run_engine("sp", e)


def build_program(debug=False, stop_after=None):
    nc = bass.Bass("TRN2", target_bir_lowering=False)
    P = Prog(nc)

    def din(name, shape):
        return nc.dram_tensor(name, shape, F32, kind="ExternalInput").ap()

    x_d = din("x", [T, D])
    wada_d = din("w_ada", [128, 8 * 6144])
    win_d = din("w_in", [128, 8 * INC])
    wout_d = din("w_out", [128, 8 * D])
    wup_d = din("w_up", [128, NM * 2 * 8 * 128])
    wdn_d = din("w_down", [128, NM * D])
    smalls_d = din("smalls", [128, S_TOT])
    bcast_d = din("bcastb", [128, 2048])
    out_d = nc.dram_tensor("out", [T, D], F32, kind="ExternalOutput").ap()
    dbg = {}

    ARENA_W = 53180
    arena = nc.alloc_sbuf_tensor("arena", [128, ARENA_W], F32)
    psum = nc.alloc_psum_tensor("psum", [128, 4096], F32)

    def sb(off_b, n, dt):
        assert off_b % 4 == 0
        if dt == F32:
            assert off_b // 4 + n <= ARENA_W, (off_b, n)
            return arena[:, off_b // 4: off_b // 4 + n]
        assert off_b + 2 * n <= ARENA_W * 4, (off_b, n)
        nw = (n + 1) // 2
        return arena[:, off_b // 4: off_b // 4 + nw].bitcast(BF16)[:, 0:n]

    def pbank(b, n=512, nb=1):
        return psum[:, b * 512: b * 512 + n]

    def pbank_bf(b):
        return psum[:, b * 512:(b + 1) * 512].bitcast(BF16)

    class Alloc:
        def __init__(self, base, limit):
            self.p = base
            self.limit = limit

        def take(self, nbytes, align=128):
            self.p = (self.p + align - 1) // align * align
            o = self.p
            self.p += nbytes
            assert self.p <= self.limit, ("arena overflow", self.p, self.limit)
            return o

    LIMIT = ARENA_W * 4
    CA = Alloc(0, 27 * 1024)
    smalls = sb(CA.take(S_TOT * 4), S_TOT, F32)
    gtb = sb(CA.take(2048 * 4), 2048, F32)
    ident_f = sb(CA.take(512), 128, F32)
    tri_f = sb(CA.take(512), 128, F32)
    ones_f = sb(CA.take(512), 128, F32)
    ident_b = sb(CA.take(256), 128, BF16)
    ones_b = sb(CA.take(256), 128, BF16)
    tribias_b = sb(CA.take(256), 128, BF16)
    modfm = sb(CA.take(192), 48, F32)
    ab1 = sb(CA.take(64), 16, F32)
    ab2 = sb(CA.take(64), 16, F32)
    scf = sb(CA.take(32), 8, F32)
    scb16 = sb(CA.take(16), 8, BF16)
    scbc = sb(CA.take(2048), 1024, BF16)
    ss_a = sb(CA.take(64), 16, F32)
    rstd_a = sb(CA.take(64), 16, F32)
    epsc = sb(CA.take(16), 4, F32)
    invc = sb(CA.take(8), 4, BF16)
    zc4 = sb(CA.take(16), 4, F32)
    wflb = sb(CA.take(8 * 8 * 2), 64, BF16).rearrange("p (k c) -> p k c", k=8)
    ss8 = [sb(CA.take(32), 8, F32) for _ in range(2)]
    sd8 = [sb(CA.take(32), 8, F32) for _ in range(2)]
    rs8 = [sb(CA.take(32), 8, F32) for _ in range(2)]
    g_mq8 = sb(CA.take(256), 64, F32)
    g_fk = sb(CA.take(256), 64, F32)
    zf = sb(CA.take(512), 128, F32)
    cum = sb(CA.take(512), 128, F32)
    negcum = sb(CA.take(512), 128, F32)
    totb = sb(CA.take(512), 128, F32)
    pref = sb(CA.take(512), 128, F32)
    res1 = sb(CA.take(512), 128, F32)
    kmT = sb(CA.take(128), 64, BF16)
    gate_s = [sb(CA.take(256), 64, F32) for _ in range(2)]
    cmp_s = sb(CA.take(1568), 392, F32)
    rank_s = sb(CA.take(224), 56, F32)
    junk = sb(CA.take(2048), 1024, BF16)
    CEND = CA.p

    A = Alloc(CEND, LIMIT)
    hnT_o = A.take(8 * T * 2)
    hnT = sb(hnT_o, 8 * T, BF16).rearrange("p (k t) -> p k t", k=8)
    slab_o = A.p
    SW = 72
    FW = 70
    MQ_O = A.take(NT * H * SW * 2)
    MQ = sb(MQ_O, NT * H * SW, BF16).rearrange("p (i h w) -> p i h w", i=NT, h=H)
    MK = sb(A.take(NT * H * SW * 2), NT * H * SW, BF16).rearrange("p (i h w) -> p i h w", i=NT, h=H)
    FQ = sb(A.take(NT * H * FW * 2), NT * H * FW, BF16).rearrange("p (i h w) -> p i h w", i=NT, h=H)
    FK = sb(A.take(NT * H * FW * 2), NT * H * FW, BF16).rearrange("p (i h w) -> p i h w", i=NT, h=H)
    MV_O = A.take(NT * 512 * 2)
    MV = sb(MV_O, NT * 512, BF16).rearrange("p (i h d) -> p i h d", i=NT, h=H)
    FV = sb(A.take(NT * 512 * 2), NT * 512, BF16).rearrange("p (i h d) -> p i h d", i=NT, h=H)
    slab_end = A.p
    t2_o = A.p
    wch = [sb(A.take(8 * 512 * 2), 8 * 512, BF16).rearrange("p (k c) -> p k c", k=8) for _ in range(2)]
    sq_s = [sb(A.take(2048), 512, F32) for _ in range(2)]
    t_s = [sb(A.take(2048), 512, F32) for _ in range(3)]
    t2_s = [sb(A.take(512), 128, F32) for _ in range(2)]
    rA = [sb(A.take(512), 128, F32) for _ in range(2)]
    rB = [sb(A.take(512), 128, F32) for _ in range(2)]
    qt0 = [sb(A.take(2048), 1024, BF16).rearrange("p (h q) -> p h q", h=8) for _ in range(2)]
    wlate = sb(A.take(8 * 512 * 2), 8 * 512, BF16).rearrange("p (k c) -> p k c", k=8)
    t2_end = A.p
    rope_w = sb(t2_o + 16384, 704, F32)
    B1 = Alloc(slab_o, slab_end)
    wada_s = [sb(B1.take(8 * 512 * 2), 8 * 512, BF16).rearrange("p (k c) -> p k c", k=8) for _ in range(4)]
    xg = [sb(B1.take(4 * 1024 * 4), 4096, F32).rearrange("p (s d) -> p s d", s=4) for _ in range(4)]

    B3 = Alloc(hnT_o, slab_o)
    qt = [sb(B3.take(4096), 2048, BF16) for _ in range(2)]
    kt = [sb(B3.take(4096), 2048, BF16) for _ in range(2)]
    vaug = [sb(B3.take(4096), 2048, BF16).rearrange("p (i c) -> p i c", i=NT) for _ in range(2)]
    pt = [sb(B3.take(1024), 512, BF16) for _ in range(3)]
    dlo_b3 = sb(B3.take(1024), 512, BF16)
    assert B3.p <= slab_o
    OT_O = LIMIT - 8 * T * 2
    C3 = Alloc(t2_o, OT_O)
    NNB = 3
    osb = [sb(C3.take(2048), 512, F32) for _ in range(NNB)]
    dhi = [sb(C3.take(1024), 512, BF16) for _ in range(NNB)]
    dlo = [sb(C3.take(1024), 512, BF16) for _ in range(NNB - 1)] + [dlo_b3]
    rhi = [sb(C3.take(16, 16), 4, BF16) for _ in range(NNB)]
    rlo = [sb(C3.take(16, 16), 4, F32) for _ in range(NNB)]
    srow = [sb(B3.take(2048), 512, F32) for _ in range(NNB - 1)]
    srow.append(sb(C3.take(2048), 512, F32))
    rcol = [sb(C3.take(16, 16), 4, F32) for _ in range(NNB)]
    assert B3.p <= slab_o
    oT = sb(OT_O, 8 * T, BF16).rearrange("p (k t) -> p k t", k=8)

    B4 = Alloc(CEND, OT_O)
    x1 = sb(B4.take(NT * D * 4), NT * D, F32).rearrange("p (i d) -> p i d", i=NT)
    hn2T = sb(B4.take(8 * T * 2), 8 * T, BF16).rearrange("p (k t) -> p k t", k=8)
    T5_O = B4.p
    xn2_0 = sb(B4.take(4 * 1024 * 4), 4096, F32).rearrange("p (s d) -> p s d", s=4)
    xn2_1 = sb(B4.take(4 * 1024 * 4), 4096, F32).rearrange("p (s d) -> p s d", s=4)
    xn2 = [xn2_0, xn2_1]
    woutb = sb(MV_O, 8 * 1024, BF16).rearrange("p (k c) -> p k c", k=8)
    stage = [sb(MQ_O + 4096 * q_, 1024, F32) for q_ in range(2)]
    B5 = Alloc(T5_O, LIMIT)
    GSZ = [6, 6, 5, 5]
    hT = sb(B5.take(6 * T * 2), 6 * T, BF16).rearrange("p (m t) -> p m t", m=6)
    wdnb = sb(B5.take(6 * 1024 * 2), 6 * 1024, BF16).rearrange("p (m c) -> p m c", m=6)
    wupb = [sb(B5.take(2 * 8 * 128 * 2), 2048, BF16).rearrange("p (s k c) -> p s k c", s=2, k=8) for _ in range(2)]
    stage5 = [sb(B5.take(4096), 1024, F32) for _ in range(2)]
    HW_ = 1024
    ua = [sb(B5.take((HW_ + 2) * 4), HW_ + 2, F32) for _ in range(2)]
    uv = [sb(B5.take((HW_ + 2) * 4), HW_ + 2, F32) for _ in range(2)]
    ca = [sb(B5.take(HW_ * 4), HW_, F32) for _ in range(2)]
    cv = [sb(B5.take(HW_ * 4), HW_, F32) for _ in range(2)]

    def V(eng, name, *args, reads=(), writes=(), **kw):
        def fn(e):
            return getattr(e, name)(*args, **kw)
        return P.op(eng, fn, reads=reads, writes=writes)

    def act(out, in_, func, bias=None, scale=None, accum_out=None, extra_reads=()):
        kw = {}
        rd = [in_] + list(extra_reads)
        if bias is not None:
            kw["bias"] = bias
            if not isinstance(bias, float):
                rd.append(bias)
        if scale is not None:
            kw["scale"] = scale
            if not isinstance(scale, float):
                rd.append(scale)
        wr = [out]
        if accum_out is not None:
            kw["accum_out"] = accum_out
            wr.append(accum_out)
        return P.op("act", lambda e: e.activation(out=out, in_=in_, func=func, **kw), reads=rd, writes=wr)

    def tt(eng, out, in0, in1, op):
        return P.op(eng, lambda e: e.tensor_tensor(out=out, in0=in0, in1=in1, op=op), reads=[in0, in1], writes=[out])

    def ts(eng, out, in0, s1, s2, op0, op1=None):
        rd = [in0]
        if not isinstance(s1, (float, int)):
            rd.append(s1)
        if s2 is not None and not isinstance(s2, (float, int)):
            rd.append(s2)
        if op1 is None:
            return P.op(eng, lambda e: e.tensor_scalar(out=out, in0=in0, scalar1=s1, scalar2=None, op0=op0),
                        reads=rd, writes=[out])
        return P.op(eng, lambda e: e.tensor_scalar(out=out, in0=in0, scalar1=s1, scalar2=s2, op0=op0, op1=op1),
                    reads=rd, writes=[out])

    def stt(eng, out, in0, scalar, in1, op0, op1):
        rd = [in0, in1]
        if not isinstance(scalar, (float, int)):
            rd.append(scalar)
        return P.op(eng, lambda e: e.scalar_tensor_tensor(out=out, in0=in0, scalar=scalar, in1=in1, op0=op0, op1=op1),
                    reads=rd, writes=[out])

    def copy(eng, out, in_):
        return P.op(eng, lambda e: e.tensor_copy(out=out, in_=in_), reads=[in_], writes=[out])

    def memset(eng, ap, val):
        return P.op(eng, lambda e: e.memset(ap, val), writes=[ap])

    def mm(out, lhsT, rhs, start, stop):
        return P.op("pe", lambda e: e.matmul(out, lhsT=lhsT, rhs=rhs, start=start, stop=stop),
                    reads=[lhsT, rhs], writes=[out])

    def tr(out, in_, ident):
        return P.op("pe", lambda e: e.transpose(out=out, in_=in_, identity=ident), reads=[in_, ident], writes=[out])

    def dma(eng, out, in_):
        return P.dma(eng, lambda e: e.dma_start(out=out, in_=in_), reads=[in_], writes=[out])

    def dump(name, ap, shape):
        if not debug:
            return
        d = nc.dram_tensor("dbg_" + name, list(shape), ap.dtype, kind="ExternalOutput").ap()
        dbg[name] = d
        dma("sp", d, ap)

    def warmup(n, bank, rhs):
        pbw = pbank(bank)
        for _ in range(n):
            mm(pbw, ident_b, rhs, True, True)

    dma("sp", smalls, smalls_d)
    dma("sp", gtb, bcast_d)

    def asel(ap, cmp_op, fill, step, cm):
        return P.op("pool", lambda e: e.affine_select(out=ap, in_=ap, pattern=[[step, 128]], compare_op=cmp_op,
                                                      fill=fill, base=0, channel_multiplier=cm),
                    reads=[ap], writes=[ap])
    memset("pool", ident_f, 1.0)
    asel(ident_f, ALU.is_equal, 0.0, -1, 1)
    memset("pool", tri_f, 1.0)
    asel(tri_f, ALU.is_ge, 0.0, 1, -1)
    memset("pool", res1, 0.0)
    asel(res1, ALU.is_ge, NEG, 1, -1)
    copy("pool", tribias_b, res1)
    copy("pool", ident_b, ident_f)
    memset("pool", ones_f, 1.0)
    memset("pool", ones_b, 1.0)
    memset("pool", epsc[:, 0:1], EPS)
    memset("pool", epsc[:, 1:2], 0.0)
    memset("pool", epsc[:, 2:3], 1.0)
    memset("pool", invc, 1.0 / 256.0)
    memset("pool", zc4, 0.0)
    eps_ap = epsc[:, 0:1]
    zero_ap = epsc[:, 1:2]
    one_ap = epsc[:, 2:3]

    act(scf, smalls[:, S_C:S_C + 8], AF.Silu)
    copy("dve", scb16, scf)
    copy("dve", scbc.rearrange("p (k m) -> p k m", k=8), scf.unsqueeze(2).to_broadcast([128, 8, 128]))
    scbc3 = scbc.rearrange("p (k m) -> p k m", k=8)

    wada_v = wada_d.rearrange("p (k c) -> p k c", k=8)

    def ada_dma(seg, half, buf):
        dma("pool", buf, wada_v[:, :, seg * 1024 + half * 512: seg * 1024 + (half + 1) * 512])

    def ada_compute(seg, half, buf, bank, row_form=False):
        pb = pbank(bank)
        if row_form and seg not in (2, 5):
            for k in range(8):
                mm(pb, scbc3[:, k, :], buf[:, k, :], k == 0, k == 7)
            rowf = junk.bitcast(F32)
            copy("dve", rowf[0:1, :], pb[0:1, :])

            def stage2():
                pc_ = pbank(6)[:, 300:304]
                for jj in range(4):
                    tr(pc_[:, jj:jj + 1], rowf[0:1, jj * 128:(jj + 1) * 128], ones_f[0:1, 0:1])
                c0 = seg * 8 + half * 4
                tt("dve", modfm[:, c0:c0 + 4], pc_, smalls[:, S_BADA + c0: S_BADA + c0 + 4], ALU.add)
            return stage2
        if seg in (2, 5):
            slot = 0 if seg == 2 else 1
            for k in range(8):
                mm(pb, scbc3[:, k, :], buf[:, k, :], k == 0, k == 7)
            dst = gtb[:, slot * 1024 + half * 512: slot * 1024 + (half + 1) * 512]
            tt("dve", dst, pb, dst, ALU.add)
        else:
            for jj in range(4):
                for k in range(8):
                    mm(pb[:, jj:jj + 1], buf[:, k, jj * 128:(jj + 1) * 128], scb16[:, k:k + 1], k == 0, k == 7)
            c0 = seg * 8 + half * 4
            tt("dve", modfm[:, c0:c0 + 4], pb[:, 0:4], smalls[:, S_BADA + c0: S_BADA + c0 + 4], ALU.add)

    def make_ab(ab, gcol, scseg, shseg):
        ts("dve", ab[:, 0:8], modfm[:, scseg * 8:scseg * 8 + 8], 1.0, None, ALU.add)
        tt("dve", ab[:, 0:8], ab[:, 0:8], smalls[:, gcol:gcol + 8], ALU.mult)
        copy("dve", ab[:, 8:16], modfm[:, shseg * 8:shseg * 8 + 8])

    early = [(1, 0), (1, 1), (0, 0), (0, 1)]
    for pi, (seg, half) in enumerate(early):
        ada_dma(seg, half, wada_s[pi])
    late_pieces = [(4, 0), (4, 1), (3, 0), (3, 1), (2, 0), (2, 1), (5, 0), (5, 1)]
    ts("dve", g_mq8, smalls[:, S_MQG:S_MQG + 64], 0.125, None, ALU.mult)
    tt("dve", g_fk, smalls[:, S_FQG:S_FQG + 64], smalls[:, S_FKG:S_FKG + 64], ALU.mult)
    ts("dve", g_fk, g_fk, 0.125, None, ALU.mult)
    CSv = smalls[:, S_CS:S_CS + 256].rearrange("p (i c) -> p i c", i=NT)
    NSv = smalls[:, S_NS:S_NS + 256].rearrange("p (i c) -> p i c", i=NT)
    if ROPE_ON_DEVICE:
        TWO_PI = 6.283185307179586
        C1 = 6.28125
        C2 = TWO_PI - C1
        pos_i = rope_w[:, 0:16].bitcast(mybir.dt.int32)
        pos_f = rope_w[:, 16:32]
        invf = rope_w[:, 32:40]
        ang = rope_w[:, 64:192].rearrange("p (i j) -> p i j", i=NT)
        kf = rope_w[:, 192:320].rearrange("p (i j) -> p i j", i=NT)
        ki = rope_w[:, 320:448].bitcast(mybir.dt.int32).rearrange("p (i j) -> p i j", i=NT)
        rr = rope_w[:, 448:576].rearrange("p (i j) -> p i j", i=NT)
        cr = rope_w[:, 576:704].rearrange("p (i j) -> p i j", i=NT)
        P.op("pool", lambda e: e.iota(pos_i, pattern=[[128, 16]], base=0, channel_multiplier=1), writes=[pos_i])
        copy("dve", pos_f, pos_i)
        for j_ in range(8):
            memset("pool", invf[:, j_:j_ + 1], float(np.power(np.float32(ROPE_THETA), np.float32(-2.0 * j_ / 16))))
        tt("dve", ang, pos_f.unsqueeze(2).to_broadcast([128, NT, 8]), invf.unsqueeze(1).to_broadcast([128, NT, 8]), ALU.mult)

        def reduce_to_pi(dst, shift):
            ts("dve", kf, ang, 1.0 / TWO_PI, shift / TWO_PI, ALU.mult, ALU.add)
            copy("dve", ki, kf)
            copy("dve", kf, ki)
            stt("dve", dst, kf, -C1, ang, ALU.mult, ALU.add)
            stt("dve", dst, kf, -C2, dst, ALU.mult, ALU.add)
            if shift != 0.0:
                ts("dve", dst, dst, shift, None, ALU.add)
            ts("dve", kf, dst, math.pi, -TWO_PI, ALU.is_gt, ALU.mult)
            tt("dve", dst, dst, kf, ALU.add)
            ts("dve", kf, dst, -math.pi, TWO_PI, ALU.is_lt, ALU.mult)
            tt("dve", dst, dst, kf, ALU.add)
        reduce_to_pi(rr, 0.0)
        reduce_to_pi(cr, math.pi / 2)
        act(rr, rr, AF.Sin)
        act(cr, cr, AF.Sin)
        copy("dve", CSv[:, :, 0:8], cr)
        copy("dve", CSv[:, :, 8:16], cr)
        copy("dve", NSv[:, :, 8:16], rr)
        ts("dve", NSv[:, :, 0:8], rr, -1.0, None, ALU.mult)


    def norm_rstd(i, src):
        act(junk, src, AF.Square, accum_out=ss_a[:, i:i + 1])
        act(rstd_a[:, i:i + 1], ss_a[:, i:i + 1], AF.Sqrt, bias=float(EPS), scale=1.0 / D)
        P.op("dve", (lambda o, a: (lambda e: e.reciprocal(out=o, in_=a)))(rstd_a[:, i:i + 1], rstd_a[:, i:i + 1]),
             reads=[rstd_a[:, i:i + 1]], writes=[rstd_a[:, i:i + 1]])

    def norm_stats(i, src, dst):
        norm_rstd(i, src)
        ts("dve", dst, src, rstd_a[:, i:i + 1], None, ALU.mult)

    trb = [0]

    def norm_tr(g, xb, ab, dstT, tr_banks=(0, 1, 2, 3), split=False):
        for k in range(8):
            pb = pbank(tr_banks[trb[0] % len(tr_banks)])
            trb[0] += 1
            for s_ in range(4):
                tr(pb[:, s_ * 128:(s_ + 1) * 128], xb[:, s_, k * 128:(k + 1) * 128], ident_f)
            if split and k % 2 == 1:
                ts("dve", dstT[:, k, g * 512:(g + 1) * 512], pb, ab[:, k:k + 1], ab[:, 8 + k:9 + k], ALU.mult, ALU.add)
            else:
                act(dstT[:, k, g * 512:(g + 1) * 512], pb, AF.Identity, bias=ab[:, 8 + k:9 + k], scale=ab[:, k:k + 1])

    x_t = x_d.rearrange("(i p) d -> i p d", p=128)
    memset("dve", ss_a, 0.0)
    for i in range(NT):
        dst_ = xg[i // 4][:, i % 4, :]
        dma("sp", dst_, x_t[i])
        norm_stats(i, dst_, dst_)
    for pi, (seg, half) in enumerate(early):
        ada_compute(seg, half, wada_s[pi], 6 + pi % 2)
    make_ab(ab1, S_GMIX, 1, 0)
    for g in range(4):
        norm_tr(g, xg[g], ab1, hnT, split=True)
    dump("hnT", hnT, [128, 8, T])
    if stop_after == "T1":
        P.emit()
        return nc, dbg

    win_v = win_d.rearrange("p (k c) -> p k c", k=8)
    memset("pool", MQ[:, :, :, 64:72], 0.0)
    memset("pool", MK[:, :, :, 64:72], 0.0)
    for n in range(8):
        memset("pool", MK[:, 2 * n:2 * n + 2, :, 64 + n:65 + n], 1.0)
    memset("pool", FK[:, :, :, 64:67], 1.0)
    memset("pool", FQ[:, :, :, 67:70], 1.0)

    CSv = smalls[:, S_CS:S_CS + 256].rearrange("p (i c) -> p i c", i=NT)
    NSv = smalls[:, S_NS:S_NS + 256].rearrange("p (i c) -> p i c", i=NT)
    groups = ["mq", "mk", "fl", "mv", "fq", "fk", "fv"]
    gcol = {"mq": 0, "mk": 512, "mv": 1024, "fq": 1536, "fk": 2048, "fv": 2560, "fl": 3072}
    zf3 = zf.rearrange("p (i h) -> p i h", i=NT)
    seq = []
    for i_ in range(NT):
        seq += [("mq", i_), ("mv", i_)]
    for t_ in range(NT + 3):
        if t_ < NT:
            seq.append(("mk", t_))
        if t_ >= 3:
            seq.append(("fv", t_ - 3))
    for g_ in ("fl", "fq", "fk"):
        seq += [(g_, i_) for i_ in range(NT)]
    units = [(0, g_, i_) for (g_, i_) in seq]
    NU = len(units)
    pos_of = {(g_, i_): n_ for n_, (g_, i_) in enumerate(seq)}
    wfl = wflb[:, :, 0:8]
    gbuf = {"mq": wch[0], "mv": wch[1], "mk": wlate, "fv": wch[0], "fl": wfl, "fq": wch[1], "fk": wch[0]}

    def w_dma(g_):
        c0_ = gcol[g_]
        if g_ == "fl":
            dma("pool", wfl, win_v[:, :, c0_:c0_ + 8])
        else:
            dma("pool", gbuf[g_], win_v[:, :, c0_:c0_ + 512])
    for g_ in ("mq", "mv", "mk", "fl"):
        w_dma(g_)
    wdma_at = {pos_of[("mq", 15)] + 1: ["fv"], pos_of[("mv", 15)] + 2: ["fq"], pos_of[("fv", 15)] + 2: ["fk"]}
    warmup(16, 4, hnT[:, 0, 0:512])

    def pe_unit(n):
        gi, gname, i = units[n]
        for g_ in wdma_at.get(n, []):
            w_dma(g_)
        wb = gbuf[gname]
        pb = pbank(n % 4)
        ncol = 8 if gname == "fl" else 512
        for k in range(8):
            mm(pb[:, 0:ncol], hnT[:, k, i * 128:(i + 1) * 128], wb[:, k, 0:ncol], k == 0, k == 7)

    def stage_a(n):
        gi, gname, i = units[n]
        pb = pbank(n % 4)
        sl = n % 2
        pb3 = pb.rearrange("p (h d) -> p h d", h=8)
        if gname == "fl":
            tt("dve", zf3[:, i, :], pb[:, 0:8], smalls[:, S_BF:S_BF + 8], ALU.add)
            return
        if gname in ("mv", "fv"):
            dst = MV if gname == "mv" else FV
            act(dst[:, i], pb3, AF.Copy)
            return
        act(sq_s[sl], pb, AF.Square)
        P.op("dve", (lambda o, a: (lambda e: e.tensor_reduce(out=o, in_=a, axis=AX.X, op=ALU.add)))(
            ss8[sl], sq_s[sl].rearrange("p (h d) -> p h d", h=8)),
            reads=[sq_s[sl]], writes=[ss8[sl]])
        act(sd8[sl], ss8[sl], AF.Sqrt, bias=float(EPS), scale=1.0 / DH)

    def stage_a2(n):
        gi, gname, i = units[n]
        if gname in ("fl", "mv", "fv"):
            return
        pb = pbank(n % 4)
        sl = n % 2
        pb3 = pb.rearrange("p (h d) -> p h d", h=8)
        P.op("dve", (lambda o, a: (lambda e: e.reciprocal(out=o, in_=a)))(rs8[sl], sd8[sl]),
             reads=[sd8[sl]], writes=[rs8[sl]])
        rsb = rs8[sl].unsqueeze(2).to_broadcast([128, 8, 64])
        if gname == "fq":
            tt("dve", FQ[:, i, :, 0:64], pb3, rsb, ALU.mult)
            return
        t3 = t_s[n % 3].rearrange("p (h d) -> p h d", h=8)
        tt("dve", t3, pb3, rsb, ALU.mult)

    def stage_b(n):
        gi, gname, i = units[n]
        sl = n % 2
        if gname not in ("mq", "mk", "fk"):
            return
        t3 = t_s[n % 3].rearrange("p (h d) -> p h d", h=8)
        if gname == "fk":
            tt("pool", FK[:, i, :, 0:64], t3, g_fk.unsqueeze(1).to_broadcast([128, 8, 64]), ALU.mult)
            return
        gsrc = g_mq8 if gname == "mq" else smalls[:, S_MKG:S_MKG + 64]
        dst = MQ if gname == "mq" else MK
        u3 = t2_s[sl].rearrange("p (h d) -> p h d", h=8)
        tt("pool", u3, t3[:, :, 0:16], gsrc[:, 0:16].unsqueeze(1).to_broadcast([128, 8, 16]), ALU.mult)
        tt("pool", dst[:, i, :, 16:64], t3[:, :, 16:64], gsrc[:, 16:64].unsqueeze(1).to_broadcast([128, 8, 48]), ALU.mult)
        a3 = rA[sl].rearrange("p (h c) -> p h c", h=8)
        b3 = rB[sl].rearrange("p (h c) -> p h c", h=8)
        tt("dve", a3, u3, CSv[:, i].unsqueeze(1).to_broadcast([128, 8, 16]), ALU.mult)
        tt("dve", b3[:, :, 0:8], u3[:, :, 8:16], NSv[:, i, 0:8].unsqueeze(1).to_broadcast([128, 8, 8]), ALU.mult)
        tt("dve", b3[:, :, 8:16], u3[:, :, 0:8], NSv[:, i, 8:16].unsqueeze(1).to_broadcast([128, 8, 8]), ALU.mult)
        tt("dve", dst[:, i, :, 0:16], a3, b3, ALU.add)

    def cum_part(part):
        cum3 = cum.rearrange("p (i h) -> p i h", i=NT)
        r3 = res1.rearrange("p (i h) -> p i h", i=NT)
        pref3 = pref.rearrange("p (i h) -> p i h", i=NT)
        tot3 = totb.rearrange("p (i h) -> p i h", i=NT)
        if part == 0:
            act(zf, zf, AF.Exp, scale=-1.0)
            act(zf, zf, AF.Ln, bias=1.0, scale=1.0)
            ts("dve", zf, zf, -1.0, None, ALU.mult)
            pbc = pbank(6)[:, 0:128]
            pbt = pbank(6)[:, 128:256]
            mm(pbc, tri_f, zf, True, True)
            mm(pbt, ones_f, zf, True, True)
            copy("dve", totb, pbt)
            memset("dve", pref3[:, 0, :], 0.0)
            for i in range(1, 8):
                tt("dve", pref3[:, i, :], pref3[:, i - 1, :], tot3[:, i - 1, :], ALU.add)
        elif part == 1:
            for i in range(8, NT):
                tt("dve", pref3[:, i, :], pref3[:, i - 1, :], tot3[:, i - 1, :], ALU.add)
            tt("dve", cum, pbank(6)[:, 0:128], pref, ALU.add)
            ts("dve", negcum, cum, -1.0, None, ALU.mult)
        else:
            copy("dve", FQ[:, :, :, 64], cum3)
            tt("dve", r3, cum3, FQ[:, :, :, 64], ALU.subtract)
            copy("dve", FQ[:, :, :, 65], r3)
            tt("dve", r3, r3, FQ[:, :, :, 65], ALU.subtract)
            copy("dve", FQ[:, :, :, 66], r3)
            for q_ in range(3):
                ts("dve", FK[:, :, :, 67 + q_], FQ[:, :, :, 64 + q_], -1.0, None, ALU.mult)

    kmT3 = kmT.rearrange("p (h n) -> p h n", h=8)

    def kmean_block():
        pbk = pbank(6)
        for h in range(H):
            for n_ in range(7):
                for s_ in range(2):
                    mm(pbk[0:64, h * 8 + n_: h * 8 + n_ + 1], MK[:, 2 * n_ + s_, h, 0:64], invc[:, 0:1], s_ == 0, s_ == 1)
        memset("dve", kmT[0:64, :], 0.0)
        for h in range(H):
            copy("dve", kmT3[0:64, h, 0:7], pbk[0:64, h * 8: h * 8 + 7])

    def gate_tile_a(i):
        sl = i % 2
        pbt_ = pbank_bf(5)
        for h in range(H):
            tr(pbt_[0:64, h * 128:(h + 1) * 128], MQ[:, i, h, 0:64], ident_b)
        copy("dve", qt0[sl][0:64], pbt_[0:64, 0:1024].rearrange("p (h q) -> p h q", h=8))

    def gate_tile(i):
        b = i // 2
        sl = i % 2
        pbg = pbank(7)
        for h in range(H):
            mm(pbg[:, h * 8:(h + 1) * 8], qt0[sl][0:64, h, :], kmT3[0:64, h, :], True, True)
        copy("dve", gate_s[sl], pbg[:, 0:64])
        g3 = gate_s[sl].rearrange("p (h n) -> p h n", h=8)
        cmp4 = cmp_s[:, 0:8 * b * b].rearrange("p (h n m) -> p h n m", h=8, n=b)
        in0 = g3[:, :, 0:b].unsqueeze(2).to_broadcast([128, 8, b, b])
        in1 = g3[:, :, 0:b].unsqueeze(3).to_broadcast([128, 8, b, b])
        tt("dve", cmp4, in0, in1, ALU.is_gt)
        rk = rank_s[:, 0:8 * b].rearrange("p (h n) -> p h n", h=8)
        P.op("dve", (lambda o, a: (lambda e: e.tensor_reduce(out=o, in_=a, axis=AX.X, op=ALU.add)))(rk, cmp4),
             reads=[cmp4], writes=[rk])
        ts("dve", MQ[:, i, :, 64:64 + b], rk, 3.0, NEG, ALU.is_ge, ALU.mult)

    n_km = pos_of[("mk", 15)] + 6
    n_cum = pos_of[("fl", 15)] + 5
    extra_at = {n_km: [kmean_block], n_cum: [lambda: cum_part(0)], n_cum + 4: [lambda: cum_part(1)],
                n_cum + 8: [lambda: cum_part(2)]}
    for i_ in range(8, NT):
        extra_at.setdefault(n_km + 2 + 3 * (i_ - 8), []).append((lambda ii: (lambda: gate_tile_a(ii)))(i_))
        extra_at.setdefault(n_km + 4 + 3 * (i_ - 8), []).append((lambda ii: (lambda: gate_tile(ii)))(i_))
    n_late = pos_of[("mk", 15)] + 4

    for n in range(NU + 6):
        if n < NU:
            pe_unit(n)
        lk = n - n_late
        if lk >= 0 and lk % 6 == 0:
            if 1 <= lk // 6 <= len(late_pieces):
                sg, hf = late_pieces[lk // 6 - 1]
                st2_ = ada_compute(sg, hf, wlate, 4, row_form=True)
                if st2_ is not None:
                    extra_at.setdefault(n + 2, []).append(st2_)
            if lk // 6 < len(late_pieces):
                sg, hf = late_pieces[lk // 6]
                ada_dma(sg, hf, wlate)
        if 0 <= n - 1 < NU:
            stage_a(n - 1)
        if 0 <= n - 4 < NU:
            stage_b(n - 4)
        if 0 <= n - 2 < NU:
            stage_a2(n - 2)
        for f_ in extra_at.get(n, []):
            f_()
    make_ab(ab2, S_GFFN, 4, 3)
    dump("modfm", modfm, [128, 48])
    dump("gtb", gtb, [128, 2048])
    dump("cum", cum, [128, 128])

    dump("MQ", MQ, [128, NT, H, SW])
    dump("MK", MK, [128, NT, H, SW])
    dump("FQ", FQ, [128, NT, H, FW])
    dump("FK", FK, [128, NT, H, FW])
    dump("MV", MV, [128, NT, H, 64])
    dump("FV", FV, [128, NT, H, 64])
    if stop_after == "T2":
        P.emit()
        return nc, dbg

    memset("pool", vaug[0][:, :, 64:128], 0.0)
    memset("pool", vaug[0][:, :, 64:65], 1.0)
    memset("pool", vaug[1][:, :, 0:64], 0.0)
    memset("pool", vaug[1][:, :, 0:1], 1.0)
    negcum3 = negcum.rearrange("p (i h) -> p i h", i=NT)
    s_banks = (0, 1, 2)
    o_banks = (3, 4)
    tr_banks = (5,)
    LOOK = 2
    PRE = 34
    DEFER = 8

    def head_cfg(hg):
        typ = 0 if hg < 8 else 1
        h = hg % 8
        QS, KS, VS, W = (MQ, MK, MV, SW) if typ == 0 else (FQ, FK, FV, FW)
        par = hg % 2
        return dict(typ=typ, h=h, QS=QS, KS=KS, VS=VS, W=W, par=par, pair=hg // 2,
                    qtb=qt[hg % 2], ktb=kt[hg % 2], va=vaug[par],
                    Mv=65 if par == 0 else 128,
                    rows=slice(0, 64) if par == 0 else slice(64, 128),
                    srp=64 if par == 0 else 0)

    trc = [0]

    trbank_of = {}

    def prologue_part(hg, part):
        cf = head_cfg(hg)
        h = cf["h"]
        if part == 0:
            if cf["par"] == 0:
                copy("pool", vaug[0][:, :, 0:64], cf["VS"][:, :, h, :])
            else:
                copy("pool", vaug[1][:, :, 64:128], cf["VS"][:, :, h, :])
            return
        W = cf["W"]
        grp = (part - 1) // 2
        sub = (part - 1) % 2
        S_, dstT = ((cf["QS"], cf["qtb"]), (cf["KS"], cf["ktb"]))[grp // 2]
        half = grp % 2
        if sub == 0:
            tb_ = (5, 6, 7) if hg == 0 else tr_banks
            trbank_of[(hg, grp)] = pbank_bf(tb_[trc[0] % len(tb_)])
            trc[0] += 1
        pbt_ = trbank_of[(hg, grp)]
        for s_ in range(4 * sub, 4 * sub + 4):
            i = half * 8 + s_
            tr(pbt_[0:W, s_ * 128:(s_ + 1) * 128], S_[:, i, h, 0:W], ident_b)
        if sub == 1:
            copy("dve", dstT[0:W, half * 1024:(half + 1) * 1024], pbt_[0:W, 0:1024])

    steps = []
    for hg in range(16):
        for c in range(4):
            for j in range(4 * c + 4):
                steps.append((hg, c, j))
    NS_ = len(steps)
    first_step = {}
    for si, (hg, c, j) in enumerate(steps):
        if hg not in first_step:
            first_step[hg] = si
    pro_at = {}
    for hg in range(16):
        for part in range(9):
            at = max(0, first_step[hg] - PRE + 3 * part) if hg > 0 else 0
            pro_at.setdefault(at, []).append((hg, part))
    st_info = [None] * NS_
    obank_of = {}
    ocnt = 0
    deferred = []
    ncnt = [0]

    def do_S(si):
        hg, c, j = steps[si]
        cf = head_cfg(hg)
        W = cf["W"]
        q0 = max(512 * c, 128 * j)
        N = 512 * (c + 1) - q0
        sbk = pbank(s_banks[si % 3])
        diag = j >= 4 * c
        mm(sbk[:, 0:N], cf["ktb"][0:W, j * 128:(j + 1) * 128], cf["qtb"][0:W, q0:q0 + N], True, not diag)
        if diag:
            mm(sbk[:, 0:128], ident_b, tribias_b, False, True)
        st_info[si] = (q0, N, sbk)

    def do_rest(si):
        nonlocal ocnt
        hg, c, j = steps[si]
        cf = head_cfg(hg)
        q0, N, sbk = st_info[si]
        if j == 0:
            obank_of[(hg, c)] = pbank(o_banks[ocnt % 2])
            ocnt += 1
        ob = obank_of[(hg, c)]
        ptb = pt[si % 3]
        act(ptb[:, 0:N], sbk[:, 0:N], AF.Exp, bias=0.0, scale=1.0)
        last = (j == 4 * c + 3)
        mm(ob[0:cf["Mv"], q0 - 512 * c:512], cf["va"][:, j, 0:cf["Mv"]], ptb[:, 0:N], j == 0, last)
        if last:
            nb = ncnt[0] % NNB
            ncnt[0] += 1
            srp, rows, pair = cf["srp"], cf["rows"], cf["pair"]
            copy("dve", srow[nb][srp:srp + 1, :], ob[srp:srp + 1, :])
            copy("dve", osb[nb][rows, :], ob[rows, :])

            def mk_pcol(j4, nb=nb, srp=srp):
                def f():
                    pcol = pbank(6)[:, 0:4]
                    tr(pcol[:, j4:j4 + 1], srow[nb][srp:srp + 1, j4 * 128:(j4 + 1) * 128], ones_f[srp:srp + 1, 0:1])
                    if j4 == 3:
                        P.op("dve", (lambda o, a: (lambda e: e.reciprocal(out=o, in_=a)))(rcol[nb], pcol),
                             reads=[pcol], writes=[rcol[nb]])
                        copy("dve", rhi[nb], rcol[nb])
                        tt("dve", rlo[nb], rcol[nb], rhi[nb], ALU.subtract)
                        idb = ident_b.unsqueeze(1).to_broadcast([128, 4, 128])
                        tt("dve", dhi[nb].rearrange("p (j q) -> p j q", j=4), idb,
                           rhi[nb].unsqueeze(2).to_broadcast([128, 4, 128]), ALU.mult)
                        tt("dve", dlo[nb].rearrange("p (j q) -> p j q", j=4), idb,
                           rlo[nb].unsqueeze(2).to_broadcast([128, 4, 128]), ALU.mult)
                return f

            def fin_hi(nb=nb):
                mm(pbank(7), ones_b, dhi[nb], True, False)

            def fin_lo(nb=nb, rows=rows, pair=pair, c=c):
                bcb = pbank(7)
                mm(bcb, ones_b, dlo[nb], False, True)
                tt("dve", oT[rows, pair, c * 512:(c + 1) * 512], osb[nb][rows, :], bcb[rows, :], ALU.mult)
            base = si + LOOK
            for j4 in range(4):
                deferred.append((base + 6 + j4, mk_pcol(j4)))
            deferred.append((base + 18, fin_hi))
            deferred.append((base + 19, fin_lo))

    for n in range(NS_ + LOOK + 26):
        for (hg_, part_) in pro_at.get(n, []):
            prologue_part(hg_, part_)
        wo_i = n - first_step[9] - 4
        if wo_i >= 0 and wo_i % 5 == 0 and wo_i // 5 < 8:
            p_ = wo_i // 5
            wout_v = wout_d.rearrange("p (k c) -> p k c", k=8)
            st_ = stage[p_ % 2]
            dma("sp", st_, wout_v[:, p_, :])
            tt("pool", woutb[:, p_, :], st_, gtb[:, 0:1024], ALU.mult)
        if n < NS_:
            do_S(n)
        deferred.sort(key=lambda t_: t_[0])
        while deferred and deferred[0][0] <= n:
            deferred.pop(0)[1]()
        if 0 <= n - LOOK < NS_:
            do_rest(n - LOOK)
    while deferred:
        deferred.pop(0)[1]()
    dump("oT", oT, [128, 8, T])
    if stop_after == "T3":
        P.emit()
        return nc, dbg

    for i in range(NT):
        dma("sp", x1[:, i, :], x_t[i])
    memset("dve", ss_a, 0.0)
    wup_v = wup_d.rearrange("p (m s k c) -> p m s k c", m=NM, s=2, k=8)
    dma("pool", wupb[0], wup_v[:, 0])
    dma("pool", wupb[1], wup_v[:, 1])

    def outproj_group(g):
        for i in range(4 * g, 4 * g + 4):
            yb = psum[:, (4 + 2 * (i % 2)) * 512:(4 + 2 * (i % 2)) * 512 + 1024]
            for cb in range(2):
                for p_ in range(8):
                    mm(yb[:, cb * 512:(cb + 1) * 512], oT[:, p_, i * 128:(i + 1) * 128], woutb[:, p_, cb * 512:(cb + 1) * 512],
                       p_ == 0, p_ == 7)
            tt("dve", x1[:, i, :], yb, x1[:, i, :], ALU.add)
            norm_stats(i, x1[:, i, :], xn2[g % 2][:, i % 4, :])

    for g in range(4):
        for i in range(4 * g, 4 * g + 4):
            yb = psum[:, (4 + 2 * (i % 2)) * 512:(4 + 2 * (i % 2)) * 512 + 1024]
            for cb in range(2):
                for p_ in range(8):
                    mm(yb[:, cb * 512:(cb + 1) * 512], oT[:, p_, i * 128:(i + 1) * 128], woutb[:, p_, cb * 512:(cb + 1) * 512],
                       p_ == 0, p_ == 7)
            tt("dve", x1[:, i, :], yb, x1[:, i, :], ALU.add)
            norm_rstd(i, x1[:, i, :])

    def stats_group(g):
        for i in range(4 * g, 4 * g + 4):
            ts("dve", xn2[g % 2][:, i % 4, :], x1[:, i, :], rstd_a[:, i:i + 1], None, ALU.mult)
    stats_group(0)
    for g in range(4):
        if g + 1 < 4:
            stats_group(g + 1)
        norm_tr(g, xn2[g % 2], ab2, hn2T)
    dump("x1", x1, [128, NT, D])
    dump("hn2T", hn2T, [128, 8, T])
    if stop_after == "T4":
        P.emit()
        return nc, dbg

    wup_v = wup_d.rearrange("p (m s k c) -> p m s k c", m=NM, s=2, k=8)
    wdn_v = wdn_d.rearrange("p (m c) -> p m c", m=NM)
    cw = smalls[:, S_CONVW:S_CONVW + 132].rearrange("p (i m) -> p i m", i=3)
    cbv = smalls[:, S_CONVB:S_CONVB + 44]
    out_t = out_d.rearrange("(i p) d -> i p d", p=128)
    m0 = 0
    ucnt = 0
    wcnt = 0
    scnt5 = 0
    ycnt = 0
    for g, gs in enumerate(GSZ):
        for ml in range(gs):
            st_ = stage5[scnt5 % 2]
            scnt5 += 1
            dma("sp", st_, wdn_v[:, m0 + ml, :])
            tt("dve", wdnb[:, ml, :], st_, gtb[:, 1024:2048], ALU.mult)
        for ml in range(gs):
            m = m0 + ml
            wb = wupb[m % 2]
            if 1 <= m and m + 1 < NM:
                dma("pool", wupb[(m + 1) % 2], wup_v[:, m + 1])
            for half in range(2):
                t0 = half * HW_
                bufi = ucnt % 2
                for part, (ub, cbuf) in enumerate(((ua[bufi], ca[bufi]), (uv[bufi], cv[bufi]))):
                    ch = part * NM + m
                    pb2 = psum[:, (2 * (ucnt % 2)) * 512 + 0: (2 * (ucnt % 2)) * 512 + 1024] if part == 0 else \
                        psum[:, (2 * (ucnt % 2)) * 512 + 0: (2 * (ucnt % 2)) * 512 + 1024]
                    dbk = (2 * ucnt + part) % 4
                    pb2 = psum[:, dbk * 1024: dbk * 1024 + 1024]
                    for cb in range(2):
                        for k in range(8):
                            mm(pb2[:, cb * 512:(cb + 1) * 512], wb[:, part, k, :],
                               hn2T[:, k, t0 + cb * 512: t0 + (cb + 1) * 512], k == 0, k == 7)
                    if half == 0:
                        act(ub[:, 0:2], zc4[:, 0:2], AF.Copy)
                    else:
                        prev = (ua if part == 0 else uv)[(ucnt - 1) % 2]
                        act(ub[:, 0:2], prev[:, HW_:HW_ + 2], AF.Copy)
                    act(ub[:, 2:2 + HW_], pb2, AF.Copy)
                    act(cbuf, ub[:, 2:2 + HW_], AF.Identity, bias=cbv[:, ch:ch + 1], scale=cw[:, 2, ch:ch + 1])
                    stt("dve", cbuf, ub[:, 1:1 + HW_], cw[:, 1, ch:ch + 1], cbuf, ALU.mult, ALU.add)
                    stt("dve", cbuf, ub[:, 0:HW_], cw[:, 0, ch:ch + 1], cbuf, ALU.mult, ALU.add)
                act(ca[bufi], ca[bufi], AF.Silu)
                tt("pool", hT[:, ml, t0:t0 + HW_], ca[bufi], cv[bufi], ALU.mult)
                ucnt += 1
        for i in range(NT):
            yb = psum[:, (4 + 2 * (ycnt % 2)) * 512:(4 + 2 * (ycnt % 2)) * 512 + 1024]
            ycnt += 1
            for cb in range(2):
                for ml in range(gs):
                    mm(yb[:, cb * 512:(cb + 1) * 512], hT[:, ml, i * 128:(i + 1) * 128], wdnb[:, ml, cb * 512:(cb + 1) * 512],
                       ml == 0, ml == gs - 1)
            tt("dve", x1[:, i, :], yb, x1[:, i, :], ALU.add)
            if g == len(GSZ) - 1:
                dma("sp", out_t[i], x1[:, i, :])
        m0 += gs
    P.emit()
    return nc, dbg


def _rope_tables():
    half = 8
    inv_freq = np.power(np.float32(ROPE_THETA), (-2.0 * np.arange(half, dtype=np.float32) / 16).astype(np.float32)).astype(np.float32)
    pos = np.arange(T, dtype=np.float32)
    ang = pos[:, None] * inv_freq[None, :]
    cos = np.cos(ang).astype(np.float32)
    sin = np.sin(ang).astype(np.float32)
    cs = np.concatenate([cos, cos], axis=1)
    ns = np.concatenate([-sin, sin], axis=1)
    cs = cs.reshape(NT, 128, 16).transpose(1, 0, 2).reshape(128, NT * 16)
    ns = ns.reshape(NT, 128, 16).transpose(1, 0, 2).reshape(128, NT * 16)
    return cs, ns


def _prep_shared(inp):
    f = lambda a: np.ascontiguousarray(np.asarray(a, dtype=np.float32))
    pk = lambda w, k: f(w.reshape(k, 128, -1).transpose(1, 0, 2).reshape(128, -1))
    sh = {}
    sh["w_ada"] = pk(f(inp["w_ada"])[0], 8)
    sh["w_in"] = pk(f(inp["w_in"])[0], 8)
    sh["w_out"] = pk(f(inp["w_out"])[0], 8)
    wup = f(inp["w_up"])[0]
    wup = wup.reshape(8, 128, 2, NM, 128)
    sh["w_up"] = f(wup.transpose(1, 3, 2, 0, 4).reshape(128, -1))
    sh["w_down"] = pk(f(inp["w_down"])[0], NM)
    sm = np.zeros((128, S_TOT), np.float32)
    fm = lambda v, k: f(v).reshape(k, 128).T
    sm[:, S_GMIX:S_GMIX + 8] = fm(inp["g_mix"][0], 8)
    sm[:, S_GFFN:S_GFFN + 8] = fm(inp["g_ffn"][0], 8)
    sm[:, S_BADA:S_BADA + 48] = fm(inp["b_ada"][0], 48)
    cw = f(inp["conv_w"])[0]
    for i in range(3):
        sm[:, S_CONVW + i * 44: S_CONVW + (i + 1) * 44] = fm(cw[i], 44)
    sm[:, S_CONVB:S_CONVB + 44] = fm(inp["conv_b"][0], 44)
    sm[:, S_MQG:S_MQG + 64] = f(inp["moba_q_gain"])[0][None, :]
    sm[:, S_MKG:S_MKG + 64] = f(inp["moba_k_gain"])[0][None, :]
    sm[:, S_FQG:S_FQG + 64] = f(inp["fox_q_gain"])[0][None, :]
    sm[:, S_FKG:S_FKG + 64] = f(inp["fox_k_gain"])[0][None, :]
    sm[:, S_BF:S_BF + 8] = f(inp["b_forget"])[0][None, :]
    if not ROPE_ON_DEVICE:
        cs, ns = _rope_tables()
        sm[:, S_CS:S_CS + 256] = cs
        sm[:, S_NS:S_NS + 256] = ns
    sh["smalls"] = sm
    ba = f(inp["b_ada"])[0]
    bc = np.zeros((128, 2048), np.float32)
    bc[:, 0:1024] = ba[None, 2048:3072]
    bc[:, 1024:2048] = ba[None, 5120:6144]
    sh["bcastb"] = bc
    return sh


def _core_inputs(sh, x_b, c_b):
    m = dict(sh)
    sm = sh["smalls"].copy()
    sm[:, S_C:S_C + 8] = np.asarray(c_b, np.float32).reshape(8, 128).T
    m["smalls"] = sm
    m["x"] = np.ascontiguousarray(np.asarray(x_b, np.float32))
    return m


_CACHE = {}


def kernel(**inputs):
    x = np.asarray(inputs["x"], np.float32)
    c = np.asarray(inputs["c"], np.float32)
    sh = _prep_shared(inputs)
    if "nc" not in _CACHE:
        _CACHE["nc"] = build_program(debug=False)[0]
    nc = _CACHE["nc"]
    in_maps = [_core_inputs(sh, x[b], c[b]) for b in range(8)]
    res = run_bass_kernel_spmd(nc, in_maps, core_ids=list(range(8)))
    out = np.stack([np.asarray(res.results[b]["out"], np.float32).reshape(T, D) for b in range(8)], axis=0)
    return out
```

```python
import math
import numpy as np
import concourse.bass as bass
import concourse.mybir as mybir
from concourse.bass_utils import run_bass_kernel_spmd

F32 = mybir.dt.float32
BF16 = mybir.dt.bfloat16
AF = mybir.ActivationFunctionType
ALU = mybir.AluOpType
AX = mybir.AxisListType

T = 2048
D = 1024
NT = 16
H = 8
DH = 64
DFF = 2816
NM = 22
INC = 3080
EPS = 1e-6
NEG = -30000.0
ROPE_THETA = 500000.0

SEM_EPOCH = 20000
ROPE_ON_DEVICE = True
GRAN = 128

S_C, S_GMIX, S_GFFN, S_BADA, S_CONVW, S_CONVB = 0, 8, 16, 24, 72, 204
S_MQG, S_MKG, S_FQG, S_FKG, S_BF, S_CS, S_NS = 256, 320, 384, 448, 512, 576, 832
S_TOT = 1152


class Op:
    __slots__ = ("eng", "fn", "deps", "signal", "sig_idx", "is_dma", "dma_sem", "dma_val", "seq")

    def __init__(self, eng, fn, is_dma):
        self.eng = eng
        self.fn = fn
        self.deps = []
        self.signal = is_dma
        self.sig_idx = -1
        self.is_dma = is_dma
        self.dma_sem = None
        self.dma_val = 0
        self.seq = -1


def _ap_range(ap):
    sp = str(ap.space)
    esz = mybir.dt.size(ap.dtype) if hasattr(mybir.dt, "size") else None
    if esz is None:
        esz = 2 if ap.dtype == BF16 else 4
    dims = list(ap.ap)
    row = dims[0][0]
    off = ap.offset % row if row > 0 else ap.offset
    span = 1
    for st, cnt in dims[1:]:
        span += (cnt - 1) * abs(st)
    return sp, off * esz, (off + span) * esz


class Prog:
    ENGS = ("pe", "act", "dve", "pool", "sp")

    def __init__(self, nc, n_dma_sems=12):
        self.nc = nc
        self.ops = {e: [] for e in self.ENGS}
        self.n_dma_sems = n_dma_sems
        self.sb_w = {}
        self.sb_r = {}
        self.ps = {}
        self.nseq = 0

    @staticmethod
    def _esz(dt):
        return 2 if dt == BF16 else 4

    def _range(self, ap):
        sps = str(ap.space)
        sp = "psum" if sps == "PSUM" else ("sbuf" if sps == "SB" else "dram")
        esz = self._esz(ap.dtype)
        dims = list(ap.ap)
        row = dims[0][0]
        off = ap.offset % row if row > 0 else ap.offset
        span = 1
        for st, cnt in dims[1:]:
            span += (cnt - 1) * abs(st)
        return sp, off * esz, (off + span) * esz

    def _add(self, o, reads, writes):
        deps = {}

        def add_dep(d):
            if d is not None and d is not o:
                deps[id(d)] = d

        acc = []
        for ap in reads:
            acc.append((ap, False))
        for ap in writes:
            acc.append((ap, True))
        for ap, is_w in acc:
            sp, lo, hi = self._range(ap)
            if sp == "dram":
                continue
            if sp == "psum":
                for b in range(lo // 2048, (hi - 1) // 2048 + 1):
                    st = self.ps.setdefault(b, {})
                    for e2, (op2, w2) in st.items():
                        if e2 != o.eng:
                            add_dep(op2)
                        elif o.eng != "pe" and (w2 or is_w):
                            add_dep(op2)
            else:
                for g in range(lo // GRAN, (hi - 1) // GRAN + 1):
                    add_dep(self.sb_w.get(g))
                    if is_w:
                        rs = self.sb_r.get(g)
                        if rs:
                            for r in rs.values():
                                add_dep(r)
        for ap, is_w in acc:
            sp, lo, hi = self._range(ap)
            if sp == "dram":
                continue
            if sp == "psum":
                for b in range(lo // 2048, (hi - 1) // 2048 + 1):
                    st = self.ps.setdefault(b, {})
                    prev = st.get(o.eng)
                    if prev is not None and prev[0] is o:
                        st[o.eng] = (o, prev[1] or is_w)
                    else:
                        st[o.eng] = (o, is_w)
            else:
                key = ("dma", id(o)) if o.is_dma else o.eng
                for g in range(lo // GRAN, (hi - 1) // GRAN + 1):
                    if is_w:
                        self.sb_w[g] = o
                        self.sb_r[g] = {}
                    else:
                        self.sb_r.setdefault(g, {})[key] = o
        best = {}
        dl = []
        for d in deps.values():
            if d.is_dma:
                dl.append(d)
                continue
            if d.eng == "pe" and o.eng == "pe" and not o.is_dma:
                continue
            b = best.get(d.eng)
            if b is None or d.seq > b.seq:
                best[d.eng] = d
        dl.extend(best.values())
        o.deps = dl
        for d in dl:
            d.signal = True
        o.seq = self.nseq
        self.nseq += 1
        self.ops[o.eng].append(o)
        return o

    def op(self, eng, fn, reads=(), writes=()):
        return self._add(Op(eng, fn, False), reads, writes)

    def dma(self, eng, fn, reads=(), writes=()):
        return self._add(Op(eng, fn, True), reads, writes)

    def emit(self):
        nc = self.nc
        nsig = {}
        for e in self.ENGS:
            k = 0
            for o in self.ops[e]:
                if (not o.is_dma) and o.signal:
                    k += 1
                    o.sig_idx = k
            nsig[e] = k
        esems = {}
        for e in self.ENGS:
            n_ep = max(1, (nsig[e] + SEM_EPOCH - 1) // SEM_EPOCH)
            esems[e] = [nc.alloc_semaphore(f"s_{e}_{i}") for i in range(n_ep)]
        dsems, dcount = {}, {}
        for e in self.ENGS:
            dl = [o for o in self.ops[e] if o.is_dma]
            if dl:
                n = min(self.n_dma_sems, len(dl))
                dsems[e] = [nc.alloc_semaphore(f"d_{e}_{i}") for i in range(n)]
                dcount[e] = [0] * n
                for k, o in enumerate(dl):
                    j = k % n
                    dcount[e][j] += 16
                    o.dma_sem = (e, j)
                    o.dma_val = dcount[e][j]

        def sem_of(d):
            if d.is_dma:
                e, j = d.dma_sem
                return ("d", e, j), dsems[e][j], d.dma_val
            ep = (d.sig_idx - 1) // SEM_EPOCH
            return ("c", d.eng, ep), esems[d.eng][ep], d.sig_idx - ep * SEM_EPOCH

        prog = self

        def run_engine(ename, eng):
            seen = {}
            for o in prog.ops[ename]:
                waits = {}
                for d in o.deps:
                    key, sem, val = sem_of(d)
                    if seen.get(key, 0) >= val:
                        continue
                    if key not in waits or waits[key][1] < val:
                        waits[key] = (sem, val)
                if o.is_dma:
                    e, j = o.dma_sem
                    key = ("d", e, j)
                    pv = o.dma_val - 16
                    if pv > 0 and seen.get(key, 0) < pv:
                        if key not in waits or waits[key][1] < pv:
                            waits[key] = (dsems[e][j], pv)
                for key, (sem, val) in waits.items():
                    eng.wait_ge(sem, val)
                    seen[key] = val
                ins = o.fn(eng)
                if o.is_dma:
                    e, j = o.dma_sem
                    ins.then_inc(dsems[e][j], 16)
                elif o.signal:
                    ep = (o.sig_idx - 1) // SEM_EPOCH
                    ins.then_inc(esems[ename][ep], 1)
            if ename in dsems:
                for j, s in enumerate(dsems[ename]):
                    if dcount[ename][j] > 0:
                        eng.wait_ge(s, dcount[ename][j])

        with nc.Block() as block:
            @block.tensor
            def _(e):
                run_engine("pe", e)

            @block.scalar
            def _(e):
                run_engine("act", e)

            @block.vector
            def _(e):
                run_engine("dve", e)

            @block.gpsimd
            def _(e):
                run_engine("pool", e)

            @block.sync
            def _(e):
                run_engine("sp", e)


def build_program(debug=False, stop_after=None):
    nc = bass.Bass("TRN2", target_bir_lowering=False)
    P = Prog(nc)

    def din(name, shape):
        return nc.dram_tensor(name, shape, F32, kind="ExternalInput").ap()

    x_d = din("x", [T, D])
    wada_d = din("w_ada", [128, 8 * 6144])
    win_d = din("w_in", [128, 8 * INC])
    wout_d = din("w_out", [128, 8 * D])
    wup_d = din("w_up", [128, NM * 2 * 8 * 128])
    wdn_d = din("w_down", [128, NM * D])
    smalls_d = din("smalls", [128, S_TOT])
    bcast_d = din("bcastb", [128, 2048])
    out_d = nc.dram_tensor("out", [T, D], F32, kind="ExternalOutput").ap()
    dbg = {}

    ARENA_W = 53180
    arena = nc.alloc_sbuf_tensor("arena", [128, ARENA_W], F32)
    psum = nc.alloc_psum_tensor("psum", [128, 4096], F32)

    def sb(off_b, n, dt):
        assert off_b % 4 == 0
        if dt == F32:
            assert off_b // 4 + n <= ARENA_W, (off_b, n)
            return arena[:, off_b // 4: off_b // 4 + n]
        assert off_b + 2 * n <= ARENA_W * 4, (off_b, n)
        nw = (n + 1) // 2
        return arena[:, off_b // 4: off_b // 4 + nw].bitcast(BF16)[:, 0:n]

    def pbank(b, n=512, nb=1):
        return psum[:, b * 512: b * 512 + n]

    def pbank_bf(b):
        return psum[:, b * 512:(b + 1) * 512].bitcast(BF16)

    class Alloc:
        def __init__(self, base, limit):
            self.p = base
            self.limit = limit

        def take(self, nbytes, align=128):
            self.p = (self.p + align - 1) // align * align
            o = self.p
            self.p += nbytes
            assert self.p <= self.limit, ("arena overflow", self.p, self.limit)
            return o

    LIMIT = ARENA_W * 4
    CA = Alloc(0, 27 * 1024)
    smalls = sb(CA.take(S_TOT * 4), S_TOT, F32)
    gtb = sb(CA.take(2048 * 4), 2048, F32)
    ident_f = sb(CA.take(512), 128, F32)
    tri_f = sb(CA.take(512), 128, F32)
    ones_f = sb(CA.take(512), 128, F32)
    ident_b = sb(CA.take(256), 128, BF16)
    ones_b = sb(CA.take(256), 128, BF16)
    tribias_b = sb(CA.take(256), 128, BF16)
    modfm = sb(CA.take(192), 48, F32)
    ab1 = sb(CA.take(64), 16, F32)
    ab2 = sb(CA.take(64), 16, F32)
    scf = sb(CA.take(32), 8, F32)
    scb16 = sb(CA.take(16), 8, BF16)
    scbc = sb(CA.take(2048), 1024, BF16)
    ss_a = sb(CA.take(64), 16, F32)
    rstd_a = sb(CA.take(64), 16, F32)
    epsc = sb(CA.take(16), 4, F32)
    invc = sb(CA.take(8), 4, BF16)
    zc4 = sb(CA.take(16), 4, F32)
    wflb = sb(CA.take(8 * 8 * 2), 64, BF16).rearrange("p (k c) -> p k c", k=8)
    ss8 = [sb(CA.take(32), 8, F32) for _ in range(2)]
    sd8 = [sb(CA.take(32), 8, F32) for _ in range(2)]
    rs8 = [sb(CA.take(32), 8, F32) for _ in range(2)]
    g_mq8 = sb(CA.take(256), 64, F32)
    g_fk = sb(CA.take(256), 64, F32)
    zf = sb(CA.take(512), 128, F32)
    cum = sb(CA.take(512), 128, F32)
    negcum = sb(CA.take(512), 128, F32)
    totb = sb(CA.take(512), 128, F32)
    pref = sb(CA.take(512), 128, F32)
    res1 = sb(CA.take(512), 128, F32)
    kmT = sb(CA.take(128), 64, BF16)
    gate_s = [sb(CA.take(256), 64, F32) for _ in range(2)]
    cmp_s = sb(CA.take(1568), 392, F32)
    rank_s = sb(CA.take(224), 56, F32)
    junk = sb(CA.take(2048), 1024, BF16)
    CEND = CA.p

    A = Alloc(CEND, LIMIT)
    hnT_o = A.take(8 * T * 2)
    hnT = sb(hnT_o, 8 * T, BF16).rearrange("p (k t) -> p k t", k=8)
    slab_o = A.p
    SW = 72
    FW = 70
    MQ_O = A.take(NT * H * SW * 2)
    MQ = sb(MQ_O, NT * H * SW, BF16).rearrange("p (i h w) -> p i h w", i=NT, h=H)
    MK = sb(A.take(NT * H * SW * 2), NT * H * SW, BF16).rearrange("p (i h w) -> p i h w", i=NT, h=H)
    FQ = sb(A.take(NT * H * FW * 2), NT * H * FW, BF16).rearrange("p (i h w) -> p i h w", i=NT, h=H)
    FK = sb(A.take(NT * H * FW * 2), NT * H * FW, BF16).rearrange("p (i h w) -> p i h w", i=NT, h=H)
    MV_O = A.take(NT * 512 * 2)
    MV = sb(MV_O, NT * 512, BF16).rearrange("p (i h d) -> p i h d", i=NT, h=H)
    FV = sb(A.take(NT * 512 * 2), NT * 512, BF16).rearrange("p (i h d) -> p i h d", i=NT, h=H)
    slab_end = A.p
    t2_o = A.p
    wch = [sb(A.take(8 * 512 * 2), 8 * 512, BF16).rearrange("p (k c) -> p k c", k=8) for _ in range(2)]
    sq_s = [sb(A.take(2048), 512, F32) for _ in range(2)]
    t_s = [sb(A.take(2048), 512, F32) for _ in range(3)]
    t2_s = [sb(A.take(512), 128, F32) for _ in range(2)]
    rA = [sb(A.take(512), 128, F32) for _ in range(2)]
    rB = [sb(A.take(512), 128, F32) for _ in range(2)]
    qt0 = [sb(A.take(2048), 1024, BF16).rearrange("p (h q) -> p h q", h=8) for _ in range(2)]
    wlate = sb(A.take(8 * 512 * 2), 8 * 512, BF16).rearrange("p (k c) -> p k c", k=8)
    t2_end = A.p
    rope_w = sb(t2_o + 16384, 704, F32)
    B1 = Alloc(slab_o, slab_end)
    wada_s = [sb(B1.take(8 * 512 * 2), 8 * 512, BF16).rearrange("p (k c) -> p k c", k=8) for _ in range(4)]
    xg = [sb(B1.take(4 * 1024 * 4), 4096, F32).rearrange("p (s d) -> p s d", s=4) for _ in range(4)]

    B3 = Alloc(hnT_o, slab_o)
    qt = [sb(B3.take(4096), 2048, BF16) for _ in range(2)]
    kt = [sb(B3.take(4096), 2048, BF16) for _ in range(2)]
    vaug = [sb(B3.take(4096), 2048, BF16).rearrange("p (i c) -> p i c", i=NT) for _ in range(2)]
    pt = [sb(B3.take(1024), 512, BF16) for _ in range(3)]
    dlo_b3 = sb(B3.take(1024), 512, BF16)
    assert B3.p <= slab_o
    OT_O = LIMIT - 8 * T * 2
    C3 = Alloc(t2_o, OT_O)
    NNB = 3
    osb = [sb(C3.take(2048), 512, F32) for _ in range(NNB)]
    dhi = [sb(C3.take(1024), 512, BF16) for _ in range(NNB)]
    dlo = [sb(C3.take(1024), 512, BF16) for _ in range(NNB - 1)] + [dlo_b3]
    rhi = [sb(C3.take(16, 16), 4, BF16) for _ in range(NNB)]
    rlo = [sb(C3.take(16, 16), 4, F32) for _ in range(NNB)]
    srow = [sb(B3.take(2048), 512, F32) for _ in range(NNB - 1)]
    srow.append(sb(C3.take(2048), 512, F32))
    rcol = [sb(C3.take(16, 16), 4, F32) for _ in range(NNB)]
    assert B3.p <= slab_o
    oT = sb(OT_O, 8 * T, BF16).rearrange("p (k t) -> p k t", k=8)

    B4 = Alloc(CEND, OT_O)
    x1 = sb(B4.take(NT * D * 4), NT * D, F32).rearrange("p (i d) -> p i d", i=NT)
    hn2T = sb(B4.take(8 * T * 2), 8 * T, BF16).rearrange("p (k t) -> p k t", k=8)
    T5_O = B4.p
    xn2_0 = sb(B4.take(4 * 1024 * 4), 4096, F32).rearrange("p (s d) -> p s d", s=4)
    xn2_1 = sb(B4.take(4 * 1024 * 4), 4096, F32).rearrange("p (s d) -> p s d", s=4)
    xn2 = [xn2_0, xn2_1]
    woutb = sb(MV_O, 8 * 1024, BF16).rearrange("p (k c) -> p k c", k=8)
    stage = [sb(MQ_O + 4096 * q_, 1024, F32) for q_ in range(2)]
    B5 = Alloc(T5_O, LIMIT)
    GSZ = [6, 6, 5, 5]
    hT = sb(B5.take(6 * T * 2), 6 * T, BF16).rearrange("p (m t) -> p m t", m=6)
    wdnb = sb(B5.take(6 * 1024 * 2), 6 * 1024, BF16).rearrange("p (m c) -> p m c", m=6)
    wupb = [sb(B5.take(2 * 8 * 128 * 2), 2048, BF16).rearrange("p (s k c) -> p s k c", s=2, k=8) for _ in range(2)]
    stage5 = [sb(B5.take(4096), 1024, F32) for _ in range(2)]
    HW_ = 1024
    ua = [sb(B5.take((HW_ + 2) * 4), HW_ + 2, F32) for _ in range(2)]
    uv = [sb(B5.take((HW_ + 2) * 4), HW_ + 2, F32) for _ in range(2)]
    ca = [sb(B5.take(HW_ * 4), HW_, F32) for _ in range(2)]
    cv = [sb(B5.take(HW_ * 4), HW_, F32) for _ in range(2)]

    def V(eng, name, *args, reads=(), writes=(), **kw):
        def fn(e):
            return getattr(e, name)(*args, **kw)
        return P.op(eng, fn, reads=reads, writes=writes)

    def act(out, in_, func, bias=None, scale=None, accum_out=None, extra_reads=()):
        kw = {}
        rd = [in_] + list(extra_reads)
        if bias is not None:
            kw["bias"] = bias
            if not isinstance(bias, float):
                rd.append(bias)
        if scale is not None:
            kw["scale"] = scale
            if not isinstance(scale, float):
                rd.append(scale)
        wr = [out]
        if accum_out is not None:
            kw["accum_out"] = accum_out
            wr.append(accum_out)
        return P.op("act", lambda e: e.activation(out=out, in_=in_, func=func, **kw), reads=rd, writes=wr)

    def tt(eng, out, in0, in1, op):
        return P.op(eng, lambda e: e.tensor_tensor(out=out, in0=in0, in1=in1, op=op), reads=[in0, in1], writes=[out])

    def ts(eng, out, in0, s1, s2, op0, op1=None):
        rd = [in0]
        if not isinstance(s1, (float, int)):
            rd.append(s1)
        if s2 is not None and not isinstance(s2, (float, int)):
            rd.append(s2)
        if op1 is None:
            return P.op(eng, lambda e: e.tensor_scalar(out=out, in0=in0, scalar1=s1, scalar2=None, op0=op0),
                        reads=rd, writes=[out])
        return P.op(eng, lambda e: e.tensor_scalar(out=out, in0=in0, scalar1=s1, scalar2=s2, op0=op0, op1=op1),
                    reads=rd, writes=[out])

    def stt(eng, out, in0, scalar, in1, op0, op1):
        rd = [in0, in1]
        if not isinstance(scalar, (float, int)):
            rd.append(scalar)
        return P.op(eng, lambda e: e.scalar_tensor_tensor(out=out, in0=in0, scalar=scalar, in1=in1, op0=op0, op1=op1),
                    reads=rd, writes=[out])

    def copy(eng, out, in_):
        return P.op(eng, lambda e: e.tensor_copy(out=out, in_=in_), reads=[in_], writes=[out])

    def memset(eng, ap, val):
        return P.op(eng, lambda e: e.memset(ap, val), writes=[ap])

    def mm(out, lhsT, rhs, start, stop):
        return P.op("pe", lambda e: e.matmul(out, lhsT=lhsT, rhs=rhs, start=start, stop=stop),
                    reads=[lhsT, rhs], writes=[out])

    def tr(out, in_, ident):
        return P.op("pe", lambda e: e.transpose(out=out, in_=in_, identity=ident), reads=[in_, ident], writes=[out])

    def dma(eng, out, in_):
        return P.dma(eng, lambda e: e.dma_start(out=out, in_=in_), reads=[in_], writes=[out])

    def dump(name, ap, shape):
        if not debug:
            return
        d = nc.dram_tensor("dbg_" + name, list(shape), ap.dtype, kind="ExternalOutput").ap()
        dbg[name] = d
        dma("sp", d, ap)

    def warmup(n, bank, rhs):
        pbw = pbank(bank)
        for _ in range(n):
            mm(pbw, ident_b, rhs, True, True)

    dma("sp", smalls, smalls_d)
    dma("sp", gtb, bcast_d)

    def asel(ap, cmp_op, fill, step, cm):
        return P.op("pool", lambda e: e.affine_select(out=ap, in_=ap, pattern=[[step, 128]], compare_op=cmp_op,
                                                      fill=fill, base=0, channel_multiplier=cm),
                    reads=[ap], writes=[ap])
    memset("pool", ident_f, 1.0)
    asel(ident_f, ALU.is_equal, 0.0, -1, 1)
    memset("pool", tri_f, 1.0)
    asel(tri_f, ALU.is_ge, 0.0, 1, -1)
    memset("pool", res1, 0.0)
    asel(res1, ALU.is_ge, NEG, 1, -1)
    copy("pool", tribias_b, res1)
    copy("pool", ident_b, ident_f)
    memset("pool", ones_f, 1.0)
    memset("pool", ones_b, 1.0)
    memset("pool", epsc[:, 0:1], EPS)
    memset("pool", epsc[:, 1:2], 0.0)
    memset("pool", epsc[:, 2:3], 1.0)
    memset("pool", invc, 1.0 / 256.0)
    memset("pool", zc4, 0.0)
    eps_ap = epsc[:, 0:1]
    zero_ap = epsc[:, 1:2]
    one_ap = epsc[:, 2:3]

    act(scf, smalls[:, S_C:S_C + 8], AF.Silu)
    copy("dve", scb16, scf)
    copy("dve", scbc.rearrange("p (k m) -> p k m", k=8), scf.unsqueeze(2).to_broadcast([128, 8, 128]))
    scbc3 = scbc.rearrange("p (k m) -> p k m", k=8)

    wada_v = wada_d.rearrange("p (k c) -> p k c", k=8)

    def ada_dma(seg, half, buf):
        dma("pool", buf, wada_v[:, :, seg * 1024 + half * 512: seg * 1024 + (half + 1) * 512])

    def ada_compute(seg, half, buf, bank, row_form=False):
        pb = pbank(bank)
        if row_form and seg not in (2, 5):
            for k in range(8):
                mm(pb, scbc3[:, k, :], buf[:, k, :], k == 0, k == 7)
            rowf = junk.bitcast(F32)
            copy("dve", rowf[0:1, :], pb[0:1, :])

            def stage2():
                pc_ = pbank(6)[:, 300:304]
                for jj in range(4):
                    tr(pc_[:, jj:jj + 1], rowf[0:1, jj * 128:(jj + 1) * 128], ones_f[0:1, 0:1])
                c0 = seg * 8 + half * 4
                tt("dve", modfm[:, c0:c0 + 4], pc_, smalls[:, S_BADA + c0: S_BADA + c0 + 4], ALU.add)
            return stage2
        if seg in (2, 5):
            slot = 0 if seg == 2 else 1
            for k in range(8):
                mm(pb, scbc3[:, k, :], buf[:, k, :], k == 0, k == 7)
            dst = gtb[:, slot * 1024 + half * 512: slot * 1024 + (half + 1) * 512]
            tt("dve", dst, pb, dst, ALU.add)
        else:
            for jj in range(4):
                for k in range(8):
                    mm(pb[:, jj:jj + 1], buf[:, k, jj * 128:(jj + 1) * 128], scb16[:, k:k + 1], k == 0, k == 7)
            c0 = seg * 8 + half * 4
            tt("dve", modfm[:, c0:c0 + 4], pb[:, 0:4], smalls[:, S_BADA + c0: S_BADA + c0 + 4], ALU.add)

    def make_ab(ab, gcol, scseg, shseg):
        ts("dve", ab[:, 0:8], modfm[:, scseg * 8:scseg * 8 + 8], 1.0, None, ALU.add)
        tt("dve", ab[:, 0:8], ab[:, 0:8], smalls[:, gcol:gcol + 8], ALU.mult)
        copy("dve", ab[:, 8:16], modfm[:, shseg * 8:shseg * 8 + 8])

    early = [(1, 0), (1, 1), (0, 0), (0, 1)]
    for pi, (seg, half) in enumerate(early):
        ada_dma(seg, half, wada_s[pi])
    late_pieces = [(4, 0), (4, 1), (3, 0), (3, 1), (2, 0), (2, 1), (5, 0), (5, 1)]
    ts("dve", g_mq8, smalls[:, S_MQG:S_MQG + 64], 0.125, None, ALU.mult)
    tt("dve", g_fk, smalls[:, S_FQG:S_FQG + 64], smalls[:, S_FKG:S_FKG + 64], ALU.mult)
    ts("dve", g_fk, g_fk, 0.125, None, ALU.mult)
    CSv = smalls[:, S_CS:S_CS + 256].rearrange("p (i c) -> p i c", i=NT)
    NSv = smalls[:, S_NS:S_NS + 256].rearrange("p (i c) -> p i c", i=NT)
    if ROPE_ON_DEVICE:
        TWO_PI = 6.283185307179586
        C1 = 6.28125
        C2 = TWO_PI - C1
        pos_i = rope_w[:, 0:16].bitcast(mybir.dt.int32)
        pos_f = rope_w[:, 16:32]
        invf = rope_w[:, 32:40]
        ang = rope_w[:, 64:192].rearrange("p (i j) -> p i j", i=NT)
        kf = rope_w[:, 192:320].rearrange("p (i j) -> p i j", i=NT)
        ki = rope_w[:, 320:448].bitcast(mybir.dt.int32).rearrange("p (i j) -> p i j", i=NT)
        rr = rope_w[:, 448:576].rearrange("p (i j) -> p i j", i=NT)
        cr = rope_w[:, 576:704].rearrange("p (i j) -> p i j", i=NT)
        P.op("pool", lambda e: e.iota(pos_i, pattern=[[128, 16]], base=0, channel_multiplier=1), writes=[pos_i])
        copy("dve", pos_f, pos_i)
        for j_ in range(8):
            memset("pool", invf[:, j_:j_ + 1], float(np.power(np.float32(ROPE_THETA), np.float32(-2.0 * j_ / 16))))
        tt("dve", ang, pos_f.unsqueeze(2).to_broadcast([128, NT, 8]), invf.unsqueeze(1).to_broadcast([128, NT, 8]), ALU.mult)

        def reduce_to_pi(dst, shift):
            ts("dve", kf, ang, 1.0 / TWO_PI, shift / TWO_PI, ALU.mult, ALU.add)
            copy("dve", ki, kf)
            copy("dve", kf, ki)
            stt("dve", dst, kf, -C1, ang, ALU.mult, ALU.add)
            stt("dve", dst, kf, -C2, dst, ALU.mult, ALU.add)
            if shift != 0.0:
                ts("dve", dst, dst, shift, None, ALU.add)
            ts("dve", kf, dst, math.pi, -TWO_PI, ALU.is_gt, ALU.mult)
            tt("dve", dst, dst, kf, ALU.add)
            ts("dve", kf, dst, -math.pi, TWO_PI, ALU.is_lt, ALU.mult)
            tt("dve", dst, dst, kf, ALU.add)
        reduce_to_pi(rr, 0.0)
        reduce_to_pi(cr, math.pi / 2)
        act(rr, rr, AF.Sin)
        act(cr, cr, AF.Sin)
        copy("dve", CSv[:, :, 0:8], cr)
        copy("dve", CSv[:, :, 8:16], cr)
        copy("dve", NSv[:, :, 8:16], rr)
        ts("dve", NSv[:, :, 0:8], rr, -1.0, None, ALU.mult)


    def norm_rstd(i, src):
        act(junk, src, AF.Square, accum_out=ss_a[:, i:i + 1])
        act(rstd_a[:, i:i + 1], ss_a[:, i:i + 1], AF.Sqrt, bias=float(EPS), scale=1.0 / D)
        P.op("dve", (lambda o, a: (lambda e: e.reciprocal(out=o, in_=a)))(rstd_a[:, i:i + 1], rstd_a[:, i:i + 1]),
             reads=[rstd_a[:, i:i + 1]], writes=[rstd_a[:, i:i + 1]])

    def norm_stats(i, src, dst):
        norm_rstd(i, src)
        ts("dve", dst, src, rstd_a[:, i:i + 1], None, ALU.mult)

    trb = [0]

    def norm_tr(g, xb, ab, dstT, tr_banks=(0, 1, 2, 3), split=False):
        for k in range(8):
            pb = pbank(tr_banks[trb[0] % len(tr_banks)])
            trb[0] += 1
            for s_ in range(4):
                tr(pb[:, s_ * 128:(s_ + 1) * 128], xb[:, s_, k * 128:(k + 1) * 128], ident_f)
            if split and k % 2 == 1:
                ts("dve", dstT[:, k, g * 512:(g + 1) * 512], pb, ab[:, k:k + 1], ab[:, 8 + k:9 + k], ALU.mult, ALU.add)
            else:
                act(dstT[:, k, g * 512:(g + 1) * 512], pb, AF.Identity, bias=ab[:, 8 + k:9 + k], scale=ab[:, k:k + 1])

    x_t = x_d.rearrange("(i p) d -> i p d", p=128)
    memset("dve", ss_a, 0.0)
    for i in range(NT):
        dst_ = xg[i // 4][:, i % 4, :]
        dma("sp", dst_, x_t[i])
        norm_stats(i, dst_, dst_)
    for pi, (seg, half) in enumerate(early):
        ada_compute(seg, half, wada_s[pi], 6 + pi % 2)
    make_ab(ab1, S_GMIX, 1, 0)
    for g in range(4):
        norm_tr(g, xg[g], ab1, hnT, split=True)
    dump("hnT", hnT, [128, 8, T])
    if stop_after == "T1":
        P.emit()
        return nc, dbg

    win_v = win_d.rearrange("p (k c) -> p k c", k=8)
    memset("pool", MQ[:, :, :, 64:72], 0.0)
    memset("pool", MK[:, :, :, 64:72], 0.0)
    for n in range(8):
        memset("pool", MK[:, 2 * n:2 * n + 2, :, 64 + n:65 + n], 1.0)
    memset("pool", FK[:, :, :, 64:67], 1.0)
    memset("pool", FQ[:, :, :, 67:70], 1.0)

    CSv = smalls[:, S_CS:S_CS + 256].rearrange("p (i c) -> p i c", i=NT)
    NSv = smalls[:, S_NS:S_NS + 256].rearrange("p (i c) -> p i c", i=NT)
    groups = ["mq", "mk", "fl", "mv", "fq", "fk", "fv"]
    gcol = {"mq": 0, "mk": 512, "mv": 1024, "fq": 1536, "fk": 2048, "fv": 2560, "fl": 3072}
    zf3 = zf.rearrange("p (i h) -> p i h", i=NT)
    seq = []
    for i_ in range(NT):
        seq += [("mq", i_), ("mv", i_)]
    for t_ in range(NT + 3):
        if t_ < NT:
            seq.append(("mk", t_))
        if t_ >= 3:
            seq.append(("fv", t_ - 3))
    for g_ in ("fl", "fq", "fk"):
        seq += [(g_, i_) for i_ in range(NT)]
    units = [(0, g_, i_) for (g_, i_) in seq]
    NU = len(units)
    pos_of = {(g_, i_): n_ for n_, (g_, i_) in enumerate(seq)}
    wfl = wflb[:, :, 0:8]
    gbuf = {"mq": wch[0], "mv": wch[1], "mk": wlate, "fv": wch[0], "fl": wfl, "fq": wch[1], "fk": wch[0]}

    def w_dma(g_):
        c0_ = gcol[g_]
        if g_ == "fl":
            dma("pool", wfl, win_v[:, :, c0_:c0_ + 8])
        else:
            dma("pool", gbuf[g_], win_v[:, :, c0_:c0_ + 512])
    for g_ in ("mq", "mv", "mk", "fl"):
        w_dma(g_)
    wdma_at = {pos_of[("mq", 15)] + 1: ["fv"], pos_of[("mv", 15)] + 2: ["fq"], pos_of[("fv", 15)] + 2: ["fk"]}
    warmup(16, 4, hnT[:, 0, 0:512])

    def pe_unit(n):
        gi, gname, i = units[n]
        for g_ in wdma_at.get(n, []):
            w_dma(g_)
        wb = gbuf[gname]
        pb = pbank(n % 4)
        ncol = 8 if gname == "fl" else 512
        for k in range(8):
            mm(pb[:, 0:ncol], hnT[:, k, i * 128:(i + 1) * 128], wb[:, k, 0:ncol], k == 0, k == 7)

    def stage_a(n):
        gi, gname, i = units[n]
        pb = pbank(n % 4)
        sl = n % 2
        pb3 = pb.rearrange("p (h d) -> p h d", h=8)
        if gname == "fl":
            tt("dve", zf3[:, i, :], pb[:, 0:8], smalls[:, S_BF:S_BF + 8], ALU.add)
            return
        if gname in ("mv", "fv"):
            dst = MV if gname == "mv" else FV
            act(dst[:, i], pb3, AF.Copy)
            return
        act(sq_s[sl], pb, AF.Square)
        P.op("dve", (lambda o, a: (lambda e: e.tensor_reduce(out=o, in_=a, axis=AX.X, op=ALU.add)))(
            ss8[sl], sq_s[sl].rearrange("p (h d) -> p h d", h=8)),
            reads=[sq_s[sl]], writes=[ss8[sl]])
        act(sd8[sl], ss8[sl], AF.Sqrt, bias=float(EPS), scale=1.0 / DH)

    def stage_a2(n):
        gi, gname, i = units[n]
        if gname in ("fl", "mv", "fv"):
            return
        pb = pbank(n % 4)
        sl = n % 2
        pb3 = pb.rearrange("p (h d) -> p h d", h=8)
        P.op("dve", (lambda o, a: (lambda e: e.reciprocal(out=o, in_=a)))(rs8[sl], sd8[sl]),
             reads=[sd8[sl]], writes=[rs8[sl]])
        rsb = rs8[sl].unsqueeze(2).to_broadcast([128, 8, 64])
        if gname == "fq":
            tt("dve", FQ[:, i, :, 0:64], pb3, rsb, ALU.mult)
            return
        t3 = t_s[n % 3].rearrange("p (h d) -> p h d", h=8)
        tt("dve", t3, pb3, rsb, ALU.mult)

    def stage_b(n):
        gi, gname, i = units[n]
        sl = n % 2
        if gname not in ("mq", "mk", "fk"):
            return
        t3 = t_s[n % 3].rearrange("p (h d) -> p h d", h=8)
        if gname == "fk":
            tt("pool", FK[:, i, :, 0:64], t3, g_fk.unsqueeze(1).to_broadcast([128, 8, 64]), ALU.mult)
            return
        gsrc = g_mq8 if gname == "mq" else smalls[:, S_MKG:S_MKG + 64]
        dst = MQ if gname == "mq" else MK
        u3 = t2_s[sl].rearrange("p (h d) -> p h d", h=8)
        tt("pool", u3, t3[:, :, 0:16], gsrc[:, 0:16].unsqueeze(1).to_broadcast([128, 8, 16]), ALU.mult)
        tt("pool", dst[:, i, :, 16:64], t3[:, :, 16:64], gsrc[:, 16:64].unsqueeze(1).to_broadcast([128, 8, 48]), ALU.mult)
        a3 = rA[sl].rearrange("p (h c) -> p h c", h=8)
        b3 = rB[sl].rearrange("p (h c) -> p h c", h=8)
        tt("dve", a3, u3, CSv[:, i].unsqueeze(1).to_broadcast([128, 8, 16]), ALU.mult)
        tt("dve", b3[:, :, 0:8], u3[:, :, 8:16], NSv[:, i, 0:8].unsqueeze(1).to_broadcast([128, 8, 8]), ALU.mult)
        tt("dve", b3[:, :, 8:16], u3[:, :, 0:8], NSv[:, i, 8:16].unsqueeze(1).to_broadcast([128, 8, 8]), ALU.mult)
        tt("dve", dst[:, i, :, 0:16], a3, b3, ALU.add)

    def cum_part(part):
        cum3 = cum.rearrange("p (i h) -> p i h", i=NT)
        r3 = res1.rearrange("p (i h) -> p i h", i=NT)
        pref3 = pref.rearrange("p (i h) -> p i h", i=NT)
        tot3 = totb.rearrange("p (i h) -> p i h", i=NT)
        if part == 0:
            act(zf, zf, AF.Exp, scale=-1.0)
            act(zf, zf, AF.Ln, bias=1.0, scale=1.0)
            ts("dve", zf, zf, -1.0, None, ALU.mult)
        elif part == 10:
            pbc = pbank(6)[:, 0:128]
            pbt = pbank(6)[:, 128:256]
            mm(pbc, tri_f, zf, True, True)
            mm(pbt, ones_f, zf, True, True)
            copy("dve", totb, pbt)
            memset("dve", pref3[:, 0, :], 0.0)
            for i in range(1, 8):
                tt("dve", pref3[:, i, :], pref3[:, i - 1, :], tot3[:, i - 1, :], ALU.add)
        elif part == 1:
            for i in range(8, NT):
                tt("dve", pref3[:, i, :], pref3[:, i - 1, :], tot3[:, i - 1, :], ALU.add)
            tt("dve", cum, pbank(6)[:, 0:128], pref, ALU.add)
            ts("dve", negcum, cum, -1.0, None, ALU.mult)
        else:
            copy("dve", FQ[:, :, :, 64], cum3)
            tt("dve", r3, cum3, FQ[:, :, :, 64], ALU.subtract)
            copy("dve", FQ[:, :, :, 65], r3)
            tt("dve", r3, r3, FQ[:, :, :, 65], ALU.subtract)
            copy("dve", FQ[:, :, :, 66], r3)
            for q_ in range(3):
                ts("dve", FK[:, :, :, 67 + q_], FQ[:, :, :, 64 + q_], -1.0, None, ALU.mult)

    kmT3 = kmT.rearrange("p (h n) -> p h n", h=8)

    def kmean_block():
        pbk = pbank(6)
        for h in range(H):
            for n_ in range(7):
                for s_ in range(2):
                    mm(pbk[0:64, h * 8 + n_: h * 8 + n_ + 1], MK[:, 2 * n_ + s_, h, 0:64], invc[:, 0:1], s_ == 0, s_ == 1)
        memset("dve", kmT[0:64, :], 0.0)
        for h in range(H):
            copy("dve", kmT3[0:64, h, 0:7], pbk[0:64, h * 8: h * 8 + 7])

    def gate_tile_a(i):
        sl = i % 2
        pbt_ = pbank_bf(5)
        for h in range(H):
            tr(pbt_[0:64, h * 128:(h + 1) * 128], MQ[:, i, h, 0:64], ident_b)
        copy("dve", qt0[sl][0:64], pbt_[0:64, 0:1024].rearrange("p (h q) -> p h q", h=8))

    def gate_tile(i):
        b = i // 2
        sl = i % 2
        pbg = pbank(7)
        for h in range(H):
            mm(pbg[:, h * 8:(h + 1) * 8], qt0[sl][0:64, h, :], kmT3[0:64, h, :], True, True)
        copy("dve", gate_s[sl], pbg[:, 0:64])
        g3 = gate_s[sl].rearrange("p (h n) -> p h n", h=8)
        cmp4 = cmp_s[:, 0:8 * b * b].rearrange("p (h n m) -> p h n m", h=8, n=b)
        in0 = g3[:, :, 0:b].unsqueeze(2).to_broadcast([128, 8, b, b])
        in1 = g3[:, :, 0:b].unsqueeze(3).to_broadcast([128, 8, b, b])
        tt("dve", cmp4, in0, in1, ALU.is_gt)
        rk = rank_s[:, 0:8 * b].rearrange("p (h n) -> p h n", h=8)
        P.op("dve", (lambda o, a: (lambda e: e.tensor_reduce(out=o, in_=a, axis=AX.X, op=ALU.add)))(rk, cmp4),
             reads=[cmp4], writes=[rk])
        ts("dve", MQ[:, i, :, 64:64 + b], rk, 3.0, NEG, ALU.is_ge, ALU.mult)

    n_km = pos_of[("mk", 15)] + 6
    n_cum = pos_of[("fl", 15)] + 5
    extra_at = {n_km: [kmean_block], n_cum: [lambda: cum_part(0)], n_cum + 2: [lambda: cum_part(10)],
                n_cum + 5: [lambda: cum_part(1)], n_cum + 9: [lambda: cum_part(2)]}
    for i_ in range(8, NT):
        extra_at.setdefault(n_km + 2 + 3 * (i_ - 8), []).append((lambda ii: (lambda: gate_tile_a(ii)))(i_))
        extra_at.setdefault(n_km + 4 + 3 * (i_ - 8), []).append((lambda ii: (lambda: gate_tile(ii)))(i_))
    n_late = pos_of[("mk", 15)] + 4

    for n in range(NU + 6):
        if n < NU:
            pe_unit(n)
        lk = n - n_late
        if lk >= 0 and lk % 6 == 0:
            if 1 <= lk // 6 <= len(late_pieces):
                sg, hf = late_pieces[lk // 6 - 1]
                st2_ = ada_compute(sg, hf, wlate, 4, row_form=True)
                if st2_ is not None:
                    extra_at.setdefault(n + 2, []).append(st2_)
            if lk // 6 < len(late_pieces):
                sg, hf = late_pieces[lk // 6]
                ada_dma(sg, hf, wlate)
        if 0 <= n - 1 < NU:
            stage_a(n - 1)
        if 0 <= n - 4 < NU:
            stage_b(n - 4)
        if 0 <= n - 2 < NU:
            stage_a2(n - 2)
        for f_ in extra_at.get(n, []):
            f_()
    make_ab(ab2, S_GFFN, 4, 3)
    dump("modfm", modfm, [128, 48])
    dump("gtb", gtb, [128, 2048])
    dump("cum", cum, [128, 128])

    dump("MQ", MQ, [128, NT, H, SW])
    dump("MK", MK, [128, NT, H, SW])
    dump("FQ", FQ, [128, NT, H, FW])
    dump("FK", FK, [128, NT, H, FW])
    dump("MV", MV, [128, NT, H, 64])
    dump("FV", FV, [128, NT, H, 64])
    if stop_after == "T2":
        P.emit()
        return nc, dbg

    memset("pool", vaug[0][:, :, 64:128], 0.0)
    memset("pool", vaug[0][:, :, 64:65], 1.0)
    memset("pool", vaug[1][:, :, 0:64], 0.0)
    memset("pool", vaug[1][:, :, 0:1], 1.0)
    negcum3 = negcum.rearrange("p (i h) -> p i h", i=NT)
    s_banks = (0, 1, 2)
    o_banks = (3, 4)
    tr_banks = (5,)
    LOOK = 2
    PRE = 34
    DEFER = 8

    def head_cfg(hg):
        typ = 0 if hg < 8 else 1
        h = hg % 8
        QS, KS, VS, W = (MQ, MK, MV, SW) if typ == 0 else (FQ, FK, FV, FW)
        par = hg % 2
        return dict(typ=typ, h=h, QS=QS, KS=KS, VS=VS, W=W, par=par, pair=hg // 2,
                    qtb=qt[hg % 2], ktb=kt[hg % 2], va=vaug[par],
                    Mv=65 if par == 0 else 128,
                    rows=slice(0, 64) if par == 0 else slice(64, 128),
                    srp=64 if par == 0 else 0)

    trc = [0]

    trbank_of = {}

    def prologue_part(hg, part):
        cf = head_cfg(hg)
        h = cf["h"]
        if part == 0:
            if cf["par"] == 0:
                copy("pool", vaug[0][:, :, 0:64], cf["VS"][:, :, h, :])
            else:
                copy("pool", vaug[1][:, :, 64:128], cf["VS"][:, :, h, :])
            return
        W = cf["W"]
        grp = (part - 1) // 2
        sub = (part - 1) % 2
        S_, dstT = ((cf["QS"], cf["qtb"]), (cf["KS"], cf["ktb"]))[grp // 2]
        half = grp % 2
        if sub == 0:
            tb_ = (5, 6, 7) if hg == 0 else tr_banks
            trbank_of[(hg, grp)] = pbank_bf(tb_[trc[0] % len(tb_)])
            trc[0] += 1
        pbt_ = trbank_of[(hg, grp)]
        for s_ in range(4 * sub, 4 * sub + 4):
            i = half * 8 + s_
            tr(pbt_[0:W, s_ * 128:(s_ + 1) * 128], S_[:, i, h, 0:W], ident_b)
        if sub == 1:
            copy("dve", dstT[0:W, half * 1024:(half + 1) * 1024], pbt_[0:W, 0:1024])

    steps = []
    for hg in range(16):
        for c in range(4):
            for j in range(4 * c + 4):
                steps.append((hg, c, j))
    NS_ = len(steps)
    first_step = {}
    for si, (hg, c, j) in enumerate(steps):
        if hg not in first_step:
            first_step[hg] = si
    pro_at = {}
    for hg in range(16):
        for part in range(9):
            at = max(0, first_step[hg] - PRE + 3 * part) if hg > 0 else 0
            pro_at.setdefault(at, []).append((hg, part))
    st_info = [None] * NS_
    obank_of = {}
    ocnt = 0
    deferred = []
    ncnt = [0]

    def do_S(si):
        hg, c, j = steps[si]
        cf = head_cfg(hg)
        W = cf["W"]
        q0 = max(512 * c, 128 * j)
        N = 512 * (c + 1) - q0
        sbk = pbank(s_banks[si % 3])
        diag = j >= 4 * c
        mm(sbk[:, 0:N], cf["ktb"][0:W, j * 128:(j + 1) * 128], cf["qtb"][0:W, q0:q0 + N], True, not diag)
        if diag:
            mm(sbk[:, 0:128], ident_b, tribias_b, False, True)
        st_info[si] = (q0, N, sbk)

    def do_rest(si):
        nonlocal ocnt
        hg, c, j = steps[si]
        cf = head_cfg(hg)
        q0, N, sbk = st_info[si]
        if j == 0:
            obank_of[(hg, c)] = pbank(o_banks[ocnt % 2])
            ocnt += 1
        ob = obank_of[(hg, c)]
        ptb = pt[si % 3]
        act(ptb[:, 0:N], sbk[:, 0:N], AF.Exp, bias=0.0, scale=1.0)
        last = (j == 4 * c + 3)
        mm(ob[0:cf["Mv"], q0 - 512 * c:512], cf["va"][:, j, 0:cf["Mv"]], ptb[:, 0:N], j == 0, last)
        if last:
            nb = ncnt[0] % NNB
            ncnt[0] += 1
            srp, rows, pair = cf["srp"], cf["rows"], cf["pair"]
            copy("dve", srow[nb][srp:srp + 1, :], ob[srp:srp + 1, :])
            copy("dve", osb[nb][rows, :], ob[rows, :])

            def mk_pcol(j4, nb=nb, srp=srp):
                def f():
                    pcol = pbank(6)[:, 0:4]
                    tr(pcol[:, j4:j4 + 1], srow[nb][srp:srp + 1, j4 * 128:(j4 + 1) * 128], ones_f[srp:srp + 1, 0:1])
                    if j4 == 3:
                        P.op("dve", (lambda o, a: (lambda e: e.reciprocal(out=o, in_=a)))(rcol[nb], pcol),
                             reads=[pcol], writes=[rcol[nb]])
                        copy("dve", rhi[nb], rcol[nb])
                        tt("dve", rlo[nb], rcol[nb], rhi[nb], ALU.subtract)
                        idb = ident_b.unsqueeze(1).to_broadcast([128, 4, 128])
                        tt("dve", dhi[nb].rearrange("p (j q) -> p j q", j=4), idb,
                           rhi[nb].unsqueeze(2).to_broadcast([128, 4, 128]), ALU.mult)
                        tt("dve", dlo[nb].rearrange("p (j q) -> p j q", j=4), idb,
                           rlo[nb].unsqueeze(2).to_broadcast([128, 4, 128]), ALU.mult)
                return f

            def fin_hi(nb=nb):
                mm(pbank(7), ones_b, dhi[nb], True, False)

            def fin_lo(nb=nb, rows=rows, pair=pair, c=c):
                bcb = pbank(7)
                mm(bcb, ones_b, dlo[nb], False, True)
                tt("dve", oT[rows, pair, c * 512:(c + 1) * 512], osb[nb][rows, :], bcb[rows, :], ALU.mult)
            base = si + LOOK
            for j4 in range(4):
                deferred.append((base + 6 + j4, mk_pcol(j4)))
            deferred.append((base + 18, fin_hi))
            deferred.append((base + 19, fin_lo))

    for n in range(NS_ + LOOK + 26):
        for (hg_, part_) in pro_at.get(n, []):
            prologue_part(hg_, part_)
        wo_i = n - first_step[9] - 4
        if wo_i >= 0 and wo_i % 5 == 0 and wo_i // 5 < 8:
            p_ = wo_i // 5
            wout_v = wout_d.rearrange("p (k c) -> p k c", k=8)
            st_ = stage[p_ % 2]
            dma("sp", st_, wout_v[:, p_, :])
            tt("pool", woutb[:, p_, :], st_, gtb[:, 0:1024], ALU.mult)
        if n < NS_:
            do_S(n)
        deferred.sort(key=lambda t_: t_[0])
        while deferred and deferred[0][0] <= n:
            deferred.pop(0)[1]()
        if 0 <= n - LOOK < NS_:
            do_rest(n - LOOK)
    while deferred:
        deferred.pop(0)[1]()
    dump("oT", oT, [128, 8, T])
    if stop_after == "T3":
        P.emit()
        return nc, dbg

    for i in range(NT):
        dma("sp", x1[:, i, :], x_t[i])
    memset("dve", ss_a, 0.0)
    wup_v = wup_d.rearrange("p (m s k c) -> p m s k c", m=NM, s=2, k=8)
    dma("pool", wupb[0], wup_v[:, 0])
    dma("pool", wupb[1], wup_v[:, 1])

    def outproj_group(g):
        for i in range(4 * g, 4 * g + 4):
            yb = psum[:, (4 + 2 * (i % 2)) * 512:(4 + 2 * (i % 2)) * 512 + 1024]
            for cb in range(2):
                for p_ in range(8):
                    mm(yb[:, cb * 512:(cb + 1) * 512], oT[:, p_, i * 128:(i + 1) * 128], woutb[:, p_, cb * 512:(cb + 1) * 512],
                       p_ == 0, p_ == 7)
            tt("dve", x1[:, i, :], yb, x1[:, i, :], ALU.add)
            norm_stats(i, x1[:, i, :], xn2[g % 2][:, i % 4, :])

    for g in range(4):
        for i in range(4 * g, 4 * g + 4):
            yb = psum[:, (4 + 2 * (i % 2)) * 512:(4 + 2 * (i % 2)) * 512 + 1024]
            for cb in range(2):
                for p_ in range(8):
                    mm(yb[:, cb * 512:(cb + 1) * 512], oT[:, p_, i * 128:(i + 1) * 128], woutb[:, p_, cb * 512:(cb + 1) * 512],
                       p_ == 0, p_ == 7)
            tt("dve", x1[:, i, :], yb, x1[:, i, :], ALU.add)
            norm_rstd(i, x1[:, i, :])

    def stats_group(g):
        for i in range(4 * g, 4 * g + 4):
            ts("dve", xn2[g % 2][:, i % 4, :], x1[:, i, :], rstd_a[:, i:i + 1], None, ALU.mult)
    stats_group(0)
    for g in range(4):
        if g + 1 < 4:
            stats_group(g + 1)
        norm_tr(g, xn2[g % 2], ab2, hn2T)
    dump("x1", x1, [128, NT, D])
    dump("hn2T", hn2T, [128, 8, T])
    if stop_after == "T4":
        P.emit()
        return nc, dbg

    wup_v = wup_d.rearrange("p (m s k c) -> p m s k c", m=NM, s=2, k=8)
    wdn_v = wdn_d.rearrange("p (m c) -> p m c", m=NM)
    cw = smalls[:, S_CONVW:S_CONVW + 132].rearrange("p (i m) -> p i m", i=3)
    cbv = smalls[:, S_CONVB:S_CONVB + 44]
    out_t = out_d.rearrange("(i p) d -> i p d", p=128)
    m0 = 0
    ucnt = 0
    wcnt = 0
    scnt5 = 0
    ycnt = 0
    for g, gs in enumerate(GSZ):
        for ml in range(gs):
            st_ = stage5[scnt5 % 2]
            scnt5 += 1
            dma("sp", st_, wdn_v[:, m0 + ml, :])
            tt("dve", wdnb[:, ml, :], st_, gtb[:, 1024:2048], ALU.mult)
        for ml in range(gs):
            m = m0 + ml
            wb = wupb[m % 2]
            if 1 <= m and m + 1 < NM:
                dma("pool", wupb[(m + 1) % 2], wup_v[:, m + 1])
            for half in range(2):
                t0 = half * HW_
                bufi = ucnt % 2
                for part, (ub, cbuf) in enumerate(((ua[bufi], ca[bufi]), (uv[bufi], cv[bufi]))):
                    ch = part * NM + m
                    pb2 = psum[:, (2 * (ucnt % 2)) * 512 + 0: (2 * (ucnt % 2)) * 512 + 1024] if part == 0 else \
                        psum[:, (2 * (ucnt % 2)) * 512 + 0: (2 * (ucnt % 2)) * 512 + 1024]
                    dbk = (2 * ucnt + part) % 4
                    pb2 = psum[:, dbk * 1024: dbk * 1024 + 1024]
                    for cb in range(2):
                        for k in range(8):
                            mm(pb2[:, cb * 512:(cb + 1) * 512], wb[:, part, k, :],
                               hn2T[:, k, t0 + cb * 512: t0 + (cb + 1) * 512], k == 0, k == 7)
                    if half == 0:
                        act(ub[:, 0:2], zc4[:, 0:2], AF.Copy)
                    else:
                        prev = (ua if part == 0 else uv)[(ucnt - 1) % 2]
                        act(ub[:, 0:2], prev[:, HW_:HW_ + 2], AF.Copy)
                    act(ub[:, 2:2 + HW_], pb2, AF.Copy)
                    act(cbuf, ub[:, 2:2 + HW_], AF.Identity, bias=cbv[:, ch:ch + 1], scale=cw[:, 2, ch:ch + 1])
                    stt("dve", cbuf, ub[:, 1:1 + HW_], cw[:, 1, ch:ch + 1], cbuf, ALU.mult, ALU.add)
                    stt("dve", cbuf, ub[:, 0:HW_], cw[:, 0, ch:ch + 1], cbuf, ALU.mult, ALU.add)
                act(ca[bufi], ca[bufi], AF.Silu)
                tt("pool", hT[:, ml, t0:t0 + HW_], ca[bufi], cv[bufi], ALU.mult)
                ucnt += 1
        for i in range(NT):
            yb = psum[:, (4 + 2 * (ycnt % 2)) * 512:(4 + 2 * (ycnt % 2)) * 512 + 1024]
            ycnt += 1
            for cb in range(2):
                for ml in range(gs):
                    mm(yb[:, cb * 512:(cb + 1) * 512], hT[:, ml, i * 128:(i + 1) * 128], wdnb[:, ml, cb * 512:(cb + 1) * 512],
                       ml == 0, ml == gs - 1)
            tt("dve", x1[:, i, :], yb, x1[:, i, :], ALU.add)
            if g == len(GSZ) - 1:
                dma("sp", out_t[i], x1[:, i, :])
        m0 += gs
    P.emit()
    return nc, dbg


def _rope_tables():
    half = 8
    inv_freq = np.power(np.float32(ROPE_THETA), (-2.0 * np.arange(half, dtype=np.float32) / 16).astype(np.float32)).astype(np.float32)
    pos = np.arange(T, dtype=np.float32)
    ang = pos[:, None] * inv_freq[None, :]
    cos = np.cos(ang).astype(np.float32)
    sin = np.sin(ang).astype(np.float32)
    cs = np.concatenate([cos, cos], axis=1)
    ns = np.concatenate([-sin, sin], axis=1)
    cs = cs.reshape(NT, 128, 16).transpose(1, 0, 2).reshape(128, NT * 16)
    ns = ns.reshape(NT, 128, 16).transpose(1, 0, 2).reshape(128, NT * 16)
    return cs, ns


def _prep_shared(inp):
    f = lambda a: np.ascontiguousarray(np.asarray(a, dtype=np.float32))
    pk = lambda w, k: f(w.reshape(k, 128, -1).transpose(1, 0, 2).reshape(128, -1))
    sh = {}
    sh["w_ada"] = pk(f(inp["w_ada"])[0], 8)
    sh["w_in"] = pk(f(inp["w_in"])[0], 8)
    sh["w_out"] = pk(f(inp["w_out"])[0], 8)
    wup = f(inp["w_up"])[0]
    wup = wup.reshape(8, 128, 2, NM, 128)
    sh["w_up"] = f(wup.transpose(1, 3, 2, 0, 4).reshape(128, -1))
    sh["w_down"] = pk(f(inp["w_down"])[0], NM)
    sm = np.zeros((128, S_TOT), np.float32)
    fm = lambda v, k: f(v).reshape(k, 128).T
    sm[:, S_GMIX:S_GMIX + 8] = fm(inp["g_mix"][0], 8)
    sm[:, S_GFFN:S_GFFN + 8] = fm(inp["g_ffn"][0], 8)
    sm[:, S_BADA:S_BADA + 48] = fm(inp["b_ada"][0], 48)
    cw = f(inp["conv_w"])[0]
    for i in range(3):
        sm[:, S_CONVW + i * 44: S_CONVW + (i + 1) * 44] = fm(cw[i], 44)
    sm[:, S_CONVB:S_CONVB + 44] = fm(inp["conv_b"][0], 44)
    sm[:, S_MQG:S_MQG + 64] = f(inp["moba_q_gain"])[0][None, :]
    sm[:, S_MKG:S_MKG + 64] = f(inp["moba_k_gain"])[0][None, :]
    sm[:, S_FQG:S_FQG + 64] = f(inp["fox_q_gain"])[0][None, :]
    sm[:, S_FKG:S_FKG + 64] = f(inp["fox_k_gain"])[0][None, :]
    sm[:, S_BF:S_BF + 8] = f(inp["b_forget"])[0][None, :]
    if not ROPE_ON_DEVICE:
        cs, ns = _rope_tables()
        sm[:, S_CS:S_CS + 256] = cs
        sm[:, S_NS:S_NS + 256] = ns
    sh["smalls"] = sm
    ba = f(inp["b_ada"])[0]
    bc = np.zeros((128, 2048), np.float32)
    bc[:, 0:1024] = ba[None, 2048:3072]
    bc[:, 1024:2048] = ba[None, 5120:6144]
    sh["bcastb"] = bc
    return sh


def _core_inputs(sh, x_b, c_b):
    m = dict(sh)
    sm = sh["smalls"].copy()
    sm[:, S_C:S_C + 8] = np.asarray(c_b, np.float32).reshape(8, 128).T
    m["smalls"] = sm
    m["x"] = np.ascontiguousarray(np.asarray(x_b, np.float32))
    return m


_CACHE = {}


def kernel(**inputs):
    x = np.asarray(inputs["x"], np.float32)
    c = np.asarray(inputs["c"], np.float32)
    sh = _prep_shared(inputs)
    if "nc" not in _CACHE:
        _CACHE["nc"] = build_program(debug=False)[0]
    nc = _CACHE["nc"]
    in_maps = [_core_inputs(sh, x[b], c[b]) for b in range(8)]
    res = run_bass_kernel_spmd(nc, in_maps, core_ids=list(range(8)))
    out = np.stack([np.asarray(res.results[b]["out"], np.float32).reshape(T, D) for b in range(8)], axis=0)
    return out
```
